# Optimizing a Trainium2 kernel written in Bass

```python
import jax, jax.numpy as jnp
from jax import lax
import numpy as np

D_MODEL = 1024
BATCH = 8
SEQ = 2048
DEPTH = 2
DEC_BATCH = 128
DEC_SEQ = 1
PAST_LEN = 16384
PAGE_SIZE = 128

N_EVEN = (DEPTH + 1) // 2
N_ODD = DEPTH // 2
D_POOL = D_MODEL // 4
POOL_WINDOWS = (2, 4, 8, 16)
N_POOL_GROUPS = len(POOL_WINDOWS)
POOL_GROUP_DIM = D_POOL // N_POOL_GROUPS
POOL_BUF = max(POOL_WINDOWS) - 1
D_HGRN = D_MODEL - D_POOL
HGRN_HEAD_DIM = 128
N_HGRN_HEADS = D_HGRN // HGRN_HEAD_DIM
HGRN_CHUNK = 32
D_EVEN_IN = D_POOL + 4 * D_HGRN
D_CONV = D_MODEL
CONV_WIDTH = 3
CONV_BUF = CONV_WIDTH - 1
D_FF = 4 * D_MODEL
EPS = 1e-6

kernel_name = "pool_hgrn2_shortconv_hybrid_step"


def rmsnorm(x, g):
    xf = x.astype(jnp.float32)
    r = xf * lax.rsqrt(jnp.mean(xf * xf, axis=-1, keepdims=True) + EPS)
    return (r * g.astype(jnp.float32)).astype(x.dtype)


def pool_mix(u, prev, pos0, w_pool, scale):
    b_, t_, _ = u.shape
    ext = jnp.concatenate([prev.astype(u.dtype), u], axis=1)
    ext32 = ext.astype(jnp.float32)
    cs = jnp.pad(jnp.cumsum(ext32, axis=1), ((0, 0), (1, 0), (0, 0)))
    pos = pos0 + jnp.arange(t_, dtype=jnp.int32)
    means = []
    for g, w in enumerate(POOL_WINDOWS):
        sl = slice(g * POOL_GROUP_DIM, (g + 1) * POOL_GROUP_DIM)
        s = cs[:, POOL_BUF + 1:POOL_BUF + 1 + t_, sl] - cs[:, POOL_BUF + 1 - w:POOL_BUF + 1 - w + t_, sl]
        cnt = jnp.minimum(w, pos + 1).astype(jnp.float32)
        means.append(s / cnt[None, :, None])
    mean = jnp.stack(means, axis=2)
    diff = mean - u.astype(jnp.float32).reshape(b_, t_, N_POOL_GROUPS, POOL_GROUP_DIM)
    out = jnp.einsum('btgc,gcd->btgd', diff.astype(u.dtype), w_pool).reshape(b_, t_, D_POOL) * scale
    return out, ext[:, -POOL_BUF:]


def hgrn2_chunked(q, k, v, logf, s0):
    b_, t_, h_, _ = q.shape
    dv = v.shape[-1]
    cs = min(HGRN_CHUNK, t_)
    n = -(-t_ // cs)
    pad = n * cs - t_

    def blocks(a):
        a = jnp.pad(a, ((0, 0), (0, pad), (0, 0), (0, 0)))
        return jnp.moveaxis(a.reshape(b_, n, cs, h_, a.shape[-1]), 1, 0)

    mask = jnp.tril(jnp.ones((cs, cs), dtype=bool))

    def step(s, blk):
        qc, kc, vc, gc = blk
        bcum = jnp.cumsum(gc, axis=1)
        q_dec = qc * jnp.exp(bcum)
        k_dec = kc * jnp.exp(-bcum)
        inter = jnp.einsum('bthk,bhkv->bthv', q_dec, s)
        scores = jnp.where(mask, jnp.einsum('bthk,bshk->bhts', q_dec, k_dec), 0.0)
        intra = jnp.einsum('bhts,bshv->bthv', scores, vc)
        b_last = bcum[:, -1]
        k_end = kc * jnp.exp(b_last[:, None] - bcum)
        s_new = s * jnp.exp(b_last)[..., None] + jnp.einsum('bshk,bshv->bhkv', k_end, vc)
        return s_new, inter + intra

    s_fin, o = lax.scan(step, s0, (blocks(q), blocks(k), blocks(v), blocks(logf)))
    o = jnp.moveaxis(o, 0, 1).reshape(b_, n * cs, h_, dv)[:, :t_]
    return o, s_fin


def even_layer(h, pool_prev, s_prev, pos0, lb, w_in, w_pool, pool_scale, hgrn_gain, w_out):
    b_, t_, _ = h.shape
    proj = h @ w_in
    u = proj[..., :D_POOL]
    q, fz, iz, gz = jnp.split(proj[..., D_POOL:], 4, axis=-1)
    pool_out, pool_new = pool_mix(u, pool_prev, pos0, w_pool, pool_scale)
    lb32 = lb.astype(jnp.float32)
    fz32 = fz.astype(jnp.float32)
    logf = jnp.log(lb32 + (1.0 - lb32) * jax.nn.sigmoid(fz32))
    k = (1.0 - lb32) * jax.nn.sigmoid(-fz32)
    v = jax.nn.silu(iz.astype(jnp.float32))

    def heads(a):
        return a.reshape(b_, t_, N_HGRN_HEADS, HGRN_HEAD_DIM)

    o, s_new = hgrn2_chunked(heads(q.astype(jnp.float32)), heads(k), heads(v), heads(logf),
                             s_prev.astype(jnp.float32))
    o = o * lax.rsqrt(jnp.mean(o * o, axis=-1, keepdims=True) + EPS)
    o = o.reshape(b_, t_, D_HGRN) * hgrn_gain.astype(jnp.float32) * jax.nn.sigmoid(gz.astype(jnp.float32))
    mixed = jnp.concatenate([pool_out, o.astype(h.dtype)], axis=-1) @ w_out
    return mixed, pool_new, s_new.astype(s_prev.dtype)


def odd_layer(h, conv_prev, w_in, conv_w, w_out):
    t_ = h.shape[1]
    bg, cg, hv = jnp.split(h @ w_in, 3, axis=-1)
    z = cg * hv
    ext = jnp.concatenate([conv_prev.astype(z.dtype), z], axis=1)
    conv = conv_w[0] * ext[:, 0:t_]
    for j in range(1, CONV_WIDTH):
        conv = conv + conv_w[j] * ext[:, j:j + t_]
    return (bg * conv) @ w_out, ext[:, -CONV_BUF:]


def trunk(x, pool_st, hgrn_st, conv_st, pos0, norm_mix, norm_mlp, norm_final, even_w_in, pool_w,
          pool_scale, hgrn_lb_logits, hgrn_gain, even_w_out, odd_w_in, conv_w, odd_w_out, ff_w1, ff_w2):
    lb_all = jnp.cumsum(jax.nn.softmax(hgrn_lb_logits.astype(jnp.float32), axis=0), axis=0)
    new_pool, new_hgrn, new_conv = [], [], []
    for l in range(DEPTH):
        h = rmsnorm(x, norm_mix[l])
        if l % 2 == 0:
            e = l // 2
            mixed, p_new, s_new = even_layer(h, pool_st[e], hgrn_st[e], pos0, lb_all[l], even_w_in[e],
                                             pool_w[e], pool_scale[e], hgrn_gain[e], even_w_out[e])
            new_pool.append(p_new)
            new_hgrn.append(s_new)
        else:
            o = l // 2
            mixed, c_new = odd_layer(h, conv_st[o], odd_w_in[o], conv_w[o], odd_w_out[o])
            new_conv.append(c_new)
        x = x + mixed
        h = rmsnorm(x, norm_mlp[l])
        x = x + jnp.square(jax.nn.relu(h @ ff_w1[l])) @ ff_w2[l]
    return rmsnorm(x, norm_final), jnp.stack(new_pool), jnp.stack(new_hgrn), jnp.stack(new_conv)


def setup_inputs(seed: int = 0) -> dict:
    key = jax.random.key(seed)
    ks = jax.random.split(key, 20)
    f32 = jnp.float32

    def nrm(k, shape, scale):
        return jax.random.normal(k, shape, f32) * scale

    return {
        "x_prompt": nrm(ks[0], (BATCH, SEQ, D_MODEL), 1.0),
        "x_sample": nrm(ks[1], (DEC_BATCH, DEC_SEQ, D_MODEL), 1.0),
        "state_pool": nrm(ks[2], (N_EVEN, DEC_BATCH, POOL_BUF, D_POOL), 1.0),
        "state_hgrn": nrm(ks[3], (N_EVEN, DEC_BATCH, N_HGRN_HEADS, HGRN_HEAD_DIM, HGRN_HEAD_DIM), 0.5),
        "state_conv": nrm(ks[4], (N_ODD, DEC_BATCH, CONV_BUF, D_CONV), 0.5),
        "norm_mix": 1.0 + nrm(ks[5], (DEPTH, D_MODEL), 0.02),
        "norm_mlp": 1.0 + nrm(ks[6], (DEPTH, D_MODEL), 0.02),
        "norm_final": 1.0 + nrm(ks[7], (D_MODEL,), 0.02),
        "even_w_in": nrm(ks[8], (N_EVEN, D_MODEL, D_EVEN_IN), D_MODEL ** -0.5),
        "pool_w": nrm(ks[9], (N_EVEN, N_POOL_GROUPS, POOL_GROUP_DIM, POOL_GROUP_DIM), POOL_GROUP_DIM ** -0.5),
        "pool_scale": 0.5 + nrm(ks[10], (N_EVEN, D_POOL), 0.05),
        "hgrn_lb_logits": nrm(ks[11], (DEPTH + 1, D_HGRN), 0.1),
        "hgrn_gain": 1.0 + nrm(ks[12], (N_EVEN, D_HGRN), 0.02),
        "even_w_out": nrm(ks[13], (N_EVEN, D_MODEL, D_MODEL), D_MODEL ** -0.5),
        "odd_w_in": nrm(ks[14], (N_ODD, D_MODEL, 3 * D_CONV), D_MODEL ** -0.5),
        "conv_w": nrm(ks[15], (N_ODD, CONV_WIDTH, D_CONV), CONV_WIDTH ** -0.5),
        "odd_w_out": nrm(ks[16], (N_ODD, D_CONV, D_MODEL), D_CONV ** -0.5),
        "ff_w1": nrm(ks[17], (DEPTH, D_MODEL, D_FF), D_MODEL ** -0.5),
        "ff_w2": nrm(ks[18], (DEPTH, D_FF, D_MODEL), D_FF ** -0.5),
    }


def reference(x_prompt, x_sample, state_pool, state_hgrn, state_conv, norm_mix, norm_mlp, norm_final,
              even_w_in, pool_w, pool_scale, hgrn_lb_logits, hgrn_gain, even_w_out, odd_w_in, conv_w,
              odd_w_out, ff_w1, ff_w2):
    bp = x_prompt.shape[0]
    zero_pool = jnp.zeros((N_EVEN, bp, POOL_BUF, D_POOL), x_prompt.dtype)
    zero_hgrn = jnp.zeros((N_EVEN, bp, N_HGRN_HEADS, HGRN_HEAD_DIM, HGRN_HEAD_DIM), state_hgrn.dtype)
    zero_conv = jnp.zeros((N_ODD, bp, CONV_BUF, D_CONV), x_prompt.dtype)
    y_prompt, new_pool_prompt, new_hgrn_prompt, new_conv_prompt = trunk(
        x_prompt, zero_pool, zero_hgrn, zero_conv, 0, norm_mix, norm_mlp, norm_final, even_w_in, pool_w,
        pool_scale, hgrn_lb_logits, hgrn_gain, even_w_out, odd_w_in, conv_w, odd_w_out, ff_w1, ff_w2)
    y_sample, new_pool_sample, new_hgrn_sample, new_conv_sample = trunk(
        x_sample, state_pool, state_hgrn, state_conv, PAST_LEN, norm_mix, norm_mlp, norm_final, even_w_in,
        pool_w, pool_scale, hgrn_lb_logits, hgrn_gain, even_w_out, odd_w_in, conv_w, odd_w_out, ff_w1, ff_w2)
    return (y_prompt, y_sample, new_pool_prompt, new_hgrn_prompt, new_conv_prompt,
            new_pool_sample, new_hgrn_sample, new_conv_sample)
```

```python
import numpy as np
import ml_dtypes
from contextlib import ExitStack
import concourse.bass as bass
import concourse.mybir as mybir
from concourse.bass_utils import run_bass_kernel_spmd

F32 = mybir.dt.float32
BF16 = mybir.dt.bfloat16
AF = mybir.ActivationFunctionType
ALU = mybir.AluOpType
AX = mybir.AxisListType

NT = 2048
NSMP = 16
NTOK = NT + NSMP
EPS = 1e-6
NSLOT = 4
NPAGE = 18
SAME_ENGINE_SYNC = True
EMBED_WAITS = True
HEAD_OFFSET = 0
PREFETCH_F = False

U_HEAD = 0
U_POOL = 6
U_WO = 7
U_FF0 = 9
U_CONV = 25
U_OO = 33
U_FF1 = 35
NU = 51


class Buf:
    __slots__ = ("name", "w", "r")

    def __init__(self, name):
        self.name = name
        self.w = None
        self.r = {}


class DSem:
    def __init__(self, h):
        self.h = h
        self.count = 0


class Prog:
    ENG = ("sp", "act", "dve", "pool", "pe")

    def __init__(self):
        self.streams = {e: [] for e in self.ENG}
        self.seq = {e: 0 for e in self.ENG}
        self.waited = {e: {} for e in self.ENG}
        self.esem = {}
        self.dsems = []

    def _waits(self, eng, reads, writes, extra=()):
        need = {}

        def add(tok):
            if tok is None:
                return
            if tok[0] == "e":
                if tok[1] == eng and (eng == "pe" or not SAME_ENGINE_SYNC):
                    return
                k = ("e", tok[1])
            else:
                k = ("d", id(tok[1]))
            if self.waited[eng].get(k, 0) >= tok[2]:
                return
            if k not in need or need[k][2] < tok[2]:
                need[k] = tok

        for b in reads:
            add(b.w)
        for b in writes:
            add(b.w)
            for t in b.r.values():
                add(t)
        for t in extra:
            add(t)
        out = []
        for k, tok in need.items():
            self.waited[eng][k] = tok[2]
            sem = self.esem[tok[1]] if tok[0] == "e" else tok[1].h
            out.append((sem, tok[2]))
        return out

    @staticmethod
    def _mark(tok, reads, writes):
        k = ("e", tok[1]) if tok[0] == "e" else ("d", id(tok[1]))
        for b in reads:
            b.r[k] = tok
        for b in writes:
            b.w = tok
            b.r = {}

    def op(self, eng, fn, reads=(), writes=(), multi=False):
        waits = self._waits(eng, reads, writes)
        self.seq[eng] += 1
        tok = ("e", eng, self.seq[eng])
        self.streams[eng].append((waits, fn, "multi" if multi else True))
        self._mark(tok, reads, writes)
        return tok

    def dma(self, q, dsem, fns, reads=(), writes=()):
        extra = []
        if dsem.count > 0:
            extra.append(("d", dsem, dsem.count))
        waits = self._waits(q, reads, writes, extra)
        dsem.count += 16 * len(fns)
        tok = ("d", dsem, dsem.count)

        def run(eng, fns=fns, h=dsem.h):
            for f in fns:
                f(eng).then_inc(h, 16)
            return None

        self.streams[q].append((waits, run, False))
        self._mark(tok, reads, writes)
        return tok


def build():
    nc = bass.Bass("TRN2", target_bir_lowering=False)

    def din(name, shape, dt=F32):
        return nc.dram_tensor(name, shape, dt, kind="ExternalInput").ap()

    def dout(name, shape, dt=F32):
        return nc.dram_tensor(name, shape, dt, kind="ExternalOutput").ap()

    xp = din("xp", [16, 128, 1024])
    xs = din("xs", [16, 1024])
    st_pool = din("st_pool", [16, 15 * 256])
    st_hgrn = din("st_hgrn", [16, 6, 128, 128])
    st_conv = din("st_conv", [16, 2, 1024])
    wst = din("wst", [NU, 128, 4096])
    gcols_d = din("gcols", [128, 32])
    gfin_d = din("gfin", [1, 1024])
    lbl_d = din("lbl", [128, 18])
    gain_d = din("gainf", [128, 6])
    pscale_d = din("pscalef", [128, 2])
    poolw_d = din("poolw", [4, 64, 64])
    convwf_d = din("convwf", [128, 24])
    convw_d = din("convw", [3, 1024])
    c_identb = din("c_identb", [128, 128], BF16)
    c_identf = din("c_identf", [128, 128])
    c_maskT = din("c_maskT", [128, 128])
    c_onesm = din("c_onesm", [128, 128])
    c_id16 = din("c_id16", [16, 16])
    c_cfixw = din("c_cfixw", [128, 32])

    yp = dout("yp", [16, 128, 1024])
    ys = dout("ys", [16, 1024])
    npp = dout("npp", [15, 256])
    nhp = dout("nhp", [6, 128, 128])
    ncp = dout("ncp", [2, 1024])
    nps = dout("nps", [16, 15 * 256])
    nhs = dout("nhs", [16, 6, 128, 128])
    ncs = dout("ncs", [16, 2, 1024])

    P = Prog()
    with ExitStack() as es:
        E = es.enter_context

        def sb(name, shape, dt=F32):
            return E(nc.sbuf_tensor(name, shape, dt))

        x_sb = sb("x_sb", [128, 16, 1024])
        xs_sb = sb("xs_sb", [128, 1024])
        hT = sb("hT", [128, 8, NTOK], BF16)
        mixT = sb("mixT", [128, 8, NTOK], BF16)
        ring = sb("ring", [128, NSLOT, 4096], BF16)
        arena = sb("arena", [128, NPAGE * 512])
        identb = sb("identb", [128, 128], BF16)
        identf = sb("identf", [128, 128])
        maskT = sb("maskT", [128, 128])
        onesm = sb("onesm", [128, 128])
        onesb = sb("onesb", [128, 128], BF16)
        wblk = sb("wblk", [128, 2, 128], BF16)
        gcols = sb("gcols_s", [128, 32])
        lbl = sb("lbl_s", [128, 18])
        hcols = sb("hcols", [128, 32])
        pscale = sb("pscale_s", [128, 2])
        cfixw = sb("cfixw_s", [128, 32])
        convwf = sb("convwf_s", [128, 24])
        id16 = sb("id16_s", [128, 16])
        nrm = sb("nrm", [128, 32])
        tmpc = sb("tmpc", [128, 16])
        gsm2 = [sb(f"gsm{i}", [128, 64]) for i in range(2)]
        Sst2 = [sb(f"Sst{i}", [128, 128]) for i in range(2)]
        Spb2 = [sb(f"Spb{i}", [128, 2, 128], BF16) for i in range(2)]
        tSb2 = [sb(f"tSb{i}", [128, 128]) for i in range(2)]
        psT = [E(nc.psum_tensor(f"pp{i}", [128, 1024], F32)) for i in range(4)]

        for e in Prog.ENG:
            P.esem[e] = E(nc.semaphore(f"es_{e}"))

        def new_dsem(name):
            d = DSem(E(nc.semaphore(name)))
            P.dsems.append(d)
            return d

        ring_ds = [new_dsem(f"ds_ring{i}") for i in range(NSLOT)]
        xy_ds = [new_dsem(f"ds_xy{i}") for i in range(17)]
        misc_ds = [new_dsem(f"ds_misc{i}") for i in range(8)]
        misc_ptr = [0]

        def misc():
            d = misc_ds[misc_ptr[0] % len(misc_ds)]
            misc_ptr[0] += 1
            return d

        block = E(nc.Block())

        B_x = [Buf(f"x{b}") for b in range(16)] + [Buf("xs")]
        B_hT = [Buf(f"hT{t}") for t in range(5)]
        B_mix = [Buf(f"mix{c}") for c in range(8)]
        B_ring = [Buf(f"ring{i}") for i in range(NSLOT)]
        B_pg = [Buf(f"pg{i}") for i in range(NPAGE)]
        B_ps = [Buf(f"ps{i}") for i in range(8)]
        B_const = Buf("const")
        B_wblk = Buf("wblk")
        B_hcols = Buf("hcols")
        B_nrm = [Buf("nrm0"), Buf("nrm1"), Buf("nrm1b")]
        B_tmpc = Buf("tmpc")
        B_dummy = [Buf("dummy0"), Buf("dummy1")]
        B_gsm2 = [Buf("gsm0"), Buf("gsm1")]
        B_S2 = [Buf("S0"), Buf("S1")]
        B_Sp2 = [[Buf("Sp00"), Buf("Sp01")], [Buf("Sp10"), Buf("Sp11")]]
        B_tS2 = [Buf("tS0"), Buf("tS1")]
        B_nrm2 = [Buf("nrm2"), Buf("nrm3")]

        def ACT(out, in_, func, R, W, scale=1.0, bias=0.0, accum=None):
            def f(e):
                if accum is None:
                    return e.activation(out=out, in_=in_, func=func, bias=bias, scale=scale)
                return e.activation(out=out, in_=in_, func=func, bias=bias, scale=scale, accum_out=accum)
            return P.op("act", f, R, W, multi=accum is not None)

        def TT(out, in0, in1, op, R, W, eng="dve"):
            return P.op(eng, lambda e: e.tensor_tensor(out=out, in0=in0, in1=in1, op=op), R, W)

        def TS(out, in0, s1, op0, R, W, s2=None, op1=None, eng="dve"):
            def f(e):
                if op1 is None:
                    return e.tensor_scalar(out=out, in0=in0, scalar1=s1, scalar2=None, op0=op0)
                return e.tensor_scalar(out=out, in0=in0, scalar1=s1, scalar2=s2, op0=op0, op1=op1)
            return P.op(eng, f, R, W)

        def STT(out, in0, scalar, in1, op0, op1, R, W):
            return P.op("dve", lambda e: e.scalar_tensor_tensor(out=out, in0=in0, scalar=scalar, in1=in1,
                                                                 op0=op0, op1=op1), R, W)

        def CP(out, in_, R, W, eng="dve"):
            return P.op(eng, lambda e: e.tensor_copy(out=out, in_=in_), R, W)

        def MEMSET(ap, val, W, eng="dve"):
            return P.op(eng, lambda e: e.memset(ap, val), (), W)

        def MMG(mms, R, W):
            def f(e):
                ins = None
                for (o, l, r, s, t) in mms:
                    ins = e.matmul(o, l, r, start=s, stop=t)
                return ins
            return P.op("pe", f, R, W, multi=True)

        def TRG(trs, R, W):
            def f(e):
                ins = None
                for (o, i, idn) in trs:
                    ins = e.transpose(out=o, in_=i, identity=idn)
                return ins
            return P.op("pe", f, R, W, multi=True)

        def DMA(q, dsem, pairs, R, W, **kw):
            fns = [(lambda e, o=o, i=i: e.dma_start(out=o, in_=i, **kw)) for (o, i) in pairs]
            return P.dma(q, dsem, fns, R, W)

        def bank(k):
            return psT[k // 2][:, (k % 2) * 512:(k % 2 + 1) * 512]

        def bank_bf(k):
            return bank(k).bitcast(BF16)

        def pair(i):
            return psT[i][:, :]

        class Rot:
            def __init__(self, items):
                self.items = items
                self.i = 0

            def next(self):
                v = self.items[self.i % len(self.items)]
                self.i += 1
                return v

        def pgf(p, n=1):
            return arena[:, p * 512:(p + n) * 512]

        def pgb(p, n=1):
            return [B_pg[i] for i in range(p, p + n)]

        unit_cols = {}
        for u in range(NU):
            unit_cols[u] = 4096
        unit_cols[U_POOL] = 2048
        for c in range(8):
            unit_cols[U_CONV + c] = 3072
        next_load = [0]

        def issue_load(u):
            s = u % NSLOT
            ncol = unit_cols[u]
            pairs = []
            c0 = 0
            while c0 < ncol:
                c1 = min(c0 + 2048, ncol)
                pairs.append((ring[:, s, c0:c1], wst[u, :, c0:c1]))
                c0 = c1
            DMA("pool", ring_ds[s], pairs, (), [B_ring[s]])

        def need(u, la=NSLOT - 1):
            lim = min(u + la, NU - 1)
            while next_load[0] <= lim:
                issue_load(next_load[0])
                next_load[0] += 1
            return u % NSLOT

        def tokslice(t):
            if t < 4:
                return slice(t * 512, (t + 1) * 512)
            return slice(NT, NTOK)

        need(0)
        DMA("sp", misc(), [
            (identb[:, :], c_identb), (identf[:, :], c_identf), (maskT[:, :], c_maskT), (onesm[:, :], c_onesm),
            (gcols[:, :], gcols_d), (lbl[:, :], lbl_d), (hcols[:, 18:24], gain_d), (pscale[:, :], pscale_d),
            (cfixw[:, :], c_cfixw), (convwf[:, :], convwf_d), (id16[0:16, :], c_id16),
        ], (), [B_const, B_hcols])
        for b in range(16):
            DMA("sp", xy_ds[b], [(x_sb[:, b, :], xp[b])], (), [B_x[b]])
        DMA("sp", xy_ds[16], [(xs_sb[0:16, :], xs)], (), [B_x[16]])
        MEMSET(wblk[:, :, :], 0.0, [B_wblk])
        DMA("pool", new_dsem("ds_wblk"), [
            (wblk[(g % 2) * 64:(g % 2) * 64 + 64, g // 2, (g % 2) * 64:(g % 2) * 64 + 64], poolw_d[g]) for g in range(4)
        ], (), [B_wblk])
        ACT(onesb[:, :], onesm[:, :], AF.Copy, [B_const], [B_const])
        ACT(lbl[:, :], lbl[:, :], AF.Exp, [B_const], [B_const])
        TT(tmpc[:, 0:6], lbl[:, 0:6], lbl[:, 6:12], ALU.add, [B_const], [B_tmpc])
        TT(tmpc[:, 0:6], tmpc[:, 0:6], lbl[:, 12:18], ALU.add, [B_const, B_tmpc], [B_tmpc])
        P.op("dve", lambda e: e.reciprocal(out=tmpc[:, 6:12], in_=tmpc[:, 0:6]), [B_tmpc], [B_tmpc])
        TT(hcols[:, 0:6], lbl[:, 0:6], tmpc[:, 6:12], ALU.mult, [B_const, B_tmpc], [B_hcols])
        TS(hcols[:, 6:12], hcols[:, 0:6], -1.0, ALU.mult, [B_hcols], [B_hcols], s2=1.0, op1=ALU.add)
        TS(hcols[:, 12:18], hcols[:, 6:12], -1.0, ALU.mult, [B_hcols], [B_hcols])
        ACT(hcols[:, 24:30], hcols[:, 6:12], AF.Ln, [B_hcols], [B_hcols])
        LB, OML, NOML, GAIN, LNOML = 0, 6, 12, 18, 24

        ps_norm = Rot([0, 1])
        norm_ctr = [0]
        norm_pending = []
        norm_junk = [[15, 16, 17]]

        def emit_norm(b, n):
            nj = len(norm_junk[0])
            k = norm_ctr[0] % nj
            norm_ctr[0] += 1
            np_ = 128 if b < 16 else 16
            xb = x_sb[:, b, :] if b < 16 else xs_sb[0:16, :]
            Bx = B_x[b]
            junk = VPGF(norm_junk[0][k])[0:np_, :].bitcast(BF16)
            Bj = VPGB(norm_junk[0][k])
            ss = nrm[0:np_, 4 * k + 0:4 * k + 1]
            lt = nrm[0:np_, 4 * k + 1:4 * k + 2]
            rs = nrm[0:np_, 4 * k + 2:4 * k + 3]
            Bn = [B_nrm[k]]
            ACT(junk, xb, AF.Square, [Bx], Bj + Bn, accum=ss)
            ACT(lt, ss, AF.Ln, Bn, Bn, scale=1.0 / 1024, bias=EPS)
            ACT(rs, lt, AF.Exp, Bn, Bn, scale=-0.5)
            if n == 4:
                gfin = pgf(10, 2)
                STT(xb, xb, rs, gfin[0:np_, :], ALU.mult, ALU.mult, [Bx] + Bn + pgb(10, 2), [Bx])
                if b < 16:
                    DMA("sp", xy_ds[b], [(yp[b], x_sb[:, b, :])], [Bx], ())
                else:
                    DMA("sp", xy_ds[16], [(ys, xs_sb[0:16, :])], [Bx], ())
                return
            if n == 0:
                TS(junk, xb, rs, ALU.mult, [Bx] + Bn, Bj)
            else:
                ACT(junk, xb, AF.Copy, [Bx] + Bn, Bj, scale=rs)

            def part_b():
                pk = ps_norm.next()
                pb = bank_bf(pk)
                if b < 16:
                    TRG([(pb[:, c * 128:(c + 1) * 128], junk[:, c * 128:(c + 1) * 128], identb[:, :]) for c in range(8)],
                        Bj + [B_const], [B_ps[pk]])
                    t = b // 4
                    TT(hT[:, :, b * 128:(b + 1) * 128], pb[:, 0:1024].rearrange("p (c t) -> p c t", t=128),
                       gcols[:, n * 8:(n + 1) * 8].unsqueeze(2).broadcast_to([128, 8, 128]), ALU.mult,
                       [B_ps[pk], B_const], [B_hT[t]])
                else:
                    TRG([(pb[:, c * 16:(c + 1) * 16], junk[:, c * 128:(c + 1) * 128], identb[0:16, 0:16]) for c in range(8)],
                        Bj + [B_const], [B_ps[pk]])
                    TT(hT[:, :, NT:NTOK], pb[:, 0:128].rearrange("p (c t) -> p c t", t=16),
                       gcols[:, n * 8:(n + 1) * 8].unsqueeze(2).broadcast_to([128, 8, 16]), ALU.mult,
                       [B_ps[pk], B_const], [B_hT[4]])
            norm_pending.append(part_b)

        def flush_norm(keep=0):
            while len(norm_pending) > keep:
                norm_pending.pop(0)()

        ps_pair = Rot([2, 3])

        def emit_outproj(u0, norm_after):
            need(u0)
            s0, s1 = u0 % NSLOT, (u0 + 1) % NSLOT
            for b in list(range(16)) + [16]:
                pi = ps_pair.next()
                pp = pair(pi)
                tk = slice(b * 128, (b + 1) * 128) if b < 16 else slice(NT, NTOK)
                np_ = 128 if b < 16 else 16
                mms = []
                for half, s in ((0, s0), (1, s1)):
                    for kc in range(8):
                        mms.append((pp[0:np_, half * 512:(half + 1) * 512], mixT[:, kc, tk],
                                    ring[:, s, kc * 512:(kc + 1) * 512], kc == 0, kc == 7))
                MMG(mms, B_mix + [B_ring[s0], B_ring[s1]], [B_ps[2 * pi], B_ps[2 * pi + 1]])
                xb = x_sb[:, b, :] if b < 16 else xs_sb[0:16, :]
                TT(xb, xb, pp[0:np_, :], ALU.add, [B_x[b], B_ps[2 * pi], B_ps[2 * pi + 1]], [B_x[b]])
                emit_norm(b, norm_after)
                flush_norm(keep=2)
            flush_norm()

        def emit_mlp(u0, norm_after):
            ps_single = Rot([0, 1, 2, 3])
            ps_pr = Rot([2, 3])
            steps = [(s, t) for s in range(8) for t in range(5)]
            hid = [mixT[:, 0, 0:2048].rearrange("p (c t) -> p c t", t=512),
                   mixT[:, 1, 0:2048].rearrange("p (c t) -> p c t", t=512)]
            hid_s = [mixT[:, 2, 0:64].rearrange("p (c t) -> p c t", t=16),
                     mixT[:, 3, 0:64].rearrange("p (c t) -> p c t", t=16)]
            B_hid = [B_mix[0], B_mix[1]]
            B_hidc = [[Buf(f"hid{i}_{n}") for n in range(4)] for i in range(2)]
            B_hids = [B_mix[2], B_mix[3]]

            def ff1_items(k):
                s, t = steps[k]
                items = []
                if t == 0:
                    items.append(lambda s=s: need(u0 + 2 * s, NSLOT - 2))
                if t == 2:
                    items.append(lambda s=s: need(u0 + 2 * s + 1, NSLOT - 2))
                sl = (u0 + 2 * s) % NSLOT
                if t < 4:
                    hb = hid[k % 2]
                    for n in range(4):
                        def it(n=n, hb=hb, sl=sl, t=t, k=k):
                            pk = ps_single.next()
                            MMG([(bank(pk), ring[:, sl, kc * 512 + n * 128:kc * 512 + (n + 1) * 128],
                                  hT[:, kc, tokslice(t)], kc == 0, kc == 7) for kc in range(8)],
                                [B_ring[sl], B_hT[t]], [B_ps[pk]])
                            Wh = [B_hidc[k % 2][n]] + ([B_hid[k % 2]] if k < 2 else [])
                            ACT(hb[:, n, :], bank(pk), AF.Relu, [B_ps[pk]], Wh)
                            TT(hb[:, n, :], hb[:, n, :], hb[:, n, :], ALU.mult, [B_hidc[k % 2][n]], [B_hidc[k % 2][n]])
                        items.append(it)
                else:
                    hb = hid_s[s % 2]

                    def it(hb=hb, sl=sl, s=s):
                        pk = ps_single.next()
                        mms = []
                        for n in range(4):
                            for kc in range(8):
                                mms.append((bank(pk)[:, n * 16:(n + 1) * 16],
                                            ring[:, sl, kc * 512 + n * 128:kc * 512 + (n + 1) * 128],
                                            hT[:, kc, NT:NTOK], kc == 0, kc == 7))
                        MMG(mms, [B_ring[sl], B_hT[4]], [B_ps[pk]])
                        ACT(hb, bank(pk)[:, 0:64].rearrange("p (c t) -> p c t", t=16), AF.Relu, [B_ps[pk]], [B_hids[s % 2]])
                        TT(hb, hb, hb, ALU.mult, [B_hids[s % 2]], [B_hids[s % 2]])
                    items.append(it)
                return items

            def ff2_items(k):
                s, t = steps[k]
                sl2 = (u0 + 2 * s + 1) % NSLOT
                items = []
                if t < 4:
                    hb = hid[k % 2]
                    for j in range(4):
                        def it(j=j, hb=hb, sl2=sl2, t=t, s=s, k=k):
                            b = t * 4 + j
                            pi = ps_pr.next()
                            pp = pair(pi)
                            mms = []
                            for half in range(2):
                                for kc in range(4):
                                    mms.append((pp[:, half * 512:(half + 1) * 512], hb[:, kc, j * 128:(j + 1) * 128],
                                                ring[:, sl2, kc * 1024 + half * 512:kc * 1024 + (half + 1) * 512],
                                                kc == 0, kc == 3))
                            MMG(mms, [B_ring[sl2], B_hid[k % 2]] + B_hidc[k % 2], [B_ps[2 * pi], B_ps[2 * pi + 1]])
                            TT(x_sb[:, b, :], x_sb[:, b, :], pp, ALU.add, [B_x[b], B_ps[2 * pi], B_ps[2 * pi + 1]], [B_x[b]])
                            if s == 7:
                                emit_norm(b, norm_after)
                                flush_norm(keep=2)
                        items.append(it)
                else:
                    hb = hid_s[s % 2]

                    def it(hb=hb, sl2=sl2, s=s):
                        pi = ps_pr.next()
                        pp = pair(pi)
                        mms = []
                        for half in range(2):
                            for kc in range(4):
                                mms.append((pp[0:16, half * 512:(half + 1) * 512], hb[:, kc, :],
                                            ring[:, sl2, kc * 1024 + half * 512:kc * 1024 + (half + 1) * 512],
                                            kc == 0, kc == 3))
                        MMG(mms, [B_ring[sl2], B_hids[s % 2]], [B_ps[2 * pi], B_ps[2 * pi + 1]])
                        TT(xs_sb[0:16, :], xs_sb[0:16, :], pp[0:16, :], ALU.add,
                           [B_x[16], B_ps[2 * pi], B_ps[2 * pi + 1]], [B_x[16]])
                        if s == 7:
                            emit_norm(16, norm_after)
                            flush_norm()
                    items.append(it)
                return items

            for it in ff1_items(0):
                it()
            for k in range(len(steps)):
                a = ff1_items(k + 1) if k + 1 < len(steps) else []
                bb = ff2_items(k)
                for i in range(max(len(a), len(bb))):
                    if i < len(a):
                        a[i]()
                    if i < len(bb):
                        bb[i]()

        def VPGF(i):
            if i < NPAGE:
                return arena[:, i * 512:(i + 1) * 512]
            j = i - NPAGE
            return mixT[:, j // 2, (j % 2) * 1024:(j % 2 + 1) * 1024].bitcast(F32)

        def VPGB(i):
            if i < NPAGE:
                return [B_pg[i]]
            return [B_mix[(i - NPAGE) // 2]]

        class Small:
            def __init__(self, pages):
                self.pages = pages

            def slot(self, i, np_=16, w=128, n=1):
                assert (i % 4) + n <= 4
                return VPGF(self.pages[i // 4])[0:np_, (i % 4) * 128:(i % 4) * 128 + (n - 1) * 128 + w]

            def bufs(self):
                out = []
                for p in self.pages:
                    out += VPGB(p)
                return out

        class Cx:
            pass

        def make_cx(ci, prompt_pages, sh_base, vm, sbf, msk, small_pages, banks):
            cx = Cx()
            cx.ci = ci
            cx.pp = prompt_pages
            cx.sh_base, cx.vm, cx.sbf, cx.msk = sh_base, vm, sbf, msk
            cx.small = Small(small_pages)
            cx.ps = Rot(banks)
            cx.banks = banks
            cx.gsm = gsm2[ci]
            cx.B_gsm = B_gsm2[ci]
            cx.Sst, cx.Spb, cx.tSb = Sst2[ci], Spb2[ci], tSb2[ci]
            cx.B_S, cx.B_Sp, cx.B_tS = B_S2[ci], B_Sp2[ci], B_tS2[ci]
            cx.nrmc = 16 + 4 * ci
            cx.B_nrm = B_nrm2[ci]
            return cx

        def gen_head_prompt(h, t, sl, cx):
            assert norm_done[0] >= 4 * (t + 1) and (t == 0 or not norm_pending or norm_done[0] > 4 * (t + 1)), (t, norm_done[0])
            T0 = t * 512
            tk = slice(T0, T0 + 512)
            ps = cx.ps
            gs = cx.gsm
            Bg = [cx.B_gsm]

            def W(kc, a, b):
                return ring[:, sl, kc * 512 + a:kc * 512 + b]

            Rw = [B_ring[sl], B_hT[t]]
            pg = cx.pp
            sn, si, lf, G, Ei, osq, rstd = (VPGF(pg[i]) for i in range(7))
            Bsn, Bsi, Blf, BG, BEi, Bosq, Brstd = (VPGB(pg[i]) for i in range(7))
            Ee, rel = si, lf
            p7 = VPGF(pg[7]).bitcast(BF16)
            p8 = VPGF(pg[8]).bitcast(BF16)
            p9 = VPGF(pg[9]).bitcast(BF16)
            B7, B8, B9 = VPGB(pg[7]), VPGB(pg[8]), VPGB(pg[9])
            vtok, qd = p7[:, 0:512], p7[:, 512:1024]
            kd, kt = p8[:, 0:512], p8[:, 512:1024]
            sm, sg = p9[:, 0:512], p9[:, 512:1024]
            if t == 0 or not PREFETCH_F:
                cx.kf_next = ps.next()
                MMG([(bank(cx.kf_next), W(kc, 128, 256), hT[:, kc, tk], kc == 0, kc == 7) for kc in range(8)], Rw,
                    [B_ps[cx.kf_next]])
            kf = cx.kf_next
            yield
            ef, L2 = sn, Ei
            ACT(ef, bank(kf), AF.Exp, [B_ps[kf]], Bsn)
            yield
            kq = ps.next()
            MMG([(bank(kq), W(kc, 0, 128), hT[:, kc, tk], kc == 0, kc == 7) for kc in range(8)], Rw, [B_ps[kq]])
            yield
            ki = ps.next()
            MMG([(bank(ki)[:, j * 128:(j + 1) * 128], hT[:, kc, T0 + j * 128:T0 + (j + 1) * 128], W(kc, 256, 384),
                  kc == 0, kc == 7) for j in range(4) for kc in range(8)], Rw, [B_ps[ki]])
            yield
            kg = ps.next()
            MMG([(bank(kg), W(kc, 384, 512), hT[:, kc, tk], kc == 0, kc == 7) for kc in range(8)], Rw, [B_ps[kg]])
            yield
            ACT(L2, ef, AF.Ln, Bsn, BEi, bias=1.0)
            ACT(lf, ef, AF.Ln, Bsn + [B_hcols], Blf, bias=hcols[:, LB + h:LB + h + 1])
            yield
            if t == 0:
                MEMSET(cx.Sst[:, :], 0.0, [cx.B_S])
                P.op("dve", lambda e: e.tensor_tensor_scan(out=G, data0=lf, data1=L2, initial=0.0,
                                                           op0=ALU.add, op1=ALU.subtract), Blf + BEi, BG)
                MEMSET(gs[:, 16:17], 0.0, Bg)
            else:
                CP(gs[:, 16:17], gs[:, 20:21], Bg, Bg)
                P.op("dve", lambda e: e.tensor_tensor_scan(out=G, data0=lf, data1=L2, initial=gs[:, 16:17],
                                                           op0=ALU.add, op1=ALU.subtract), Blf + BEi + Bg, BG)
            yield
            G3 = G.rearrange("p (j t) -> p j t", t=128)
            TT(rel.rearrange("p (j t) -> p j t", t=128), G3, G3[:, :, 63:64].broadcast_to([128, 4, 128]), ALU.subtract,
               BG, Blf)
            yield
            ACT(Ee, rel, AF.Exp, Blf, Bsi)
            TT(L2, rel, L2, ALU.add, Blf + BEi, BEi)
            yield
            TT(qd, bank(kq), Ee, ALU.mult, [B_ps[kq]] + Bsi, B7)
            ACT(kd, L2, AF.Exp, BEi + [B_hcols], B8, scale=-1.0, bias=hcols[:, LNOML + h:LNOML + h + 1])
            yield
            CP(gs[:, 17:21], G3[:, :, 127], BG, Bg)
            TT(gs[:, 24:28], G3[:, :, 63], gs[:, 16:20], ALU.subtract, BG + Bg, Bg)
            TT(gs[:, 28:32], gs[:, 17:21], G3[:, :, 63], ALU.subtract, BG + Bg, Bg)
            TT(gs[:, 32:36], gs[:, 17:21], gs[:, 16:20], ALU.subtract, Bg, Bg)
            ACT(gs[:, 40:52], gs[:, 24:36], AF.Exp, Bg, Bg)
            eM, eBM, eB = 40, 44, 48
            yield
            ACT(vtok, bank(ki), AF.Silu, [B_ps[ki]], B7)
            ACT(sg, bank(kg), AF.Tanh, [B_ps[kg]], B9, scale=0.5)
            ACT(sg, sg, AF.Identity, B9, B9, scale=0.5, bias=0.5)
            if cx.ci == 1:
                ACT(tmpc[:, 12 + cx.ci:13 + cx.ci], hcols[:, 0:1], AF.Exp, [B_hcols], [B_dummy[cx.ci]])
            yield
            kt_ps = ps.next()
            TRG([(bank_bf(kt_ps)[:, j * 128:(j + 1) * 128], kd[:, j * 128:(j + 1) * 128], identb[:, :]) for j in range(4)],
                B8 + [B_const], [B_ps[kt_ps]])
            ks = ps.next()
            MMG([(bank(ks)[:, j * 128:(j + 1) * 128], kd[:, j * 128:(j + 1) * 128], qd[:, j * 128:(j + 1) * 128], True, True)
                 for j in range(4)], B7 + B8, [B_ps[ks]])
            yield
            ACT(kt, bank_bf(kt_ps)[:, 0:512], AF.Copy, [B_ps[kt_ps]], B8)
            TT(sm.rearrange("p (j t) -> p j t", t=128), bank(ks).rearrange("p (j t) -> p j t", t=128),
               maskT[:, :].unsqueeze(1).broadcast_to([128, 4, 128]), ALU.mult, [B_ps[ks], B_const], B9)
            yield
            kdS = ps.next()
            MMG([(bank(kdS)[:, j * 128:(j + 1) * 128], kt[:, j * 128:(j + 1) * 128], vtok[:, j * 128:(j + 1) * 128], True, True)
                 for j in range(4)], B7 + B8, [B_ps[kdS]])
            yield
            if PREFETCH_F and t < 3:
                cx.kf_next = ps.next()
                tk2 = slice(T0 + 512, T0 + 1024)
                MMG([(bank(cx.kf_next), W(kc, 128, 256), hT[:, kc, tk2], kc == 0, kc == 7) for kc in range(8)],
                    [B_ring[sl], B_hT[t + 1]], [B_ps[cx.kf_next]])
            tSa = lf.rearrange("p (j v) -> p j v", v=128)
            for j in range(4):
                ACT(tSa[:, j, :], bank(kdS)[:, j * 128:(j + 1) * 128], AF.Copy, [B_ps[kdS]] + Bg, Blf,
                    scale=gs[:, eBM + j:eBM + j + 1])
            yield
            ko = ps.next()
            for j in range(4):
                sp_i = j % 2
                ACT(cx.Spb[:, sp_i, :], cx.Sst[:, :], AF.Copy, [cx.B_S] + Bg, [cx.B_Sp[sp_i]], scale=gs[:, eM + j:eM + j + 1])
                MMG([(bank(ko)[:, j * 128:(j + 1) * 128], cx.Spb[:, sp_i, :], qd[:, j * 128:(j + 1) * 128], True, False),
                     (bank(ko)[:, j * 128:(j + 1) * 128], vtok[:, j * 128:(j + 1) * 128], sm[:, j * 128:(j + 1) * 128], False, True)],
                    [cx.B_Sp[sp_i]] + B7 + B9, [B_ps[ko]])
                STT(cx.Sst[:, :], cx.Sst[:, :], gs[:, eB + j:eB + j + 1], tSa[:, j, :], ALU.mult, ALU.add,
                    [cx.B_S] + Blf + Bg, [cx.B_S])
                yield
            osqb = osq.bitcast(BF16)[:, 0:512]
            ACT(osqb, bank(ko), AF.Square, [B_ps[ko]], Bosq)
            yield
            km = ps.next()
            MMG([(bank(km), onesb[:, :], osqb, True, True)], Bosq + [B_const], [B_ps[km]])
            yield
            ACT(rstd, bank(km), AF.Ln, [B_ps[km]], Brstd, bias=EPS)
            ACT(rstd, rstd, AF.Exp, Brstd, Brstd, scale=-0.5)
            yield
            TT(osq, bank(ko), rstd, ALU.mult, [B_ps[ko]] + Brstd, Bosq)
            STT(mixT[:, 2 + h, tk], osq, hcols[:, GAIN + h:GAIN + h + 1], sg, ALU.mult, ALU.mult,
                Bosq + B9 + [B_hcols], [B_mix[2 + h]])
            if t == 3:
                DMA("sp", misc(), [(nhp[h], cx.Sst[:, :])], [cx.B_S], ())
            yield

        def gen_head_sample(h, sl, cx):
            ps = cx.ps
            Rw = [B_ring[sl], B_hT[4]]
            Sh = arena[:, cx.sh_base * 512:(cx.sh_base + 4) * 512].rearrange("p (b v) -> p b v", v=128)
            BSh = pgb(cx.sh_base, 4)
            DMA("sp", misc(), [(Sh[:, 4 * g:4 * g + 4, :], st_hgrn[4 * g:4 * g + 4, h].rearrange("b k v -> k b v"))
                               for g in range(4)], (), BSh)
            kp = ps.next()
            MMG([(bank(kp)[0:16, :], hT[:, kc, NT:NTOK], ring[:, sl, kc * 512:(kc + 1) * 512], kc == 0, kc == 7)
                 for kc in range(8)], Rw, [B_ps[kp]])
            yield
            S_ = cx.small
            Bs = S_.bufs()
            qfig = S_.slot(0, n=4)
            si_s, vv, kk = (S_.slot(i) for i in (4, 5, 6))
            wide = S_.slot(9, np_=128, w=128)
            rs_s, t1_s = wide[:, 0:16], wide[:, 16:32]
            osq_s = wide[:, 32:48].bitcast(BF16)[:, 0:16]
            qT = S_.slot(10, np_=128, w=16).bitcast(BF16)[:, 0:16]
            fm = S_.slot(11, np_=128, w=64)
            snT, kkT, fT, sgT = fm[:, 0:16], fm[:, 16:32], fm[:, 32:48], fm[:, 48:64]
            ACT(si_s, bank(kp)[0:16, 256:384], AF.Sigmoid, [B_ps[kp]], Bs)
            ACT(qfig, bank(kp)[0:16, 0:512], AF.Copy, [B_ps[kp]], Bs)
            yield
            TT(vv, bank(kp)[0:16, 256:384], si_s, ALU.mult, [B_ps[kp]] + Bs, Bs)
            k2 = ps.next()
            TRG([(bank(k2)[:, 0:16], qfig[:, 0:128], identf[0:16, 0:16]),
                 (bank(k2)[:, 16:32], qfig[:, 128:256], identf[0:16, 0:16]),
                 (bank(k2)[:, 32:48], qfig[:, 384:512], identf[0:16, 0:16])], Bs + [B_const], [B_ps[k2]])
            yield
            ACT(snT, bank(k2)[:, 16:32], AF.Sigmoid, [B_ps[k2]], Bs, scale=-1.0)
            ACT(sgT, bank(k2)[:, 32:48], AF.Sigmoid, [B_ps[k2]], Bs)
            ACT(qT, bank(k2)[:, 0:16], AF.Copy, [B_ps[k2]], Bs)
            yield
            TS(kkT, snT, hcols[:, OML + h:OML + h + 1], ALU.mult, Bs + [B_hcols], Bs)
            TS(fT, kkT, -1.0, ALU.mult, Bs, Bs, s2=1.0, op1=ALU.add)
            k3 = ps.next()
            TRG([(bank(k3)[0:16, 0:128], kkT, identf[:, :])], Bs + [B_const], [B_ps[k3]])
            yield
            kkb = kk.bitcast(BF16)[:, 0:128]
            ACT(kkb, bank(k3)[0:16, 0:128], AF.Copy, [B_ps[k3]], Bs)
            kk = kkb
            yield
            Vmf = VPGF(cx.vm)[0:16, :].bitcast(BF16)
            BVm = VPGB(cx.vm)
            for g in range(4):
                Vm = Vmf[:, (g % 2) * 512:(g % 2 + 1) * 512].rearrange("p (b v) -> p b v", v=128)
                TT(Vm, vv.unsqueeze(1).broadcast_to([16, 4, 128]),
                   id16[0:16, 4 * g:4 * g + 4].unsqueeze(2).broadcast_to([16, 4, 128]), ALU.mult, Bs + [B_const], BVm)
                kk_ps = ps.next()
                MMG([(bank(kk_ps), kk, Vm, True, True)], Bs + BVm, [B_ps[kk_ps]])
                TT(Sh[:, 4 * g:4 * g + 4, :], Sh[:, 4 * g:4 * g + 4, :],
                   fT[:, 4 * g:4 * g + 4].unsqueeze(2).broadcast_to([128, 4, 128]), ALU.mult, BSh + Bs, BSh)
                yield
                TT(Sh[:, 4 * g:4 * g + 4, :], Sh[:, 4 * g:4 * g + 4, :], bank(kk_ps).rearrange("p (b v) -> p b v", v=128),
                   ALU.add, BSh + [B_ps[kk_ps]], BSh)
                yield
            DMA("sp", misc(), [(nhs[4 * g:4 * g + 4, h].rearrange("b k v -> k b v"), Sh[:, 4 * g:4 * g + 4, :])
                               for g in range(4)], BSh, ())
            Sbf = VPGF(cx.sbf).bitcast(BF16)
            BSbf = VPGB(cx.sbf)
            ko_ = ps.next()
            oT = bank(ko_)[:, 0:16]
            for g in range(4):
                sb_g = Sbf[:, (g % 2) * 512:(g % 2 + 1) * 512].rearrange("p (b v) -> p b v", v=128)
                ACT(sb_g, Sh[:, 4 * g:4 * g + 4, :], AF.Copy, BSh, BSbf)
                yield
                MMG([(bank(ko_)[:, 4 * g + bb:4 * g + bb + 1], sb_g[:, bb, :], qT[:, 4 * g + bb:4 * g + bb + 1], True, True)
                     for bb in range(4)], Bs + BSbf, [B_ps[ko_]])
                yield
            ACT(osq_s, oT, AF.Square, [B_ps[ko_]], Bs)
            yield
            km_ = ps.next()
            MMG([(bank(km_)[:, 0:16], onesb[:, :], osq_s, True, True)], Bs + [B_const], [B_ps[km_]])
            yield
            ACT(rs_s, bank(km_)[:, 0:16], AF.Ln, [B_ps[km_]], Bs, bias=EPS)
            ACT(rs_s, rs_s, AF.Exp, Bs, Bs, scale=-0.5)
            yield
            TT(t1_s, oT, rs_s, ALU.mult, [B_ps[ko_]] + Bs, Bs)
            STT(mixT[:, 2 + h, NT:NTOK], t1_s, hcols[:, GAIN + h:GAIN + h + 1], sgT, ALU.mult, ALU.mult,
                Bs + [B_hcols], [B_mix[2 + h]])
            yield

        def gen_head(h, cx):
            sl = (U_HEAD + h) % NSLOT
            for t in range(4):
                yield from gen_head_prompt(h, t, sl, cx)
            yield from gen_head_sample(h, sl, cx)

        def interleave(gens, offset=0):
            gens = list(gens)
            for _ in range(offset):
                try:
                    next(gens[0])
                except StopIteration:
                    break
            while gens:
                for g in list(gens):
                    try:
                        next(g)
                    except StopIteration:
                        gens.remove(g)

        ps_all = Rot(list(range(8)))

        def gen_pool_chunk(c, sl):
            pm = ({"ue": 0, "A": 4, "B": 6, "C": 6, "d": 10, "fx": 16}, {"ue": 2, "A": 8, "B": 11, "C": 13, "d": 15, "fx": 17})[c]
            ue = pgf(pm["ue"], 2)[:, 0:528]
            Bu = pgb(pm["ue"], 2)
            A_ = pgf(pm["A"], 2)[:, 0:528]
            B_ = pgf(pm["B"], 2)[:, 0:528]
            C_ = pgf(pm["C"], 2)[:, 0:528]
            BA, BB, BC = pgb(pm["A"], 2), pgb(pm["B"], 2), pgb(pm["C"], 2)
            dbf = pgf(pm["d"]).bitcast(BF16)[:, 0:512]
            Bd = pgb(pm["d"])
            fx = pgf(pm["fx"])[:, 0:16]
            Bfx = pgb(pm["fx"])
            def proj(tt):
                k_ = ps_all.next()
                tk_ = slice(tt * 512, (tt + 1) * 512)
                MMG([(bank(k_), ring[:, sl, kc * 256 + c * 128:kc * 256 + (c + 1) * 128], hT[:, kc, tk_], kc == 0, kc == 7)
                     for kc in range(8)], [B_ring[sl], B_hT[tt]], [B_ps[k_]])
                return k_

            ku_next = proj(0)
            yield
            for t in range(4):
                T0 = t * 512
                tk = slice(T0, T0 + 512)
                ku = ku_next
                if t == 0:
                    MEMSET(ue[:, 0:16], 0.0, Bu)
                ACT(ue[:, 16:528], bank(ku), AF.Copy, [B_ps[ku]], Bu)
                if t < 3:
                    ku_next = proj(t + 1)
                yield
                TT(A_[:, 2:528], ue[:, 2:528], ue[:, 1:527], ALU.add, Bu, BA)
                yield
                if c == 0:
                    TT(B_[64:128, 4:528], A_[64:128, 4:528], A_[64:128, 2:526], ALU.add, BA, BB)
                    STT(dbf[0:64, :], A_[0:64, 16:528], 0.5, ue[0:64, 16:528], ALU.mult, ALU.subtract, BA + Bu, Bd)
                    yield
                    STT(dbf[64:128, :], B_[64:128, 16:528], 0.25, ue[64:128, 16:528], ALU.mult, ALU.subtract, BB + Bu, Bd)
                    sel_lo, sel_hi = A_, B_
                    yield
                    yield
                else:
                    TT(B_[:, 4:528], A_[:, 4:528], A_[:, 2:526], ALU.add, BA, BB)
                    yield
                    TT(C_[:, 8:528], B_[:, 8:528], B_[:, 4:524], ALU.add, BB, BC)
                    yield
                    TT(A_[64:128, 16:528], C_[64:128, 16:528], C_[64:128, 8:520], ALU.add, BC + BA, BA)
                    STT(dbf[0:64, :], C_[0:64, 16:528], 0.125, ue[0:64, 16:528], ALU.mult, ALU.subtract, BC + Bu, Bd)
                    yield
                    STT(dbf[64:128, :], A_[64:128, 16:528], 0.0625, ue[64:128, 16:528], ALU.mult, ALU.subtract, BA + Bu, Bd)
                    sel_lo, sel_hi = C_, A_
                if t == 0:
                    for (lo, hi, sel) in ((0, 64, sel_lo), (64, 128, sel_hi)):
                        TT(fx[lo:hi, :], sel[lo:hi, 16:32], cfixw[lo:hi, c * 16:(c + 1) * 16], ALU.mult,
                           [B_const] + BA + BB + BC, Bfx)
                        TT(dbf[lo:hi, 0:16], fx[lo:hi, :], ue[lo:hi, 16:32], ALU.subtract, Bfx + Bu, Bd)
                yield
                kp_ = ps_all.next()
                MMG([(bank(kp_), wblk[:, c, :], dbf, True, True)], Bd + [B_wblk], [B_ps[kp_]])
                yield
                ACT(mixT[:, c, tk], bank(kp_), AF.Copy, [B_ps[kp_], B_const], [B_mix[c]], scale=pscale[:, c:c + 1])
                if t < 3:
                    CP(ue[:, 0:16], ue[:, 512:528], Bu, Bu)
                else:
                    kx = ps_all.next()
                    TRG([(bank(kx)[0:15, 0:128], ue[:, 513:528], identf[:, :])], Bu + [B_const], [B_ps[kx]])
                    stg = pgf(pm["fx"])[0:15, 0:128]
                    ACT(stg, bank(kx)[0:15, 0:128], AF.Copy, [B_ps[kx]], Bfx)
                    DMA("sp", misc(), [(npp[:, c * 128:(c + 1) * 128], stg)], Bfx, ())
                yield

        def emit_pool_sample(sl):
            Rw = [B_ring[sl], B_hT[4]]
            stp = pgf(0, 8)[0:16, 0:3840]
            Bst = pgb(0, 8)
            DMA("sp", misc(), [(stp, st_pool)], (), Bst)
            stp3 = stp.rearrange("p (r c) -> p r c", c=256)
            ku = ps_all.next()
            MMG([(bank(ku)[0:16, 0:256], hT[:, kc, NT:NTOK], ring[:, sl, kc * 256:(kc + 1) * 256], kc == 0, kc == 7)
                 for kc in range(8)], Rw, [B_ps[ku]])
            sml = pgf(11, 3)
            Bs = pgb(11, 3)
            u_s = sml[0:16, 0:256]
            sums = sml[0:16, 256:512]
            dd = sml[0:16, 512:768].bitcast(BF16)[:, 0:256]
            ACT(u_s, bank(ku)[0:16, 0:256], AF.Copy, [B_ps[ku]], Bs)
            for g, w in enumerate((2, 4, 8, 16)):
                cs = slice(g * 64, (g + 1) * 64)
                if w == 2:
                    TT(sums[:, cs], stp3[:, 14, cs], u_s[:, cs], ALU.add, Bst + Bs, Bs)
                else:
                    P.op("dve", lambda e, cs=cs, w=w: e.tensor_reduce(
                        out=sums[:, cs], in_=stp3[:, 16 - w:15, cs].rearrange("p r c -> p c r"), axis=AX.X, op=ALU.add),
                        Bst, Bs)
                    TT(sums[:, cs], sums[:, cs], u_s[:, cs], ALU.add, Bs, Bs)
                STT(dd[:, cs], sums[:, cs], 1.0 / w, u_s[:, cs], ALU.mult, ALU.subtract, Bs, Bs)
            k2 = ps_all.next()
            TRG([(bank_bf(k2)[:, c * 16:(c + 1) * 16], dd[:, c * 128:(c + 1) * 128], identb[0:16, 0:16]) for c in range(2)],
                Bs + [B_const], [B_ps[k2]])
            dT = pgf(10).bitcast(BF16)[:, 0:32]
            ACT(dT, bank_bf(k2)[:, 0:32], AF.Copy, [B_ps[k2]], pgb(10))
            k3 = ps_all.next()
            MMG([(bank(k3)[:, c * 16:(c + 1) * 16], wblk[:, c, :], dT[:, c * 16:(c + 1) * 16], True, True) for c in range(2)],
                pgb(10) + [B_wblk], [B_ps[k3]])
            for c in range(2):
                ACT(mixT[:, c, NT:NTOK], bank(k3)[:, c * 16:(c + 1) * 16], AF.Copy, [B_ps[k3], B_const], [B_mix[c]],
                    scale=pscale[:, c:c + 1])
            DMA("sp", misc(), [(nps[:, 0:14 * 256], stp[:, 256:3840]), (nps[:, 14 * 256:15 * 256], u_s)], Bst + Bs, ())

        def gen_conv_prompt(c, t, sl, cx):
            ps = cx.ps
            T0 = t * 512
            tk = slice(T0, T0 + 512)
            Rw = [B_ring[sl], B_hT[t]]
            kc_, kh, kb = cx.banks[0], cx.banks[1], cx.banks[2]
            for (pk, off) in ((kc_, 128), (kh, 256), (kb, 0)):
                MMG([(bank(pk), ring[:, sl, k8 * 384 + off:k8 * 384 + off + 128], hT[:, k8, tk], k8 == 0, k8 == 7)
                     for k8 in range(8)], Rw, [B_ps[pk]])
            yield
            pg = cx.pp
            cgs = VPGF(pg[0])
            ze = arena[:, pg[1] * 512:(pg[1] + 2) * 512][:, 0:514]
            c1 = VPGF(pg[3])
            c2 = VPGF(pg[4])
            Bc, Bz, B1, B2 = VPGB(pg[0]), pgb(pg[1], 2), VPGB(pg[3]), VPGB(pg[4])
            if t == 0:
                MEMSET(ze[:, 0:2], 0.0, Bz)
            ACT(cgs, bank(kc_), AF.Copy, [B_ps[kc_]], Bc)
            yield
            TT(ze[:, 2:514], bank(kh), cgs, ALU.mult, [B_ps[kh]] + Bc, Bz)
            yield
            w0 = convwf[:, c * 3 + 0:c * 3 + 1]
            w1 = convwf[:, c * 3 + 1:c * 3 + 2]
            w2 = convwf[:, c * 3 + 2:c * 3 + 3]
            ACT(c1, ze[:, 0:512], AF.Copy, Bz + [B_const], B1, scale=w0)
            yield
            STT(c2, ze[:, 1:513], w1, c1, ALU.mult, ALU.add, Bz + B1 + [B_const], B2)
            yield
            STT(c1, ze[:, 2:514], w2, c2, ALU.mult, ALU.add, Bz + B2 + [B_const], B1)
            yield
            TT(mixT[:, c, tk], bank(kb), c1, ALU.mult, [B_ps[kb]] + B1, [B_mix[c]])
            if t < 3:
                CP(ze[:, 0:2], ze[:, 512:514], Bz, Bz)
            else:
                kx = cx.banks[0]
                TRG([(bank(kx)[0:2, 0:128], ze[:, 512:514], identf[:, :])], Bz + [B_const], [B_ps[kx]])
                stg = VPGF(pg[0])[0:2, 0:128]
                ACT(stg, bank(kx)[0:2, 0:128], AF.Copy, [B_ps[kx]], Bc)
                DMA("sp", misc(), [(ncp[:, c * 128:(c + 1) * 128], stg)], Bc, ())
            yield

        def conv_sample_part1(c, cx):
            sl = (U_CONV + c) % NSLOT
            Rw = [B_ring[sl], B_hT[4]]
            S_ = cx.small
            Bs = S_.bufs()
            prev = S_.slot(0, n=2).rearrange("p (r c) -> p r c", c=128)
            wbc = S_.slot(4, n=3).rearrange("p (r c) -> p r c", c=128)
            DMA("sp", misc(), [(prev, st_conv[:, :, c * 128:(c + 1) * 128]),
                               (wbc, convw_d[:, c * 128:(c + 1) * 128].partition_broadcast(16))], (), Bs)
            kp = cx.banks[3]
            MMG([(bank(kp)[0:16, 0:384], hT[:, k8, NT:NTOK], ring[:, sl, k8 * 384:(k8 + 1) * 384], k8 == 0, k8 == 7)
                 for k8 in range(8)], Rw, [B_ps[kp]])

        def gen_conv_sample_part2(c, cx):
            S_ = cx.small
            Bs = S_.bufs()
            prev = S_.slot(0, n=2).rearrange("p (r c) -> p r c", c=128)
            wbc = S_.slot(4, n=3).rearrange("p (r c) -> p r c", c=128)
            kp = cx.banks[3]
            cg_s, z_s, a_, b_ = S_.slot(2), S_.slot(3), S_.slot(7), S_.slot(8)
            ACT(cg_s, bank(kp)[0:16, 128:256], AF.Copy, [B_ps[kp]], Bs)
            yield
            TT(z_s, bank(kp)[0:16, 256:384], cg_s, ALU.mult, [B_ps[kp]] + Bs, Bs)
            TT(a_, prev[:, 0, :], wbc[:, 0, :], ALU.mult, Bs, Bs)
            yield
            TT(b_, prev[:, 1, :], wbc[:, 1, :], ALU.mult, Bs, Bs)
            TT(a_, a_, b_, ALU.add, Bs, Bs)
            yield
            TT(b_, z_s, wbc[:, 2, :], ALU.mult, Bs, Bs)
            TT(a_, a_, b_, ALU.add, Bs, Bs)
            yield
            yb = S_.slot(9).bitcast(BF16)[:, 0:128]
            TT(yb, bank(kp)[0:16, 0:128], a_, ALU.mult, [B_ps[kp]] + Bs, Bs)
            yield
            k2 = kp
            TRG([(bank_bf(k2)[:, 0:16], yb, identb[0:16, 0:16])], Bs + [B_const], [B_ps[k2]])
            yield
            ACT(mixT[:, c, NT:NTOK], bank_bf(k2)[:, 0:16], AF.Copy, [B_ps[k2]], [B_mix[c]])
            DMA("sp", misc(), [(ncs[:, 0, c * 128:(c + 1) * 128], prev[:, 1, :]),
                               (ncs[:, 1, c * 128:(c + 1) * 128], z_s)], Bs, ())
            yield

        def gen_conv(c, cx):
            sl = (U_CONV + c) % NSLOT
            for t in range(4):
                yield from gen_conv_prompt(c, t, sl, cx)

        for ci in range(2):
            MEMSET(gsm2[ci][:, 63:64], 1.0, [B_gsm2[ci]])
        for b in list(range(16)) + [16]:
            emit_norm(b, 0)
            flush_norm(keep=1)
        flush_norm()
        norm_done = [16]

        def gen_init_norms():
            norm_junk[0] = [NPAGE + 2, NPAGE + 3]
            for b in range(4, 16):
                emit_norm(b, 0)
                flush_norm(keep=1)
                norm_done[0] = b + 1
                yield
            flush_norm()
            norm_junk[0] = [15, 16, 17]
            yield

        hcx = [make_cx(0, list(range(0, 10)), 0, 4, 5, 6, [7, 8, 9], [0, 1, 2, 3]),
               make_cx(1, list(range(10, 20)), 10, 14, 15, 16, [17, 18, 19], [4, 5, 6, 7])]
        for h0 in (0, 2, 4):
            need(U_HEAD + h0)
            gens = [gen_head(h0, hcx[0]), gen_head(h0 + 1, hcx[1])]
            interleave(gens, HEAD_OFFSET)
        sl = need(U_POOL)
        interleave([gen_pool_chunk(0, sl), gen_pool_chunk(1, sl)])
        emit_pool_sample(sl)
        emit_outproj(U_WO, 1)
        emit_mlp(U_FF0, 2)
        ccx = [make_cx(0, [0, 1, 2, 3, 4], 0, 0, 0, 0, [12, 13, 14], [0, 1, 2, 3]),
               make_cx(1, [6, 7, 8, 9, 10], 0, 0, 0, 0, [15, 16, 17], [4, 5, 6, 7])]
        pend = []
        for c0 in (0, 2, 4, 6):
            need(U_CONV + c0)
            interleave([gen_conv(c0, ccx[0]), gen_conv(c0 + 1, ccx[1])] + pend)
            conv_sample_part1(c0, ccx[0])
            conv_sample_part1(c0 + 1, ccx[1])
            pend = [gen_conv_sample_part2(c0, ccx[0]), gen_conv_sample_part2(c0 + 1, ccx[1])]
        interleave(pend)
        emit_outproj(U_OO, 3)
        DMA("sp", misc(), [(pgf(10, 2), gfin_d.partition_broadcast(128))], (), pgb(10, 2))
        emit_mlp(U_FF1, 4)

        final_waits = [(d.h, d.count) for d in P.dsems if d.count > 0]

        def make_body(name, final=False):
            def body(eng):
                for waits, fn, inc in P.streams[name]:
                    embed = None
                    if EMBED_WAITS and inc is True and waits:
                        embed = waits[-1]
                        waits = waits[:-1]
                    for sem, val in waits:
                        eng.wait_ge(sem, val)
                    ins = fn(eng)
                    if embed is not None:
                        ins._wait_ge(embed[0], embed[1])
                    if inc:
                        ins.then_inc(P.esem[name], 1)
                if final:
                    for h_, v_ in final_waits:
                        eng.wait_ge(h_, v_)
            return body

        block.sync(make_body("sp", final=True))
        block.scalar(make_body("act"))
        block.vector(make_body("dve"))
        block.gpsimd(make_body("pool"))
        block.tensor(make_body("pe"))
    return nc


def _unit_k1024(W, cols):
    sub = W[:, cols]
    n = sub.shape[1]
    u = sub.reshape(8, 128, n).transpose(1, 0, 2).reshape(128, 8 * n)
    out = np.zeros((128, 4096), np.float32)
    out[:, :8 * n] = u
    return out


def _unit_w2(W2, s):
    sub = W2[s * 512:(s + 1) * 512, :]
    return np.ascontiguousarray(sub.reshape(4, 128, 1024).transpose(1, 0, 2).reshape(128, 4096))


def _build_wstream(even_w_in, even_w_out, odd_w_in, odd_w_out, ff_w1, ff_w2):
    wst = np.zeros((NU, 128, 4096), np.float32)
    win = even_w_in[0]
    for h in range(6):
        cols = np.concatenate([256 + r * 768 + h * 128 + np.arange(128) for r in range(4)])
        wst[U_HEAD + h] = _unit_k1024(win, cols)
    wst[U_POOL] = _unit_k1024(win, np.arange(256))
    for half in range(2):
        wst[U_WO + half] = _unit_k1024(even_w_out[0], half * 512 + np.arange(512))
        wst[U_OO + half] = _unit_k1024(odd_w_out[0], half * 512 + np.arange(512))
    for l, u0 in ((0, U_FF0), (1, U_FF1)):
        for s in range(8):
            wst[u0 + 2 * s] = _unit_k1024(ff_w1[l], s * 512 + np.arange(512))
            wst[u0 + 2 * s + 1] = _unit_w2(ff_w2[l], s)
    for c in range(8):
        cols = np.concatenate([r * 1024 + c * 128 + np.arange(128) for r in range(3)])
        wst[U_CONV + c] = _unit_k1024(odd_w_in[0], cols)
    return wst


_NC_CACHE = {}


def kernel(x_prompt, x_sample, state_pool, state_hgrn, state_conv, norm_mix, norm_mlp, norm_final,
           even_w_in, pool_w, pool_scale, hgrn_lb_logits, hgrn_gain, even_w_out, odd_w_in, conv_w,
           odd_w_out, ff_w1, ff_w2):
    f = lambda a: np.ascontiguousarray(np.asarray(a, dtype=np.float32))
    x_prompt, x_sample, state_pool, state_hgrn, state_conv = map(f, (x_prompt, x_sample, state_pool, state_hgrn, state_conv))
    norm_mix, norm_mlp, norm_final = f(norm_mix), f(norm_mlp), f(norm_final)
    even_w_in, pool_w, pool_scale, hgrn_lb_logits, hgrn_gain = map(f, (even_w_in, pool_w, pool_scale, hgrn_lb_logits, hgrn_gain))
    even_w_out, odd_w_in, conv_w, odd_w_out, ff_w1, ff_w2 = map(f, (even_w_out, odd_w_in, conv_w, odd_w_out, ff_w1, ff_w2))

    if "nc" not in _NC_CACHE:
        _NC_CACHE["nc"] = build()
    nc = _NC_CACHE["nc"]

    wst = _build_wstream(even_w_in, even_w_out, odd_w_in, odd_w_out, ff_w1, ff_w2)
    norms = np.stack([norm_mix[0], norm_mlp[0], norm_mix[1], norm_mlp[1]])
    gcols = np.ascontiguousarray(norms.reshape(4, 8, 128).transpose(2, 0, 1).reshape(128, 32))
    lbl = np.ascontiguousarray(hgrn_lb_logits.reshape(3, 6, 128).transpose(2, 0, 1).reshape(128, 18))
    gainf = np.ascontiguousarray(hgrn_gain[0].reshape(6, 128).T)
    pscalef = np.ascontiguousarray(pool_scale[0].reshape(2, 128).T)
    convwf = np.ascontiguousarray(conv_w[0].reshape(3, 8, 128).transpose(2, 1, 0).reshape(128, 24))
    identf = np.eye(128, dtype=np.float32)
    identb = identf.astype(ml_dtypes.bfloat16)
    maskT = np.triu(np.ones((128, 128), np.float32))
    onesm = np.full((128, 128), 1.0 / 128, np.float32)
    id16 = np.eye(16, dtype=np.float32)
    cfixw = np.zeros((128, 2, 16), np.float32)
    for c in range(2):
        for p in range(128):
            w = 2 ** (2 * c + p // 64 + 1)
            for t in range(16):
                cfixw[p, c, t] = 1.0 / min(w, t + 1)
    cfixw = cfixw.reshape(128, 32)

    shared = dict(wst=wst, gcols=gcols, gfin=norm_final.reshape(1, 1024), lbl=lbl, gainf=gainf, pscalef=pscalef,
                  poolw=pool_w[0], convwf=convwf, convw=conv_w[0], c_identb=identb, c_identf=identf, c_maskT=maskT,
                  c_onesm=onesm, c_id16=id16, c_cfixw=cfixw)
    in_maps = []
    for c in range(8):
        m = dict(shared)
        m["xp"] = x_prompt[c].reshape(16, 128, 1024)
        m["xs"] = x_sample[16 * c:16 * (c + 1), 0, :]
        m["st_pool"] = state_pool[0, 16 * c:16 * (c + 1)].reshape(16, 15 * 256)
        m["st_hgrn"] = state_hgrn[0, 16 * c:16 * (c + 1)]
        m["st_conv"] = state_conv[0, 16 * c:16 * (c + 1)]
        in_maps.append({k: np.ascontiguousarray(v) for k, v in m.items()})

    res = run_bass_kernel_spmd(nc, in_maps, core_ids=list(range(8)))
    R = res.results
    y_prompt = np.stack([R[c]["yp"].reshape(2048, 1024) for c in range(8)]).astype(np.float32)
    y_sample = np.concatenate([R[c]["ys"] for c in range(8)]).reshape(128, 1, 1024).astype(np.float32)
    new_pool_prompt = np.stack([R[c]["npp"] for c in range(8)])[None].astype(np.float32)
    new_hgrn_prompt = np.stack([R[c]["nhp"] for c in range(8)])[None].astype(np.float32)
    new_conv_prompt = np.stack([R[c]["ncp"] for c in range(8)])[None].astype(np.float32)
    new_pool_sample = np.concatenate([R[c]["nps"].reshape(16, 15, 256) for c in range(8)])[None].astype(np.float32)
    new_hgrn_sample = np.concatenate([R[c]["nhs"] for c in range(8)])[None].astype(np.float32)
    new_conv_sample = np.concatenate([R[c]["ncs"] for c in range(8)])[None].astype(np.float32)
    return (y_prompt, y_sample, new_pool_prompt, new_hgrn_prompt, new_conv_prompt,
            new_pool_sample, new_hgrn_sample, new_conv_sample)
```

```python
import numpy as np
import ml_dtypes
from contextlib import ExitStack
import concourse.bass as bass
import concourse.mybir as mybir
from concourse.bass_utils import run_bass_kernel_spmd

F32 = mybir.dt.float32
BF16 = mybir.dt.bfloat16
AF = mybir.ActivationFunctionType
ALU = mybir.AluOpType
AX = mybir.AxisListType

NT = 2048
NSMP = 16
NTOK = NT + NSMP
EPS = 1e-6
NSLOT = 4
NPAGE = 18
SAME_ENGINE_SYNC = True
EMBED_WAITS = True
HEAD_OFFSET = 0
PREFETCH_F = False

U_HEAD = 0
U_POOL = 6
U_WO = 7
U_FF0 = 9
U_CONV = 25
U_OO = 33
U_FF1 = 35
NU = 51


class Buf:
    __slots__ = ("name", "w", "r")

    def __init__(self, name):
        self.name = name
        self.w = None
        self.r = {}


class DSem:
    def __init__(self, h):
        self.h = h
        self.count = 0


class Prog:
    ENG = ("sp", "act", "dve", "pool", "pe")

    def __init__(self):
        self.streams = {e: [] for e in self.ENG}
        self.seq = {e: 0 for e in self.ENG}
        self.waited = {e: {} for e in self.ENG}
        self.esem = {}
        self.dsems = []

    def _waits(self, eng, reads, writes, extra=()):
        need = {}

        def add(tok):
            if tok is None:
                return
            if tok[0] == "e":
                if tok[1] == eng and (eng == "pe" or not SAME_ENGINE_SYNC):
                    return
                k = ("e", tok[1])
            else:
                k = ("d", id(tok[1]))
            if self.waited[eng].get(k, 0) >= tok[2]:
                return
            if k not in need or need[k][2] < tok[2]:
                need[k] = tok

        for b in reads:
            add(b.w)
        for b in writes:
            add(b.w)
            for t in b.r.values():
                add(t)
        for t in extra:
            add(t)
        out = []
        for k, tok in need.items():
            self.waited[eng][k] = tok[2]
            sem = self.esem[tok[1]] if tok[0] == "e" else tok[1].h
            out.append((sem, tok[2]))
        return out

    @staticmethod
    def _mark(tok, reads, writes):
        k = ("e", tok[1]) if tok[0] == "e" else ("d", id(tok[1]))
        for b in reads:
            b.r[k] = tok
        for b in writes:
            b.w = tok
            b.r = {}

    def op(self, eng, fn, reads=(), writes=(), multi=False):
        waits = self._waits(eng, reads, writes)
        self.seq[eng] += 1
        tok = ("e", eng, self.seq[eng])
        self.streams[eng].append((waits, fn, "multi" if multi else True))
        self._mark(tok, reads, writes)
        return tok

    def dma(self, q, dsem, fns, reads=(), writes=()):
        extra = []
        if dsem.count > 0:
            extra.append(("d", dsem, dsem.count))
        waits = self._waits(q, reads, writes, extra)
        dsem.count += 16 * len(fns)
        tok = ("d", dsem, dsem.count)

        def run(eng, fns=fns, h=dsem.h):
            for f in fns:
                f(eng).then_inc(h, 16)
            return None

        self.streams[q].append((waits, run, False))
        self._mark(tok, reads, writes)
        return tok


def build():
    nc = bass.Bass("TRN2", target_bir_lowering=False)

    def din(name, shape, dt=F32):
        return nc.dram_tensor(name, shape, dt, kind="ExternalInput").ap()

    def dout(name, shape, dt=F32):
        return nc.dram_tensor(name, shape, dt, kind="ExternalOutput").ap()

    xp = din("xp", [16, 128, 1024])
    xs = din("xs", [16, 1024])
    st_pool = din("st_pool", [16, 15 * 256])
    st_hgrn = din("st_hgrn", [16, 6, 128, 128])
    st_conv = din("st_conv", [16, 2, 1024])
    wst = din("wst", [NU, 128, 4096])
    gcols_d = din("gcols", [128, 32])
    gfin_d = din("gfin", [1, 1024])
    lbl_d = din("lbl", [128, 18])
    gain_d = din("gainf", [128, 6])
    pscale_d = din("pscalef", [128, 2])
    poolw_d = din("poolw", [4, 64, 64])
    convwf_d = din("convwf", [128, 24])
    convw_d = din("convw", [3, 1024])
    c_identb = din("c_identb", [128, 128], BF16)
    c_identf = din("c_identf", [128, 128])
    c_maskT = din("c_maskT", [128, 128])
    c_onesm = din("c_onesm", [128, 128])
    c_id16 = din("c_id16", [16, 16])
    c_cfixw = din("c_cfixw", [128, 32])

    yp = dout("yp", [16, 128, 1024])
    ys = dout("ys", [16, 1024])
    npp = dout("npp", [15, 256])
    nhp = dout("nhp", [6, 128, 128])
    ncp = dout("ncp", [2, 1024])
    nps = dout("nps", [16, 15 * 256])
    nhs = dout("nhs", [16, 6, 128, 128])
    ncs = dout("ncs", [16, 2, 1024])

    P = Prog()
    with ExitStack() as es:
        E = es.enter_context

        def sb(name, shape, dt=F32):
            return E(nc.sbuf_tensor(name, shape, dt))

        x_sb = sb("x_sb", [128, 16, 1024])
        xs_sb = sb("xs_sb", [128, 1024])
        hT = sb("hT", [128, 8, NTOK], BF16)
        mixT = sb("mixT", [128, 8, NTOK], BF16)
        ring = sb("ring", [128, NSLOT, 4096], BF16)
        arena = sb("arena", [128, NPAGE * 512])
        identb = sb("identb", [128, 128], BF16)
        identf = sb("identf", [128, 128])
        maskT = sb("maskT", [128, 128])
        onesm = sb("onesm", [128, 128])
        onesb = sb("onesb", [128, 128], BF16)
        wblk = sb("wblk", [128, 2, 128], BF16)
        gcols = sb("gcols_s", [128, 32])
        lbl = sb("lbl_s", [128, 18])
        hcols = sb("hcols", [128, 32])
        pscale = sb("pscale_s", [128, 2])
        cfixw = sb("cfixw_s", [128, 32])
        convwf = sb("convwf_s", [128, 24])
        id16 = sb("id16_s", [128, 16])
        nrm = sb("nrm", [128, 32])
        tmpc = sb("tmpc", [128, 16])
        gsm2 = [sb(f"gsm{i}", [128, 64]) for i in range(2)]
        Sst2 = [sb(f"Sst{i}", [128, 128]) for i in range(2)]
        Spb2 = [sb(f"Spb{i}", [128, 2, 128], BF16) for i in range(2)]
        tSb2 = [sb(f"tSb{i}", [128, 128]) for i in range(2)]
        psT = [E(nc.psum_tensor(f"pp{i}", [128, 1024], F32)) for i in range(4)]

        for e in Prog.ENG:
            P.esem[e] = E(nc.semaphore(f"es_{e}"))

        def new_dsem(name):
            d = DSem(E(nc.semaphore(name)))
            P.dsems.append(d)
            return d

        ring_ds = [new_dsem(f"ds_ring{i}") for i in range(NSLOT)]
        xy_ds = [new_dsem(f"ds_xy{i}") for i in range(17)]
        misc_ds = [new_dsem(f"ds_misc{i}") for i in range(8)]
        misc_ptr = [0]

        def misc():
            d = misc_ds[misc_ptr[0] % len(misc_ds)]
            misc_ptr[0] += 1
            return d

        block = E(nc.Block())

        B_x = [Buf(f"x{b}") for b in range(16)] + [Buf("xs")]
        B_hT = [Buf(f"hT{t}") for t in range(5)]
        B_mix = [Buf(f"mix{c}") for c in range(8)]
        B_ring = [Buf(f"ring{i}") for i in range(NSLOT)]
        B_pg = [Buf(f"pg{i}") for i in range(NPAGE)]
        B_ps = [Buf(f"ps{i}") for i in range(8)]
        B_const = Buf("const")
        B_wblk = Buf("wblk")
        B_hcols = Buf("hcols")
        B_nrm = [Buf("nrm0"), Buf("nrm1"), Buf("nrm1b")]
        B_tmpc = Buf("tmpc")
        B_dummy = [Buf("dummy0"), Buf("dummy1")]
        B_gsm2 = [Buf("gsm0"), Buf("gsm1")]
        B_S2 = [Buf("S0"), Buf("S1")]
        B_Sp2 = [[Buf("Sp00"), Buf("Sp01")], [Buf("Sp10"), Buf("Sp11")]]
        B_tS2 = [Buf("tS0"), Buf("tS1")]
        B_nrm2 = [Buf("nrm2"), Buf("nrm3")]

        def ACT(out, in_, func, R, W, scale=1.0, bias=0.0, accum=None):
            def f(e):
                if accum is None:
                    return e.activation(out=out, in_=in_, func=func, bias=bias, scale=scale)
                return e.activation(out=out, in_=in_, func=func, bias=bias, scale=scale, accum_out=accum)
            return P.op("act", f, R, W, multi=accum is not None)

        def TT(out, in0, in1, op, R, W, eng="dve"):
            return P.op(eng, lambda e: e.tensor_tensor(out=out, in0=in0, in1=in1, op=op), R, W)

        def TS(out, in0, s1, op0, R, W, s2=None, op1=None, eng="dve"):
            def f(e):
                if op1 is None:
                    return e.tensor_scalar(out=out, in0=in0, scalar1=s1, scalar2=None, op0=op0)
                return e.tensor_scalar(out=out, in0=in0, scalar1=s1, scalar2=s2, op0=op0, op1=op1)
            return P.op(eng, f, R, W)

        def STT(out, in0, scalar, in1, op0, op1, R, W):
            return P.op("dve", lambda e: e.scalar_tensor_tensor(out=out, in0=in0, scalar=scalar, in1=in1,
                                                                 op0=op0, op1=op1), R, W)

        def CP(out, in_, R, W, eng="dve"):
            return P.op(eng, lambda e: e.tensor_copy(out=out, in_=in_), R, W)

        def MEMSET(ap, val, W, eng="dve"):
            return P.op(eng, lambda e: e.memset(ap, val), (), W)

        def MMG(mms, R, W):
            def f(e):
                ins = None
                for (o, l, r, s, t) in mms:
                    ins = e.matmul(o, l, r, start=s, stop=t)
                return ins
            return P.op("pe", f, R, W, multi=True)

        def TRG(trs, R, W):
            def f(e):
                ins = None
                for (o, i, idn) in trs:
                    ins = e.transpose(out=o, in_=i, identity=idn)
                return ins
            return P.op("pe", f, R, W, multi=True)

        def DMA(q, dsem, pairs, R, W, **kw):
            fns = [(lambda e, o=o, i=i: e.dma_start(out=o, in_=i, **kw)) for (o, i) in pairs]
            return P.dma(q, dsem, fns, R, W)

        def bank(k):
            return psT[k // 2][:, (k % 2) * 512:(k % 2 + 1) * 512]

        def bank_bf(k):
            return bank(k).bitcast(BF16)

        def pair(i):
            return psT[i][:, :]

        class Rot:
            def __init__(self, items):
                self.items = items
                self.i = 0

            def next(self):
                v = self.items[self.i % len(self.items)]
                self.i += 1
                return v

        def pgf(p, n=1):
            return arena[:, p * 512:(p + n) * 512]

        def pgb(p, n=1):
            return [B_pg[i] for i in range(p, p + n)]

        unit_cols = {}
        for u in range(NU):
            unit_cols[u] = 4096
        unit_cols[U_POOL] = 2048
        for c in range(8):
            unit_cols[U_CONV + c] = 3072
        next_load = [0]

        def issue_load(u):
            s = u % NSLOT
            ncol = unit_cols[u]
            pairs = []
            c0 = 0
            while c0 < ncol:
                c1 = min(c0 + 2048, ncol)
                pairs.append((ring[:, s, c0:c1], wst[u, :, c0:c1]))
                c0 = c1
            DMA("pool", ring_ds[s], pairs, (), [B_ring[s]])

        def need(u, la=NSLOT - 1):
            lim = min(u + la, NU - 1)
            while next_load[0] <= lim:
                issue_load(next_load[0])
                next_load[0] += 1
            return u % NSLOT

        def tokslice(t):
            if t < 4:
                return slice(t * 512, (t + 1) * 512)
            return slice(NT, NTOK)

        need(0)
        DMA("sp", misc(), [
            (identb[:, :], c_identb), (identf[:, :], c_identf), (maskT[:, :], c_maskT), (onesm[:, :], c_onesm),
            (gcols[:, :], gcols_d), (lbl[:, :], lbl_d), (hcols[:, 18:24], gain_d), (pscale[:, :], pscale_d),
            (cfixw[:, :], c_cfixw), (convwf[:, :], convwf_d), (id16[0:16, :], c_id16),
        ], (), [B_const, B_hcols])
        for b in range(16):
            DMA("sp", xy_ds[b], [(x_sb[:, b, :], xp[b])], (), [B_x[b]])
        DMA("sp", xy_ds[16], [(xs_sb[0:16, :], xs)], (), [B_x[16]])
        MEMSET(wblk[:, :, :], 0.0, [B_wblk])
        DMA("pool", new_dsem("ds_wblk"), [
            (wblk[(g % 2) * 64:(g % 2) * 64 + 64, g // 2, (g % 2) * 64:(g % 2) * 64 + 64], poolw_d[g]) for g in range(4)
        ], (), [B_wblk])
        ACT(onesb[:, :], onesm[:, :], AF.Copy, [B_const], [B_const])
        ACT(lbl[:, :], lbl[:, :], AF.Exp, [B_const], [B_const])
        TT(tmpc[:, 0:6], lbl[:, 0:6], lbl[:, 6:12], ALU.add, [B_const], [B_tmpc])
        TT(tmpc[:, 0:6], tmpc[:, 0:6], lbl[:, 12:18], ALU.add, [B_const, B_tmpc], [B_tmpc])
        P.op("dve", lambda e: e.reciprocal(out=tmpc[:, 6:12], in_=tmpc[:, 0:6]), [B_tmpc], [B_tmpc])
        TT(hcols[:, 0:6], lbl[:, 0:6], tmpc[:, 6:12], ALU.mult, [B_const, B_tmpc], [B_hcols])
        TS(hcols[:, 6:12], hcols[:, 0:6], -1.0, ALU.mult, [B_hcols], [B_hcols], s2=1.0, op1=ALU.add)
        TS(hcols[:, 12:18], hcols[:, 6:12], -1.0, ALU.mult, [B_hcols], [B_hcols])
        ACT(hcols[:, 24:30], hcols[:, 6:12], AF.Ln, [B_hcols], [B_hcols])
        LB, OML, NOML, GAIN, LNOML = 0, 6, 12, 18, 24

        ps_norm = Rot([0, 1])
        norm_ctr = [0]
        norm_pending = []
        norm_junk = [[15, 16, 17]]

        def emit_norm(b, n):
            nj = len(norm_junk[0])
            k = norm_ctr[0] % nj
            norm_ctr[0] += 1
            np_ = 128 if b < 16 else 16
            xb = x_sb[:, b, :] if b < 16 else xs_sb[0:16, :]
            Bx = B_x[b]
            junk = VPGF(norm_junk[0][k])[0:np_, :].bitcast(BF16)
            Bj = VPGB(norm_junk[0][k])
            ss = nrm[0:np_, 4 * k + 0:4 * k + 1]
            lt = nrm[0:np_, 4 * k + 1:4 * k + 2]
            rs = nrm[0:np_, 4 * k + 2:4 * k + 3]
            Bn = [B_nrm[k]]
            ACT(junk, xb, AF.Square, [Bx], Bj + Bn, accum=ss)
            ACT(lt, ss, AF.Ln, Bn, Bn, scale=1.0 / 1024, bias=EPS)
            ACT(rs, lt, AF.Exp, Bn, Bn, scale=-0.5)
            if n == 4:
                gfin = pgf(10, 2)
                STT(xb, xb, rs, gfin[0:np_, :], ALU.mult, ALU.mult, [Bx] + Bn + pgb(10, 2), [Bx])
                if b < 16:
                    DMA("sp", xy_ds[b], [(yp[b], x_sb[:, b, :])], [Bx], ())
                else:
                    DMA("sp", xy_ds[16], [(ys, xs_sb[0:16, :])], [Bx], ())
                return
            if n == 0:
                TS(junk, xb, rs, ALU.mult, [Bx] + Bn, Bj)
            else:
                ACT(junk, xb, AF.Copy, [Bx] + Bn, Bj, scale=rs)

            def part_b():
                pk = ps_norm.next()
                pb = bank_bf(pk)
                if b < 16:
                    TRG([(pb[:, c * 128:(c + 1) * 128], junk[:, c * 128:(c + 1) * 128], identb[:, :]) for c in range(8)],
                        Bj + [B_const], [B_ps[pk]])
                    t = b // 4
                    TT(hT[:, :, b * 128:(b + 1) * 128], pb[:, 0:1024].rearrange("p (c t) -> p c t", t=128),
                       gcols[:, n * 8:(n + 1) * 8].unsqueeze(2).broadcast_to([128, 8, 128]), ALU.mult,
                       [B_ps[pk], B_const], [B_hT[t]])
                else:
                    TRG([(pb[:, c * 16:(c + 1) * 16], junk[:, c * 128:(c + 1) * 128], identb[0:16, 0:16]) for c in range(8)],
                        Bj + [B_const], [B_ps[pk]])
                    TT(hT[:, :, NT:NTOK], pb[:, 0:128].rearrange("p (c t) -> p c t", t=16),
                       gcols[:, n * 8:(n + 1) * 8].unsqueeze(2).broadcast_to([128, 8, 16]), ALU.mult,
                       [B_ps[pk], B_const], [B_hT[4]])
            norm_pending.append(part_b)

        def flush_norm(keep=0):
            while len(norm_pending) > keep:
                norm_pending.pop(0)()

        ps_pair = Rot([2, 3])

        def emit_outproj(u0, norm_after):
            need(u0)
            s0, s1 = u0 % NSLOT, (u0 + 1) % NSLOT
            for b in list(range(16)) + [16]:
                pi = ps_pair.next()
                pp = pair(pi)
                tk = slice(b * 128, (b + 1) * 128) if b < 16 else slice(NT, NTOK)
                np_ = 128 if b < 16 else 16
                mms = []
                for half, s in ((0, s0), (1, s1)):
                    for kc in range(8):
                        mms.append((pp[0:np_, half * 512:(half + 1) * 512], mixT[:, kc, tk],
                                    ring[:, s, kc * 512:(kc + 1) * 512], kc == 0, kc == 7))
                MMG(mms, B_mix + [B_ring[s0], B_ring[s1]], [B_ps[2 * pi], B_ps[2 * pi + 1]])
                xb = x_sb[:, b, :] if b < 16 else xs_sb[0:16, :]
                TT(xb, xb, pp[0:np_, :], ALU.add, [B_x[b], B_ps[2 * pi], B_ps[2 * pi + 1]], [B_x[b]])
                emit_norm(b, norm_after)
                flush_norm(keep=2)
            flush_norm()

        def emit_mlp(u0, norm_after):
            ps_single = Rot([0, 1, 2, 3])
            ps_pr = Rot([2, 3])
            steps = [(s, t) for s in range(8) for t in range(5)]
            hid = [mixT[:, 0, 0:2048].rearrange("p (c t) -> p c t", t=512),
                   mixT[:, 1, 0:2048].rearrange("p (c t) -> p c t", t=512)]
            hid_s = [mixT[:, 2, 0:64].rearrange("p (c t) -> p c t", t=16),
                     mixT[:, 3, 0:64].rearrange("p (c t) -> p c t", t=16)]
            B_hid = [B_mix[0], B_mix[1]]
            B_hidc = [[Buf(f"hid{i}_{n}") for n in range(4)] for i in range(2)]
            B_hids = [B_mix[2], B_mix[3]]

            def ff1_items(k):
                s, t = steps[k]
                items = []
                if t == 0:
                    items.append(lambda s=s: need(u0 + 2 * s, NSLOT - 2))
                if t == 2:
                    items.append(lambda s=s: need(u0 + 2 * s + 1, NSLOT - 2))
                sl = (u0 + 2 * s) % NSLOT
                if t < 4:
                    hb = hid[k % 2]
                    for n in range(4):
                        def it(n=n, hb=hb, sl=sl, t=t, k=k):
                            pk = ps_single.next()
                            MMG([(bank(pk), ring[:, sl, kc * 512 + n * 128:kc * 512 + (n + 1) * 128],
                                  hT[:, kc, tokslice(t)], kc == 0, kc == 7) for kc in range(8)],
                                [B_ring[sl], B_hT[t]], [B_ps[pk]])
                            Wh = [B_hidc[k % 2][n]] + ([B_hid[k % 2]] if k < 2 else [])
                            ACT(hb[:, n, :], bank(pk), AF.Relu, [B_ps[pk]], Wh)
                            TT(hb[:, n, :], hb[:, n, :], hb[:, n, :], ALU.mult, [B_hidc[k % 2][n]], [B_hidc[k % 2][n]])
                        items.append(it)
                else:
                    hb = hid_s[s % 2]

                    def it(hb=hb, sl=sl, s=s):
                        pk = ps_single.next()
                        mms = []
                        for n in range(4):
                            for kc in range(8):
                                mms.append((bank(pk)[:, n * 16:(n + 1) * 16],
                                            ring[:, sl, kc * 512 + n * 128:kc * 512 + (n + 1) * 128],
                                            hT[:, kc, NT:NTOK], kc == 0, kc == 7))
                        MMG(mms, [B_ring[sl], B_hT[4]], [B_ps[pk]])
                        ACT(hb, bank(pk)[:, 0:64].rearrange("p (c t) -> p c t", t=16), AF.Relu, [B_ps[pk]], [B_hids[s % 2]])
                        TT(hb, hb, hb, ALU.mult, [B_hids[s % 2]], [B_hids[s % 2]])
                    items.append(it)
                return items

            def ff2_items(k):
                s, t = steps[k]
                sl2 = (u0 + 2 * s + 1) % NSLOT
                items = []
                if t < 4:
                    hb = hid[k % 2]
                    for j in range(4):
                        def it(j=j, hb=hb, sl2=sl2, t=t, s=s, k=k):
                            b = t * 4 + j
                            pi = ps_pr.next()
                            pp = pair(pi)
                            mms = []
                            for half in range(2):
                                for kc in range(4):
                                    mms.append((pp[:, half * 512:(half + 1) * 512], hb[:, kc, j * 128:(j + 1) * 128],
                                                ring[:, sl2, kc * 1024 + half * 512:kc * 1024 + (half + 1) * 512],
                                                kc == 0, kc == 3))
                            MMG(mms, [B_ring[sl2], B_hid[k % 2]] + B_hidc[k % 2], [B_ps[2 * pi], B_ps[2 * pi + 1]])
                            TT(x_sb[:, b, :], x_sb[:, b, :], pp, ALU.add, [B_x[b], B_ps[2 * pi], B_ps[2 * pi + 1]], [B_x[b]])
                            if s == 7:
                                emit_norm(b, norm_after)
                                flush_norm(keep=2)
                        items.append(it)
                else:
                    hb = hid_s[s % 2]

                    def it(hb=hb, sl2=sl2, s=s):
                        pi = ps_pr.next()
                        pp = pair(pi)
                        mms = []
                        for half in range(2):
                            for kc in range(4):
                                mms.append((pp[0:16, half * 512:(half + 1) * 512], hb[:, kc, :],
                                            ring[:, sl2, kc * 1024 + half * 512:kc * 1024 + (half + 1) * 512],
                                            kc == 0, kc == 3))
                        MMG(mms, [B_ring[sl2], B_hids[s % 2]], [B_ps[2 * pi], B_ps[2 * pi + 1]])
                        TT(xs_sb[0:16, :], xs_sb[0:16, :], pp[0:16, :], ALU.add,
                           [B_x[16], B_ps[2 * pi], B_ps[2 * pi + 1]], [B_x[16]])
                        if s == 7:
                            emit_norm(16, norm_after)
                            flush_norm()
                    items.append(it)
                return items

            for it in ff1_items(0):
                it()
            for k in range(len(steps)):
                a = ff1_items(k + 1) if k + 1 < len(steps) else []
                bb = ff2_items(k)
                for i in range(max(len(a), len(bb))):
                    if i < len(a):
                        a[i]()
                    if i < len(bb):
                        bb[i]()

        def VPGF(i):
            if i < NPAGE:
                return arena[:, i * 512:(i + 1) * 512]
            j = i - NPAGE
            return mixT[:, j // 2, (j % 2) * 1024:(j % 2 + 1) * 1024].bitcast(F32)

        def VPGB(i):
            if i < NPAGE:
                return [B_pg[i]]
            return [B_mix[(i - NPAGE) // 2]]

        class Small:
            def __init__(self, pages):
                self.pages = pages

            def slot(self, i, np_=16, w=128, n=1):
                assert (i % 4) + n <= 4
                return VPGF(self.pages[i // 4])[0:np_, (i % 4) * 128:(i % 4) * 128 + (n - 1) * 128 + w]

            def bufs(self):
                out = []
                for p in self.pages:
                    out += VPGB(p)
                return out

        class Cx:
            pass

        def make_cx(ci, prompt_pages, sh_base, vm, sbf, msk, small_pages, banks):
            cx = Cx()
            cx.ci = ci
            cx.pp = prompt_pages
            cx.sh_base, cx.vm, cx.sbf, cx.msk = sh_base, vm, sbf, msk
            cx.small = Small(small_pages)
            cx.ps = Rot(banks)
            cx.banks = banks
            cx.gsm = gsm2[ci]
            cx.B_gsm = B_gsm2[ci]
            cx.Sst, cx.Spb, cx.tSb = Sst2[ci], Spb2[ci], tSb2[ci]
            cx.B_S, cx.B_Sp, cx.B_tS = B_S2[ci], B_Sp2[ci], B_tS2[ci]
            cx.nrmc = 16 + 4 * ci
            cx.B_nrm = B_nrm2[ci]
            return cx

        def gen_head_prompt(h, t, sl, cx):
            assert norm_done[0] >= 4 * (t + 1) and (t == 0 or not norm_pending or norm_done[0] > 4 * (t + 1)), (t, norm_done[0])
            T0 = t * 512
            tk = slice(T0, T0 + 512)
            ps = cx.ps
            gs = cx.gsm
            Bg = [cx.B_gsm]

            def W(kc, a, b):
                return ring[:, sl, kc * 512 + a:kc * 512 + b]

            Rw = [B_ring[sl], B_hT[t]]
            pg = cx.pp
            sn, si, lf, G, Ei, osq, rstd = (VPGF(pg[i]) for i in range(7))
            Bsn, Bsi, Blf, BG, BEi, Bosq, Brstd = (VPGB(pg[i]) for i in range(7))
            Ee, rel = si, lf
            p7 = VPGF(pg[7]).bitcast(BF16)
            p8 = VPGF(pg[8]).bitcast(BF16)
            p9 = VPGF(pg[9]).bitcast(BF16)
            B7, B8, B9 = VPGB(pg[7]), VPGB(pg[8]), VPGB(pg[9])
            vtok, qd = p7[:, 0:512], p7[:, 512:1024]
            kd, kt = p8[:, 0:512], p8[:, 512:1024]
            sm, sg = p9[:, 0:512], p9[:, 512:1024]
            if t == 0 or not PREFETCH_F:
                cx.kf_next = ps.next()
                MMG([(bank(cx.kf_next), W(kc, 128, 256), hT[:, kc, tk], kc == 0, kc == 7) for kc in range(8)], Rw,
                    [B_ps[cx.kf_next]])
            kf = cx.kf_next
            yield
            ef, L2 = sn, Ei
            ACT(ef, bank(kf), AF.Exp, [B_ps[kf]], Bsn)
            yield
            kq = ps.next()
            MMG([(bank(kq), W(kc, 0, 128), hT[:, kc, tk], kc == 0, kc == 7) for kc in range(8)], Rw, [B_ps[kq]])
            yield
            ki = ps.next()
            MMG([(bank(ki)[:, j * 128:(j + 1) * 128], hT[:, kc, T0 + j * 128:T0 + (j + 1) * 128], W(kc, 256, 384),
                  kc == 0, kc == 7) for j in range(4) for kc in range(8)], Rw, [B_ps[ki]])
            yield
            kg = ps.next()
            MMG([(bank(kg), W(kc, 384, 512), hT[:, kc, tk], kc == 0, kc == 7) for kc in range(8)], Rw, [B_ps[kg]])
            yield
            ACT(L2, ef, AF.Ln, Bsn, BEi, bias=1.0)
            ACT(lf, ef, AF.Ln, Bsn + [B_hcols], Blf, bias=hcols[:, LB + h:LB + h + 1])
            yield
            if t == 0:
                MEMSET(cx.Sst[:, :], 0.0, [cx.B_S])
                P.op("dve", lambda e: e.tensor_tensor_scan(out=G, data0=lf, data1=L2, initial=0.0,
                                                           op0=ALU.add, op1=ALU.subtract), Blf + BEi, BG)
                MEMSET(gs[:, 16:17], 0.0, Bg)
            else:
                CP(gs[:, 16:17], gs[:, 20:21], Bg, Bg)
                P.op("dve", lambda e: e.tensor_tensor_scan(out=G, data0=lf, data1=L2, initial=gs[:, 16:17],
                                                           op0=ALU.add, op1=ALU.subtract), Blf + BEi + Bg, BG)
            yield
            G3 = G.rearrange("p (j t) -> p j t", t=128)
            TT(rel.rearrange("p (j t) -> p j t", t=128), G3, G3[:, :, 63:64].broadcast_to([128, 4, 128]), ALU.subtract,
               BG, Blf)
            yield
            ACT(Ee, rel, AF.Exp, Blf, Bsi)
            TT(L2, rel, L2, ALU.add, Blf + BEi, BEi)
            yield
            TT(qd, bank(kq), Ee, ALU.mult, [B_ps[kq]] + Bsi, B7)
            ACT(kd, L2, AF.Exp, BEi + [B_hcols], B8, scale=-1.0, bias=hcols[:, LNOML + h:LNOML + h + 1])
            yield
            CP(gs[:, 17:21], G3[:, :, 127], BG, Bg)
            TT(gs[:, 24:28], G3[:, :, 63], gs[:, 16:20], ALU.subtract, BG + Bg, Bg)
            TT(gs[:, 28:32], gs[:, 17:21], G3[:, :, 63], ALU.subtract, BG + Bg, Bg)
            TT(gs[:, 32:36], gs[:, 17:21], gs[:, 16:20], ALU.subtract, Bg, Bg)
            ACT(gs[:, 40:52], gs[:, 24:36], AF.Exp, Bg, Bg)
            eM, eBM, eB = 40, 44, 48
            yield
            ACT(vtok, bank(ki), AF.Silu, [B_ps[ki]], B7)
            ACT(sg, bank(kg), AF.Tanh, [B_ps[kg]], B9, scale=0.5)
            ACT(sg, sg, AF.Identity, B9, B9, scale=0.5, bias=0.5)
            if cx.ci == 1:
                ACT(tmpc[:, 12 + cx.ci:13 + cx.ci], hcols[:, 0:1], AF.Exp, [B_hcols], [B_dummy[cx.ci]])
            yield
            kt_ps = ps.next()
            TRG([(bank_bf(kt_ps)[:, j * 128:(j + 1) * 128], kd[:, j * 128:(j + 1) * 128], identb[:, :]) for j in range(4)],
                B8 + [B_const], [B_ps[kt_ps]])
            ks = ps.next()
            MMG([(bank(ks)[:, j * 128:(j + 1) * 128], kd[:, j * 128:(j + 1) * 128], qd[:, j * 128:(j + 1) * 128], True, True)
                 for j in range(4)], B7 + B8, [B_ps[ks]])
            yield
            ACT(kt, bank_bf(kt_ps)[:, 0:512], AF.Copy, [B_ps[kt_ps]], B8)
            TT(sm.rearrange("p (j t) -> p j t", t=128), bank(ks).rearrange("p (j t) -> p j t", t=128),
               maskT[:, :].unsqueeze(1).broadcast_to([128, 4, 128]), ALU.mult, [B_ps[ks], B_const], B9)
            yield
            kdS = ps.next()
            MMG([(bank(kdS)[:, j * 128:(j + 1) * 128], kt[:, j * 128:(j + 1) * 128], vtok[:, j * 128:(j + 1) * 128], True, True)
                 for j in range(4)], B7 + B8, [B_ps[kdS]])
            yield
            if PREFETCH_F and t < 3:
                cx.kf_next = ps.next()
                tk2 = slice(T0 + 512, T0 + 1024)
                MMG([(bank(cx.kf_next), W(kc, 128, 256), hT[:, kc, tk2], kc == 0, kc == 7) for kc in range(8)],
                    [B_ring[sl], B_hT[t + 1]], [B_ps[cx.kf_next]])
            tSa = lf.rearrange("p (j v) -> p j v", v=128)
            TT(tSa, bank(kdS).rearrange("p (j v) -> p j v", v=128),
               gs[:, eBM:eBM + 4].unsqueeze(2).broadcast_to([128, 4, 128]), ALU.mult, [B_ps[kdS]] + Bg, Blf)
            yield
            ko = ps.next()
            for j in range(4):
                sp_i = j % 2
                ACT(cx.Spb[:, sp_i, :], cx.Sst[:, :], AF.Copy, [cx.B_S] + Bg, [cx.B_Sp[sp_i]], scale=gs[:, eM + j:eM + j + 1])
                MMG([(bank(ko)[:, j * 128:(j + 1) * 128], cx.Spb[:, sp_i, :], qd[:, j * 128:(j + 1) * 128], True, False),
                     (bank(ko)[:, j * 128:(j + 1) * 128], vtok[:, j * 128:(j + 1) * 128], sm[:, j * 128:(j + 1) * 128], False, True)],
                    [cx.B_Sp[sp_i]] + B7 + B9, [B_ps[ko]])
                STT(cx.Sst[:, :], cx.Sst[:, :], gs[:, eB + j:eB + j + 1], tSa[:, j, :], ALU.mult, ALU.add,
                    [cx.B_S] + Blf + Bg, [cx.B_S])
                yield
            osqb = osq.bitcast(BF16)[:, 0:512]
            ACT(osqb, bank(ko), AF.Square, [B_ps[ko]], Bosq)
            yield
            km = ps.next()
            MMG([(bank(km), onesb[:, :], osqb, True, True)], Bosq + [B_const], [B_ps[km]])
            yield
            ACT(rstd, bank(km), AF.Ln, [B_ps[km]], Brstd, bias=EPS)
            ACT(rstd, rstd, AF.Exp, Brstd, Brstd, scale=-0.5)
            yield
            TT(osq, bank(ko), rstd, ALU.mult, [B_ps[ko]] + Brstd, Bosq)
            STT(mixT[:, 2 + h, tk], osq, hcols[:, GAIN + h:GAIN + h + 1], sg, ALU.mult, ALU.mult,
                Bosq + B9 + [B_hcols], [B_mix[2 + h]])
            if t == 3:
                DMA("sp", misc(), [(nhp[h], cx.Sst[:, :])], [cx.B_S], ())
            yield

        def gen_head_sample(h, sl, cx):
            ps = cx.ps
            Rw = [B_ring[sl], B_hT[4]]
            Sh = arena[:, cx.sh_base * 512:(cx.sh_base + 4) * 512].rearrange("p (b v) -> p b v", v=128)
            BSh = pgb(cx.sh_base, 4)
            DMA("sp", misc(), [(Sh[:, 4 * g:4 * g + 4, :], st_hgrn[4 * g:4 * g + 4, h].rearrange("b k v -> k b v"))
                               for g in range(4)], (), BSh)
            kp = ps.next()
            MMG([(bank(kp)[0:16, :], hT[:, kc, NT:NTOK], ring[:, sl, kc * 512:(kc + 1) * 512], kc == 0, kc == 7)
                 for kc in range(8)], Rw, [B_ps[kp]])
            yield
            S_ = cx.small
            Bs = S_.bufs()
            qfig = S_.slot(0, n=4)
            si_s, vv, kk = (S_.slot(i) for i in (4, 5, 6))
            wide = S_.slot(9, np_=128, w=128)
            rs_s, t1_s = wide[:, 0:16], wide[:, 16:32]
            osq_s = wide[:, 32:48].bitcast(BF16)[:, 0:16]
            qT = S_.slot(10, np_=128, w=16).bitcast(BF16)[:, 0:16]
            fm = S_.slot(11, np_=128, w=64)
            snT, kkT, fT, sgT = fm[:, 0:16], fm[:, 16:32], fm[:, 32:48], fm[:, 48:64]
            ACT(si_s, bank(kp)[0:16, 256:384], AF.Sigmoid, [B_ps[kp]], Bs)
            ACT(qfig, bank(kp)[0:16, 0:512], AF.Copy, [B_ps[kp]], Bs)
            yield
            TT(vv, bank(kp)[0:16, 256:384], si_s, ALU.mult, [B_ps[kp]] + Bs, Bs)
            k2 = ps.next()
            TRG([(bank(k2)[:, 0:16], qfig[:, 0:128], identf[0:16, 0:16]),
                 (bank(k2)[:, 16:32], qfig[:, 128:256], identf[0:16, 0:16]),
                 (bank(k2)[:, 32:48], qfig[:, 384:512], identf[0:16, 0:16])], Bs + [B_const], [B_ps[k2]])
            yield
            ACT(snT, bank(k2)[:, 16:32], AF.Sigmoid, [B_ps[k2]], Bs, scale=-1.0)
            ACT(sgT, bank(k2)[:, 32:48], AF.Sigmoid, [B_ps[k2]], Bs)
            ACT(qT, bank(k2)[:, 0:16], AF.Copy, [B_ps[k2]], Bs)
            yield
            TS(kkT, snT, hcols[:, OML + h:OML + h + 1], ALU.mult, Bs + [B_hcols], Bs)
            TS(fT, kkT, -1.0, ALU.mult, Bs, Bs, s2=1.0, op1=ALU.add)
            k3 = ps.next()
            TRG([(bank(k3)[0:16, 0:128], kkT, identf[:, :])], Bs + [B_const], [B_ps[k3]])
            yield
            kkb = kk.bitcast(BF16)[:, 0:128]
            ACT(kkb, bank(k3)[0:16, 0:128], AF.Copy, [B_ps[k3]], Bs)
            kk = kkb
            yield
            Vmf = VPGF(cx.vm)[0:16, :].bitcast(BF16)
            BVm = VPGB(cx.vm)
            for g in range(4):
                Vm = Vmf[:, (g % 2) * 512:(g % 2 + 1) * 512].rearrange("p (b v) -> p b v", v=128)
                TT(Vm, vv.unsqueeze(1).broadcast_to([16, 4, 128]),
                   id16[0:16, 4 * g:4 * g + 4].unsqueeze(2).broadcast_to([16, 4, 128]), ALU.mult, Bs + [B_const], BVm)
                kk_ps = ps.next()
                MMG([(bank(kk_ps), kk, Vm, True, True)], Bs + BVm, [B_ps[kk_ps]])
                TT(Sh[:, 4 * g:4 * g + 4, :], Sh[:, 4 * g:4 * g + 4, :],
                   fT[:, 4 * g:4 * g + 4].unsqueeze(2).broadcast_to([128, 4, 128]), ALU.mult, BSh + Bs, BSh)
                yield
                TT(Sh[:, 4 * g:4 * g + 4, :], Sh[:, 4 * g:4 * g + 4, :], bank(kk_ps).rearrange("p (b v) -> p b v", v=128),
                   ALU.add, BSh + [B_ps[kk_ps]], BSh)
                yield
            DMA("sp", misc(), [(nhs[4 * g:4 * g + 4, h].rearrange("b k v -> k b v"), Sh[:, 4 * g:4 * g + 4, :])
                               for g in range(4)], BSh, ())
            Sbf = VPGF(cx.sbf).bitcast(BF16)
            BSbf = VPGB(cx.sbf)
            ko_ = ps.next()
            oT = bank(ko_)[:, 0:16]
            for g in range(4):
                sb_g = Sbf[:, (g % 2) * 512:(g % 2 + 1) * 512].rearrange("p (b v) -> p b v", v=128)
                ACT(sb_g, Sh[:, 4 * g:4 * g + 4, :], AF.Copy, BSh, BSbf)
                yield
                MMG([(bank(ko_)[:, 4 * g + bb:4 * g + bb + 1], sb_g[:, bb, :], qT[:, 4 * g + bb:4 * g + bb + 1], True, True)
                     for bb in range(4)], Bs + BSbf, [B_ps[ko_]])
                yield
            ACT(osq_s, oT, AF.Square, [B_ps[ko_]], Bs)
            yield
            km_ = ps.next()
            MMG([(bank(km_)[:, 0:16], onesb[:, :], osq_s, True, True)], Bs + [B_const], [B_ps[km_]])
            yield
            ACT(rs_s, bank(km_)[:, 0:16], AF.Ln, [B_ps[km_]], Bs, bias=EPS)
            ACT(rs_s, rs_s, AF.Exp, Bs, Bs, scale=-0.5)
            yield
            TT(t1_s, oT, rs_s, ALU.mult, [B_ps[ko_]] + Bs, Bs)
            STT(mixT[:, 2 + h, NT:NTOK], t1_s, hcols[:, GAIN + h:GAIN + h + 1], sgT, ALU.mult, ALU.mult,
                Bs + [B_hcols], [B_mix[2 + h]])
            yield

        def gen_head(h, cx):
            sl = (U_HEAD + h) % NSLOT
            for t in range(4):
                yield from gen_head_prompt(h, t, sl, cx)
            yield from gen_head_sample(h, sl, cx)

        def interleave(gens, offset=0):
            gens = list(gens)
            for _ in range(offset):
                try:
                    next(gens[0])
                except StopIteration:
                    break
            while gens:
                for g in list(gens):
                    try:
                        next(g)
                    except StopIteration:
                        gens.remove(g)

        ps_all = Rot(list(range(8)))

        def gen_pool_chunk(c, sl):
            pm = ({"ue": 0, "A": 4, "B": 6, "C": 6, "d": 10, "fx": 16}, {"ue": 2, "A": 8, "B": 11, "C": 13, "d": 15, "fx": 17})[c]
            ue = pgf(pm["ue"], 2)[:, 0:528]
            Bu = pgb(pm["ue"], 2)
            A_ = pgf(pm["A"], 2)[:, 0:528]
            B_ = pgf(pm["B"], 2)[:, 0:528]
            C_ = pgf(pm["C"], 2)[:, 0:528]
            BA, BB, BC = pgb(pm["A"], 2), pgb(pm["B"], 2), pgb(pm["C"], 2)
            dbf = pgf(pm["d"]).bitcast(BF16)[:, 0:512]
            Bd = pgb(pm["d"])
            fx = pgf(pm["fx"])[:, 0:16]
            Bfx = pgb(pm["fx"])
            def proj(tt):
                k_ = ps_all.next()
                tk_ = slice(tt * 512, (tt + 1) * 512)
                MMG([(bank(k_), ring[:, sl, kc * 256 + c * 128:kc * 256 + (c + 1) * 128], hT[:, kc, tk_], kc == 0, kc == 7)
                     for kc in range(8)], [B_ring[sl], B_hT[tt]], [B_ps[k_]])
                return k_

            ku_next = proj(0)
            yield
            for t in range(4):
                T0 = t * 512
                tk = slice(T0, T0 + 512)
                ku = ku_next
                if t == 0:
                    MEMSET(ue[:, 0:16], 0.0, Bu)
                ACT(ue[:, 16:528], bank(ku), AF.Copy, [B_ps[ku]], Bu)
                if t < 3:
                    ku_next = proj(t + 1)
                yield
                TT(A_[:, 2:528], ue[:, 2:528], ue[:, 1:527], ALU.add, Bu, BA)
                yield
                if c == 0:
                    TT(B_[64:128, 4:528], A_[64:128, 4:528], A_[64:128, 2:526], ALU.add, BA, BB)
                    STT(dbf[0:64, :], A_[0:64, 16:528], 0.5, ue[0:64, 16:528], ALU.mult, ALU.subtract, BA + Bu, Bd)
                    yield
                    STT(dbf[64:128, :], B_[64:128, 16:528], 0.25, ue[64:128, 16:528], ALU.mult, ALU.subtract, BB + Bu, Bd)
                    sel_lo, sel_hi = A_, B_
                    yield
                    yield
                else:
                    TT(B_[:, 4:528], A_[:, 4:528], A_[:, 2:526], ALU.add, BA, BB)
                    yield
                    TT(C_[:, 8:528], B_[:, 8:528], B_[:, 4:524], ALU.add, BB, BC)
                    yield
                    TT(A_[64:128, 16:528], C_[64:128, 16:528], C_[64:128, 8:520], ALU.add, BC + BA, BA)
                    STT(dbf[0:64, :], C_[0:64, 16:528], 0.125, ue[0:64, 16:528], ALU.mult, ALU.subtract, BC + Bu, Bd)
                    yield
                    STT(dbf[64:128, :], A_[64:128, 16:528], 0.0625, ue[64:128, 16:528], ALU.mult, ALU.subtract, BA + Bu, Bd)
                    sel_lo, sel_hi = C_, A_
                if t == 0:
                    for (lo, hi, sel) in ((0, 64, sel_lo), (64, 128, sel_hi)):
                        TT(fx[lo:hi, :], sel[lo:hi, 16:32], cfixw[lo:hi, c * 16:(c + 1) * 16], ALU.mult,
                           [B_const] + BA + BB + BC, Bfx)
                        TT(dbf[lo:hi, 0:16], fx[lo:hi, :], ue[lo:hi, 16:32], ALU.subtract, Bfx + Bu, Bd)
                yield
                kp_ = ps_all.next()
                MMG([(bank(kp_), wblk[:, c, :], dbf, True, True)], Bd + [B_wblk], [B_ps[kp_]])
                yield
                ACT(mixT[:, c, tk], bank(kp_), AF.Copy, [B_ps[kp_], B_const], [B_mix[c]], scale=pscale[:, c:c + 1])
                if t < 3:
                    CP(ue[:, 0:16], ue[:, 512:528], Bu, Bu)
                else:
                    kx = ps_all.next()
                    TRG([(bank(kx)[0:15, 0:128], ue[:, 513:528], identf[:, :])], Bu + [B_const], [B_ps[kx]])
                    stg = pgf(pm["fx"])[0:15, 0:128]
                    ACT(stg, bank(kx)[0:15, 0:128], AF.Copy, [B_ps[kx]], Bfx)
                    DMA("sp", misc(), [(npp[:, c * 128:(c + 1) * 128], stg)], Bfx, ())
                yield

        def emit_pool_sample(sl):
            Rw = [B_ring[sl], B_hT[4]]
            stp = pgf(0, 8)[0:16, 0:3840]
            Bst = pgb(0, 8)
            DMA("sp", misc(), [(stp, st_pool)], (), Bst)
            stp3 = stp.rearrange("p (r c) -> p r c", c=256)
            ku = ps_all.next()
            MMG([(bank(ku)[0:16, 0:256], hT[:, kc, NT:NTOK], ring[:, sl, kc * 256:(kc + 1) * 256], kc == 0, kc == 7)
                 for kc in range(8)], Rw, [B_ps[ku]])
            sml = pgf(11, 3)
            Bs = pgb(11, 3)
            u_s = sml[0:16, 0:256]
            sums = sml[0:16, 256:512]
            dd = sml[0:16, 512:768].bitcast(BF16)[:, 0:256]
            ACT(u_s, bank(ku)[0:16, 0:256], AF.Copy, [B_ps[ku]], Bs)
            for g, w in enumerate((2, 4, 8, 16)):
                cs = slice(g * 64, (g + 1) * 64)
                if w == 2:
                    TT(sums[:, cs], stp3[:, 14, cs], u_s[:, cs], ALU.add, Bst + Bs, Bs)
                else:
                    P.op("dve", lambda e, cs=cs, w=w: e.tensor_reduce(
                        out=sums[:, cs], in_=stp3[:, 16 - w:15, cs].rearrange("p r c -> p c r"), axis=AX.X, op=ALU.add),
                        Bst, Bs)
                    TT(sums[:, cs], sums[:, cs], u_s[:, cs], ALU.add, Bs, Bs)
                STT(dd[:, cs], sums[:, cs], 1.0 / w, u_s[:, cs], ALU.mult, ALU.subtract, Bs, Bs)
            k2 = ps_all.next()
            TRG([(bank_bf(k2)[:, c * 16:(c + 1) * 16], dd[:, c * 128:(c + 1) * 128], identb[0:16, 0:16]) for c in range(2)],
                Bs + [B_const], [B_ps[k2]])
            dT = pgf(10).bitcast(BF16)[:, 0:32]
            ACT(dT, bank_bf(k2)[:, 0:32], AF.Copy, [B_ps[k2]], pgb(10))
            k3 = ps_all.next()
            MMG([(bank(k3)[:, c * 16:(c + 1) * 16], wblk[:, c, :], dT[:, c * 16:(c + 1) * 16], True, True) for c in range(2)],
                pgb(10) + [B_wblk], [B_ps[k3]])
            for c in range(2):
                ACT(mixT[:, c, NT:NTOK], bank(k3)[:, c * 16:(c + 1) * 16], AF.Copy, [B_ps[k3], B_const], [B_mix[c]],
                    scale=pscale[:, c:c + 1])
            DMA("sp", misc(), [(nps[:, 0:14 * 256], stp[:, 256:3840]), (nps[:, 14 * 256:15 * 256], u_s)], Bst + Bs, ())

        def gen_conv_prompt(c, t, sl, cx):
            ps = cx.ps
            T0 = t * 512
            tk = slice(T0, T0 + 512)
            Rw = [B_ring[sl], B_hT[t]]
            kc_, kh, kb = cx.banks[0], cx.banks[1], cx.banks[2]
            for (pk, off) in ((kc_, 128), (kh, 256), (kb, 0)):
                MMG([(bank(pk), ring[:, sl, k8 * 384 + off:k8 * 384 + off + 128], hT[:, k8, tk], k8 == 0, k8 == 7)
                     for k8 in range(8)], Rw, [B_ps[pk]])
            yield
            pg = cx.pp
            cgs = VPGF(pg[0])
            ze = arena[:, pg[1] * 512:(pg[1] + 2) * 512][:, 0:514]
            c1 = VPGF(pg[3])
            c2 = VPGF(pg[4])
            Bc, Bz, B1, B2 = VPGB(pg[0]), pgb(pg[1], 2), VPGB(pg[3]), VPGB(pg[4])
            if t == 0:
                MEMSET(ze[:, 0:2], 0.0, Bz)
            ACT(cgs, bank(kc_), AF.Copy, [B_ps[kc_]], Bc)
            yield
            TT(ze[:, 2:514], bank(kh), cgs, ALU.mult, [B_ps[kh]] + Bc, Bz)
            yield
            w0 = convwf[:, c * 3 + 0:c * 3 + 1]
            w1 = convwf[:, c * 3 + 1:c * 3 + 2]
            w2 = convwf[:, c * 3 + 2:c * 3 + 3]
            ACT(c1, ze[:, 0:512], AF.Copy, Bz + [B_const], B1, scale=w0)
            yield
            STT(c2, ze[:, 1:513], w1, c1, ALU.mult, ALU.add, Bz + B1 + [B_const], B2)
            yield
            STT(c1, ze[:, 2:514], w2, c2, ALU.mult, ALU.add, Bz + B2 + [B_const], B1)
            yield
            TT(mixT[:, c, tk], bank(kb), c1, ALU.mult, [B_ps[kb]] + B1, [B_mix[c]])
            if t < 3:
                CP(ze[:, 0:2], ze[:, 512:514], Bz, Bz)
            else:
                kx = cx.banks[0]
                TRG([(bank(kx)[0:2, 0:128], ze[:, 512:514], identf[:, :])], Bz + [B_const], [B_ps[kx]])
                stg = VPGF(pg[0])[0:2, 0:128]
                ACT(stg, bank(kx)[0:2, 0:128], AF.Copy, [B_ps[kx]], Bc)
                DMA("sp", misc(), [(ncp[:, c * 128:(c + 1) * 128], stg)], Bc, ())
            yield

        def conv_sample_part1(c, cx):
            sl = (U_CONV + c) % NSLOT
            Rw = [B_ring[sl], B_hT[4]]
            S_ = cx.small
            Bs = S_.bufs()
            prev = S_.slot(0, n=2).rearrange("p (r c) -> p r c", c=128)
            wbc = S_.slot(4, n=3).rearrange("p (r c) -> p r c", c=128)
            DMA("sp", misc(), [(prev, st_conv[:, :, c * 128:(c + 1) * 128]),
                               (wbc, convw_d[:, c * 128:(c + 1) * 128].partition_broadcast(16))], (), Bs)
            kp = cx.banks[3]
            MMG([(bank(kp)[0:16, 0:384], hT[:, k8, NT:NTOK], ring[:, sl, k8 * 384:(k8 + 1) * 384], k8 == 0, k8 == 7)
                 for k8 in range(8)], Rw, [B_ps[kp]])

        def gen_conv_sample_part2(c, cx):
            S_ = cx.small
            Bs = S_.bufs()
            prev = S_.slot(0, n=2).rearrange("p (r c) -> p r c", c=128)
            wbc = S_.slot(4, n=3).rearrange("p (r c) -> p r c", c=128)
            kp = cx.banks[3]
            cg_s, z_s, a_, b_ = S_.slot(2), S_.slot(3), S_.slot(7), S_.slot(8)
            ACT(cg_s, bank(kp)[0:16, 128:256], AF.Copy, [B_ps[kp]], Bs)
            yield
            TT(z_s, bank(kp)[0:16, 256:384], cg_s, ALU.mult, [B_ps[kp]] + Bs, Bs)
            TT(a_, prev[:, 0, :], wbc[:, 0, :], ALU.mult, Bs, Bs)
            yield
            TT(b_, prev[:, 1, :], wbc[:, 1, :], ALU.mult, Bs, Bs)
            TT(a_, a_, b_, ALU.add, Bs, Bs)
            yield
            TT(b_, z_s, wbc[:, 2, :], ALU.mult, Bs, Bs)
            TT(a_, a_, b_, ALU.add, Bs, Bs)
            yield
            yb = S_.slot(9).bitcast(BF16)[:, 0:128]
            TT(yb, bank(kp)[0:16, 0:128], a_, ALU.mult, [B_ps[kp]] + Bs, Bs)
            yield
            k2 = kp
            TRG([(bank_bf(k2)[:, 0:16], yb, identb[0:16, 0:16])], Bs + [B_const], [B_ps[k2]])
            yield
            ACT(mixT[:, c, NT:NTOK], bank_bf(k2)[:, 0:16], AF.Copy, [B_ps[k2]], [B_mix[c]])
            DMA("sp", misc(), [(ncs[:, 0, c * 128:(c + 1) * 128], prev[:, 1, :]),
                               (ncs[:, 1, c * 128:(c + 1) * 128], z_s)], Bs, ())
            yield

        def gen_conv(c, cx):
            sl = (U_CONV + c) % NSLOT
            for t in range(4):
                yield from gen_conv_prompt(c, t, sl, cx)

        for ci in range(2):
            MEMSET(gsm2[ci][:, 63:64], 1.0, [B_gsm2[ci]])
        for b in list(range(16)) + [16]:
            emit_norm(b, 0)
            flush_norm(keep=1)
        flush_norm()
        norm_done = [16]

        def gen_init_norms():
            norm_junk[0] = [NPAGE + 2, NPAGE + 3]
            for b in range(4, 16):
                emit_norm(b, 0)
                flush_norm(keep=1)
                norm_done[0] = b + 1
                yield
            flush_norm()
            norm_junk[0] = [15, 16, 17]
            yield

        hcx = [make_cx(0, list(range(0, 10)), 0, 4, 5, 6, [7, 8, 9], [0, 1, 2, 3]),
               make_cx(1, list(range(10, 20)), 10, 14, 15, 16, [17, 18, 19], [4, 5, 6, 7])]
        for h0 in (0, 2, 4):
            need(U_HEAD + h0)
            gens = [gen_head(h0, hcx[0]), gen_head(h0 + 1, hcx[1])]
            interleave(gens, HEAD_OFFSET)
        sl = need(U_POOL)
        interleave([gen_pool_chunk(0, sl), gen_pool_chunk(1, sl)])
        emit_pool_sample(sl)
        emit_outproj(U_WO, 1)
        emit_mlp(U_FF0, 2)
        ccx = [make_cx(0, [0, 1, 2, 3, 4], 0, 0, 0, 0, [12, 13, 14], [0, 1, 2, 3]),
               make_cx(1, [6, 7, 8, 9, 10], 0, 0, 0, 0, [15, 16, 17], [4, 5, 6, 7])]
        pend = []
        for c0 in (0, 2, 4, 6):
            need(U_CONV + c0)
            interleave([gen_conv(c0, ccx[0]), gen_conv(c0 + 1, ccx[1])] + pend)
            conv_sample_part1(c0, ccx[0])
            conv_sample_part1(c0 + 1, ccx[1])
            pend = [gen_conv_sample_part2(c0, ccx[0]), gen_conv_sample_part2(c0 + 1, ccx[1])]
        interleave(pend)
        emit_outproj(U_OO, 3)
        DMA("sp", misc(), [(pgf(10, 2), gfin_d.partition_broadcast(128))], (), pgb(10, 2))
        emit_mlp(U_FF1, 4)

        final_waits = [(d.h, d.count) for d in P.dsems if d.count > 0]

        def make_body(name, final=False):
            def body(eng):
                for waits, fn, inc in P.streams[name]:
                    embed = None
                    if EMBED_WAITS and inc is True and waits:
                        embed = waits[-1]
                        waits = waits[:-1]
                    for sem, val in waits:
                        eng.wait_ge(sem, val)
                    ins = fn(eng)
                    if embed is not None:
                        ins._wait_ge(embed[0], embed[1])
                    if inc:
                        ins.then_inc(P.esem[name], 1)
                if final:
                    for h_, v_ in final_waits:
                        eng.wait_ge(h_, v_)
            return body

        block.sync(make_body("sp", final=True))
        block.scalar(make_body("act"))
        block.vector(make_body("dve"))
        block.gpsimd(make_body("pool"))
        block.tensor(make_body("pe"))
    return nc


def _unit_k1024(W, cols):
    sub = W[:, cols]
    n = sub.shape[1]
    u = sub.reshape(8, 128, n).transpose(1, 0, 2).reshape(128, 8 * n)
    out = np.zeros((128, 4096), np.float32)
    out[:, :8 * n] = u
    return out


def _unit_w2(W2, s):
    sub = W2[s * 512:(s + 1) * 512, :]
    return np.ascontiguousarray(sub.reshape(4, 128, 1024).transpose(1, 0, 2).reshape(128, 4096))


def _build_wstream(even_w_in, even_w_out, odd_w_in, odd_w_out, ff_w1, ff_w2):
    wst = np.zeros((NU, 128, 4096), np.float32)
    win = even_w_in[0]
    for h in range(6):
        cols = np.concatenate([256 + r * 768 + h * 128 + np.arange(128) for r in range(4)])
        wst[U_HEAD + h] = _unit_k1024(win, cols)
    wst[U_POOL] = _unit_k1024(win, np.arange(256))
    for half in range(2):
        wst[U_WO + half] = _unit_k1024(even_w_out[0], half * 512 + np.arange(512))
        wst[U_OO + half] = _unit_k1024(odd_w_out[0], half * 512 + np.arange(512))
    for l, u0 in ((0, U_FF0), (1, U_FF1)):
        for s in range(8):
            wst[u0 + 2 * s] = _unit_k1024(ff_w1[l], s * 512 + np.arange(512))
            wst[u0 + 2 * s + 1] = _unit_w2(ff_w2[l], s)
    for c in range(8):
        cols = np.concatenate([r * 1024 + c * 128 + np.arange(128) for r in range(3)])
        wst[U_CONV + c] = _unit_k1024(odd_w_in[0], cols)
    return wst


_NC_CACHE = {}


def kernel(x_prompt, x_sample, state_pool, state_hgrn, state_conv, norm_mix, norm_mlp, norm_final,
           even_w_in, pool_w, pool_scale, hgrn_lb_logits, hgrn_gain, even_w_out, odd_w_in, conv_w,
           odd_w_out, ff_w1, ff_w2):
    f = lambda a: np.ascontiguousarray(np.asarray(a, dtype=np.float32))
    x_prompt, x_sample, state_pool, state_hgrn, state_conv = map(f, (x_prompt, x_sample, state_pool, state_hgrn, state_conv))
    norm_mix, norm_mlp, norm_final = f(norm_mix), f(norm_mlp), f(norm_final)
    even_w_in, pool_w, pool_scale, hgrn_lb_logits, hgrn_gain = map(f, (even_w_in, pool_w, pool_scale, hgrn_lb_logits, hgrn_gain))
    even_w_out, odd_w_in, conv_w, odd_w_out, ff_w1, ff_w2 = map(f, (even_w_out, odd_w_in, conv_w, odd_w_out, ff_w1, ff_w2))

    if "nc" not in _NC_CACHE:
        _NC_CACHE["nc"] = build()
    nc = _NC_CACHE["nc"]

    wst = _build_wstream(even_w_in, even_w_out, odd_w_in, odd_w_out, ff_w1, ff_w2)
    norms = np.stack([norm_mix[0], norm_mlp[0], norm_mix[1], norm_mlp[1]])
    gcols = np.ascontiguousarray(norms.reshape(4, 8, 128).transpose(2, 0, 1).reshape(128, 32))
    lbl = np.ascontiguousarray(hgrn_lb_logits.reshape(3, 6, 128).transpose(2, 0, 1).reshape(128, 18))
    gainf = np.ascontiguousarray(hgrn_gain[0].reshape(6, 128).T)
    pscalef = np.ascontiguousarray(pool_scale[0].reshape(2, 128).T)
    convwf = np.ascontiguousarray(conv_w[0].reshape(3, 8, 128).transpose(2, 1, 0).reshape(128, 24))
    identf = np.eye(128, dtype=np.float32)
    identb = identf.astype(ml_dtypes.bfloat16)
    maskT = np.triu(np.ones((128, 128), np.float32))
    onesm = np.full((128, 128), 1.0 / 128, np.float32)
    id16 = np.eye(16, dtype=np.float32)
    cfixw = np.zeros((128, 2, 16), np.float32)
    for c in range(2):
        for p in range(128):
            w = 2 ** (2 * c + p // 64 + 1)
            for t in range(16):
                cfixw[p, c, t] = 1.0 / min(w, t + 1)
    cfixw = cfixw.reshape(128, 32)

    shared = dict(wst=wst, gcols=gcols, gfin=norm_final.reshape(1, 1024), lbl=lbl, gainf=gainf, pscalef=pscalef,
                  poolw=pool_w[0], convwf=convwf, convw=conv_w[0], c_identb=identb, c_identf=identf, c_maskT=maskT,
                  c_onesm=onesm, c_id16=id16, c_cfixw=cfixw)
    in_maps = []
    for c in range(8):
        m = dict(shared)
        m["xp"] = x_prompt[c].reshape(16, 128, 1024)
        m["xs"] = x_sample[16 * c:16 * (c + 1), 0, :]
        m["st_pool"] = state_pool[0, 16 * c:16 * (c + 1)].reshape(16, 15 * 256)
        m["st_hgrn"] = state_hgrn[0, 16 * c:16 * (c + 1)]
        m["st_conv"] = state_conv[0, 16 * c:16 * (c + 1)]
        in_maps.append({k: np.ascontiguousarray(v) for k, v in m.items()})

    res = run_bass_kernel_spmd(nc, in_maps, core_ids=list(range(8)))
    R = res.results
    y_prompt = np.stack([R[c]["yp"].reshape(2048, 1024) for c in range(8)]).astype(np.float32)
    y_sample = np.concatenate([R[c]["ys"] for c in range(8)]).reshape(128, 1, 1024).astype(np.float32)
    new_pool_prompt = np.stack([R[c]["npp"] for c in range(8)])[None].astype(np.float32)
    new_hgrn_prompt = np.stack([R[c]["nhp"] for c in range(8)])[None].astype(np.float32)
    new_conv_prompt = np.stack([R[c]["ncp"] for c in range(8)])[None].astype(np.float32)
    new_pool_sample = np.concatenate([R[c]["nps"].reshape(16, 15, 256) for c in range(8)])[None].astype(np.float32)
    new_hgrn_sample = np.concatenate([R[c]["nhs"] for c in range(8)])[None].astype(np.float32)
    new_conv_sample = np.concatenate([R[c]["ncs"] for c in range(8)])[None].astype(np.float32)
    return (y_prompt, y_sample, new_pool_prompt, new_hgrn_prompt, new_conv_prompt,
            new_pool_sample, new_hgrn_sample, new_conv_sample)
```

```python
import numpy as np
import ml_dtypes
from contextlib import ExitStack
import concourse.bass as bass
import concourse.mybir as mybir
from concourse.bass_utils import run_bass_kernel_spmd

F32 = mybir.dt.float32
BF16 = mybir.dt.bfloat16
AF = mybir.ActivationFunctionType
ALU = mybir.AluOpType
AX = mybir.AxisListType

NT = 2048
NSMP = 16
NTOK = NT + NSMP
EPS = 1e-6
NSLOT = 4
NPAGE = 18
SAME_ENGINE_SYNC = True
EMBED_WAITS = True
HEAD_OFFSET = 0
PREFETCH_F = False

U_HEAD = 0
U_POOL = 6
U_WO = 7
U_FF0 = 9
U_CONV = 25
U_OO = 33
U_FF1 = 35
NU = 51


class Buf:
    __slots__ = ("name", "w", "r")

    def __init__(self, name):
        self.name = name
        self.w = None
        self.r = {}


class DSem:
    def __init__(self, h):
        self.h = h
        self.count = 0


class Prog:
    ENG = ("sp", "act", "dve", "pool", "pe")

    def __init__(self):
        self.streams = {e: [] for e in self.ENG}
        self.seq = {e: 0 for e in self.ENG}
        self.waited = {e: {} for e in self.ENG}
        self.esem = {}
        self.dsems = []

    def _waits(self, eng, reads, writes, extra=()):
        need = {}

        def add(tok):
            if tok is None:
                return
            if tok[0] == "e":
                if tok[1] == eng and (eng == "pe" or not SAME_ENGINE_SYNC):
                    return
                k = ("e", tok[1])
            else:
                k = ("d", id(tok[1]))
            if self.waited[eng].get(k, 0) >= tok[2]:
                return
            if k not in need or need[k][2] < tok[2]:
                need[k] = tok

        for b in reads:
            add(b.w)
        for b in writes:
            add(b.w)
            for t in b.r.values():
                add(t)
        for t in extra:
            add(t)
        out = []
        for k, tok in need.items():
            self.waited[eng][k] = tok[2]
            sem = self.esem[tok[1]] if tok[0] == "e" else tok[1].h
            out.append((sem, tok[2]))
        return out

    @staticmethod
    def _mark(tok, reads, writes):
        k = ("e", tok[1]) if tok[0] == "e" else ("d", id(tok[1]))
        for b in reads:
            b.r[k] = tok
        for b in writes:
            b.w = tok
            b.r = {}

    def op(self, eng, fn, reads=(), writes=(), multi=False):
        waits = self._waits(eng, reads, writes)
        self.seq[eng] += 1
        tok = ("e", eng, self.seq[eng])
        self.streams[eng].append((waits, fn, "multi" if multi else True))
        self._mark(tok, reads, writes)
        return tok

    def dma(self, q, dsem, fns, reads=(), writes=()):
        extra = []
        if dsem.count > 0:
            extra.append(("d", dsem, dsem.count))
        waits = self._waits(q, reads, writes, extra)
        dsem.count += 16 * len(fns)
        tok = ("d", dsem, dsem.count)

        def run(eng, fns=fns, h=dsem.h):
            for f in fns:
                f(eng).then_inc(h, 16)
            return None

        self.streams[q].append((waits, run, False))
        self._mark(tok, reads, writes)
        return tok


def build():
    nc = bass.Bass("TRN2", target_bir_lowering=False)

    def din(name, shape, dt=F32):
        return nc.dram_tensor(name, shape, dt, kind="ExternalInput").ap()

    def dout(name, shape, dt=F32):
        return nc.dram_tensor(name, shape, dt, kind="ExternalOutput").ap()

    xp = din("xp", [16, 128, 1024])
    xs = din("xs", [16, 1024])
    st_pool = din("st_pool", [16, 15 * 256])
    st_hgrn = din("st_hgrn", [16, 6, 128, 128])
    st_conv = din("st_conv", [16, 2, 1024])
    wst = din("wst", [NU, 128, 4096])
    gcols_d = din("gcols", [128, 32])
    gfin_d = din("gfin", [1, 1024])
    lbl_d = din("lbl", [128, 18])
    gain_d = din("gainf", [128, 6])
    pscale_d = din("pscalef", [128, 2])
    poolw_d = din("poolw", [4, 64, 64])
    convwf_d = din("convwf", [128, 24])
    convw_d = din("convw", [3, 1024])
    c_identb = din("c_identb", [128, 128], BF16)
    c_identf = din("c_identf", [128, 128])
    c_maskT = din("c_maskT", [128, 128])
    c_onesm = din("c_onesm", [128, 128])
    c_id16 = din("c_id16", [16, 16])
    c_cfixw = din("c_cfixw", [128, 32])

    yp = dout("yp", [16, 128, 1024])
    ys = dout("ys", [16, 1024])
    npp = dout("npp", [15, 256])
    nhp = dout("nhp", [6, 128, 128])
    ncp = dout("ncp", [2, 1024])
    nps = dout("nps", [16, 15 * 256])
    nhs = dout("nhs", [16, 6, 128, 128])
    ncs = dout("ncs", [16, 2, 1024])

    P = Prog()
    with ExitStack() as es:
        E = es.enter_context

        def sb(name, shape, dt=F32):
            return E(nc.sbuf_tensor(name, shape, dt))

        x_sb = sb("x_sb", [128, 16, 1024])
        xs_sb = sb("xs_sb", [128, 1024])
        hT = sb("hT", [128, 8, NTOK], BF16)
        mixT = sb("mixT", [128, 8, NTOK], BF16)
        ring = sb("ring", [128, NSLOT, 4096], BF16)
        arena = sb("arena", [128, NPAGE * 512])
        identb = sb("identb", [128, 128], BF16)
        identf = sb("identf", [128, 128])
        maskT = sb("maskT", [128, 128])
        onesm = sb("onesm", [128, 128])
        onesb = sb("onesb", [128, 128], BF16)
        wblk = sb("wblk", [128, 2, 128], BF16)
        gcols = sb("gcols_s", [128, 32])
        lbl = sb("lbl_s", [128, 18])
        hcols = sb("hcols", [128, 32])
        pscale = sb("pscale_s", [128, 2])
        cfixw = sb("cfixw_s", [128, 32])
        convwf = sb("convwf_s", [128, 24])
        id16 = sb("id16_s", [128, 16])
        nrm = sb("nrm", [128, 32])
        tmpc = sb("tmpc", [128, 16])
        gsm2 = [sb(f"gsm{i}", [128, 64]) for i in range(2)]
        Sst2 = [sb(f"Sst{i}", [128, 128]) for i in range(2)]
        Spb2 = [sb(f"Spb{i}", [128, 2, 128], BF16) for i in range(2)]
        tSb2 = [sb(f"tSb{i}", [128, 128]) for i in range(2)]
        psT = [E(nc.psum_tensor(f"pp{i}", [128, 1024], F32)) for i in range(4)]

        for e in Prog.ENG:
            P.esem[e] = E(nc.semaphore(f"es_{e}"))

        def new_dsem(name):
            d = DSem(E(nc.semaphore(name)))
            P.dsems.append(d)
            return d

        ring_ds = [new_dsem(f"ds_ring{i}") for i in range(NSLOT)]
        xy_ds = [new_dsem(f"ds_xy{i}") for i in range(17)]
        misc_ds = [new_dsem(f"ds_misc{i}") for i in range(8)]
        misc_ptr = [0]

        def misc():
            d = misc_ds[misc_ptr[0] % len(misc_ds)]
            misc_ptr[0] += 1
            return d

        block = E(nc.Block())

        B_x = [Buf(f"x{b}") for b in range(16)] + [Buf("xs")]
        B_hT = [Buf(f"hT{t}") for t in range(5)]
        B_mix = [Buf(f"mix{c}") for c in range(8)]
        B_ring = [Buf(f"ring{i}") for i in range(NSLOT)]
        B_pg = [Buf(f"pg{i}") for i in range(NPAGE)]
        B_ps = [Buf(f"ps{i}") for i in range(8)]
        B_const = Buf("const")
        B_wblk = Buf("wblk")
        B_hcols = Buf("hcols")
        B_nrm = [Buf("nrm0"), Buf("nrm1"), Buf("nrm1b")]
        B_tmpc = Buf("tmpc")
        B_dummy = [Buf("dummy0"), Buf("dummy1")]
        B_gsm2 = [Buf("gsm0"), Buf("gsm1")]
        B_S2 = [Buf("S0"), Buf("S1")]
        B_Sp2 = [[Buf("Sp00"), Buf("Sp01")], [Buf("Sp10"), Buf("Sp11")]]
        B_tS2 = [Buf("tS0"), Buf("tS1")]
        B_nrm2 = [Buf("nrm2"), Buf("nrm3")]

        def ACT(out, in_, func, R, W, scale=1.0, bias=0.0, accum=None):
            def f(e):
                if accum is None:
                    return e.activation(out=out, in_=in_, func=func, bias=bias, scale=scale)
                return e.activation(out=out, in_=in_, func=func, bias=bias, scale=scale, accum_out=accum)
            return P.op("act", f, R, W, multi=accum is not None)

        def TT(out, in0, in1, op, R, W, eng="dve"):
            return P.op(eng, lambda e: e.tensor_tensor(out=out, in0=in0, in1=in1, op=op), R, W)

        def TS(out, in0, s1, op0, R, W, s2=None, op1=None, eng="dve"):
            def f(e):
                if op1 is None:
                    return e.tensor_scalar(out=out, in0=in0, scalar1=s1, scalar2=None, op0=op0)
                return e.tensor_scalar(out=out, in0=in0, scalar1=s1, scalar2=s2, op0=op0, op1=op1)
            return P.op(eng, f, R, W)

        def STT(out, in0, scalar, in1, op0, op1, R, W):
            return P.op("dve", lambda e: e.scalar_tensor_tensor(out=out, in0=in0, scalar=scalar, in1=in1,
                                                                 op0=op0, op1=op1), R, W)

        def CP(out, in_, R, W, eng="dve"):
            return P.op(eng, lambda e: e.tensor_copy(out=out, in_=in_), R, W)

        def MEMSET(ap, val, W, eng="dve"):
            return P.op(eng, lambda e: e.memset(ap, val), (), W)

        def MMG(mms, R, W):
            def f(e):
                ins = None
                for (o, l, r, s, t) in mms:
                    ins = e.matmul(o, l, r, start=s, stop=t)
                return ins
            return P.op("pe", f, R, W, multi=True)

        def TRG(trs, R, W):
            def f(e):
                ins = None
                for (o, i, idn) in trs:
                    ins = e.transpose(out=o, in_=i, identity=idn)
                return ins
            return P.op("pe", f, R, W, multi=True)

        def DMA(q, dsem, pairs, R, W, **kw):
            fns = [(lambda e, o=o, i=i: e.dma_start(out=o, in_=i, **kw)) for (o, i) in pairs]
            return P.dma(q, dsem, fns, R, W)

        def bank(k):
            return psT[k // 2][:, (k % 2) * 512:(k % 2 + 1) * 512]

        def bank_bf(k):
            return bank(k).bitcast(BF16)

        def pair(i):
            return psT[i][:, :]

        class Rot:
            def __init__(self, items):
                self.items = items
                self.i = 0

            def next(self):
                v = self.items[self.i % len(self.items)]
                self.i += 1
                return v

        def pgf(p, n=1):
            return arena[:, p * 512:(p + n) * 512]

        def pgb(p, n=1):
            return [B_pg[i] for i in range(p, p + n)]

        unit_cols = {}
        for u in range(NU):
            unit_cols[u] = 4096
        unit_cols[U_POOL] = 2048
        for c in range(8):
            unit_cols[U_CONV + c] = 3072
        next_load = [0]

        def issue_load(u):
            s = u % NSLOT
            ncol = unit_cols[u]
            pairs = []
            c0 = 0
            while c0 < ncol:
                c1 = min(c0 + 2048, ncol)
                pairs.append((ring[:, s, c0:c1], wst[u, :, c0:c1]))
                c0 = c1
            DMA("pool", ring_ds[s], pairs, list(load_gate), [B_ring[s]])

        load_gate = []

        def need(u, la=NSLOT - 1):
            lim = min(u + la, NU - 1)
            while next_load[0] <= lim:
                issue_load(next_load[0])
                next_load[0] += 1
            return u % NSLOT

        def tokslice(t):
            if t < 4:
                return slice(t * 512, (t + 1) * 512)
            return slice(NT, NTOK)

        need(0, 1)
        DMA("sp", misc(), [
            (identb[:, :], c_identb), (identf[:, :], c_identf), (maskT[:, :], c_maskT), (onesm[:, :], c_onesm),
            (gcols[:, :], gcols_d), (lbl[:, :], lbl_d), (hcols[:, 18:24], gain_d), (pscale[:, :], pscale_d),
            (cfixw[:, :], c_cfixw), (convwf[:, :], convwf_d), (id16[0:16, :], c_id16),
        ], (), [B_const, B_hcols])
        for b in range(16):
            DMA("sp", xy_ds[b], [(x_sb[:, b, :], xp[b])], (), [B_x[b]])
        DMA("sp", xy_ds[16], [(xs_sb[0:16, :], xs)], (), [B_x[16]])
        MEMSET(wblk[:, :, :], 0.0, [B_wblk])
        DMA("pool", new_dsem("ds_wblk"), [
            (wblk[(g % 2) * 64:(g % 2) * 64 + 64, g // 2, (g % 2) * 64:(g % 2) * 64 + 64], poolw_d[g]) for g in range(4)
        ], (), [B_wblk])
        ACT(onesb[:, :], onesm[:, :], AF.Copy, [B_const], [B_const])
        ACT(lbl[:, :], lbl[:, :], AF.Exp, [B_const], [B_const])
        TT(tmpc[:, 0:6], lbl[:, 0:6], lbl[:, 6:12], ALU.add, [B_const], [B_tmpc])
        TT(tmpc[:, 0:6], tmpc[:, 0:6], lbl[:, 12:18], ALU.add, [B_const, B_tmpc], [B_tmpc])
        P.op("dve", lambda e: e.reciprocal(out=tmpc[:, 6:12], in_=tmpc[:, 0:6]), [B_tmpc], [B_tmpc])
        TT(hcols[:, 0:6], lbl[:, 0:6], tmpc[:, 6:12], ALU.mult, [B_const, B_tmpc], [B_hcols])
        TS(hcols[:, 6:12], hcols[:, 0:6], -1.0, ALU.mult, [B_hcols], [B_hcols], s2=1.0, op1=ALU.add)
        TS(hcols[:, 12:18], hcols[:, 6:12], -1.0, ALU.mult, [B_hcols], [B_hcols])
        ACT(hcols[:, 24:30], hcols[:, 6:12], AF.Ln, [B_hcols], [B_hcols])
        LB, OML, NOML, GAIN, LNOML = 0, 6, 12, 18, 24

        ps_norm = Rot([0, 1])
        norm_ctr = [0]
        norm_pending = []
        norm_junk = [[15, 16, 17]]

        def emit_norm(b, n):
            nj = len(norm_junk[0])
            k = norm_ctr[0] % nj
            norm_ctr[0] += 1
            np_ = 128 if b < 16 else 16
            xb = x_sb[:, b, :] if b < 16 else xs_sb[0:16, :]
            Bx = B_x[b]
            junk = VPGF(norm_junk[0][k])[0:np_, :].bitcast(BF16)
            Bj = VPGB(norm_junk[0][k])
            ss = nrm[0:np_, 4 * k + 0:4 * k + 1]
            lt = nrm[0:np_, 4 * k + 1:4 * k + 2]
            rs = nrm[0:np_, 4 * k + 2:4 * k + 3]
            Bn = [B_nrm[k]]
            ACT(junk, xb, AF.Square, [Bx], Bj + Bn, accum=ss)
            ACT(lt, ss, AF.Ln, Bn, Bn, scale=1.0 / 1024, bias=EPS)
            ACT(rs, lt, AF.Exp, Bn, Bn, scale=-0.5)
            if n == 4:
                gfin = pgf(10, 2)
                STT(xb, xb, rs, gfin[0:np_, :], ALU.mult, ALU.mult, [Bx] + Bn + pgb(10, 2), [Bx])
                if b < 16:
                    DMA("sp", xy_ds[b], [(yp[b], x_sb[:, b, :])], [Bx], ())
                else:
                    DMA("sp", xy_ds[16], [(ys, xs_sb[0:16, :])], [Bx], ())
                return
            if n == 0:
                TS(junk, xb, rs, ALU.mult, [Bx] + Bn, Bj)
            else:
                ACT(junk, xb, AF.Copy, [Bx] + Bn, Bj, scale=rs)

            def part_b():
                pk = ps_norm.next()
                pb = bank_bf(pk)
                if b < 16:
                    TRG([(pb[:, c * 128:(c + 1) * 128], junk[:, c * 128:(c + 1) * 128], identb[:, :]) for c in range(8)],
                        Bj + [B_const], [B_ps[pk]])
                    t = b // 4
                    TT(hT[:, :, b * 128:(b + 1) * 128], pb[:, 0:1024].rearrange("p (c t) -> p c t", t=128),
                       gcols[:, n * 8:(n + 1) * 8].unsqueeze(2).broadcast_to([128, 8, 128]), ALU.mult,
                       [B_ps[pk], B_const], [B_hT[t]])
                else:
                    TRG([(pb[:, c * 16:(c + 1) * 16], junk[:, c * 128:(c + 1) * 128], identb[0:16, 0:16]) for c in range(8)],
                        Bj + [B_const], [B_ps[pk]])
                    TT(hT[:, :, NT:NTOK], pb[:, 0:128].rearrange("p (c t) -> p c t", t=16),
                       gcols[:, n * 8:(n + 1) * 8].unsqueeze(2).broadcast_to([128, 8, 16]), ALU.mult,
                       [B_ps[pk], B_const], [B_hT[4]])
            norm_pending.append(part_b)

        def flush_norm(keep=0):
            while len(norm_pending) > keep:
                norm_pending.pop(0)()

        ps_pair = Rot([2, 3])

        def emit_outproj(u0, norm_after):
            need(u0)
            s0, s1 = u0 % NSLOT, (u0 + 1) % NSLOT
            for b in list(range(16)) + [16]:
                pi = ps_pair.next()
                pp = pair(pi)
                tk = slice(b * 128, (b + 1) * 128) if b < 16 else slice(NT, NTOK)
                np_ = 128 if b < 16 else 16
                mms = []
                for half, s in ((0, s0), (1, s1)):
                    for kc in range(8):
                        mms.append((pp[0:np_, half * 512:(half + 1) * 512], mixT[:, kc, tk],
                                    ring[:, s, kc * 512:(kc + 1) * 512], kc == 0, kc == 7))
                MMG(mms, B_mix + [B_ring[s0], B_ring[s1]], [B_ps[2 * pi], B_ps[2 * pi + 1]])
                xb = x_sb[:, b, :] if b < 16 else xs_sb[0:16, :]
                TT(xb, xb, pp[0:np_, :], ALU.add, [B_x[b], B_ps[2 * pi], B_ps[2 * pi + 1]], [B_x[b]])
                emit_norm(b, norm_after)
                flush_norm(keep=2)
            flush_norm()

        def emit_mlp(u0, norm_after):
            ps_single = Rot([0, 1, 2, 3])
            ps_pr = Rot([2, 3])
            steps = [(s, t) for s in range(8) for t in range(5)]
            hid = [mixT[:, 0, 0:2048].rearrange("p (c t) -> p c t", t=512),
                   mixT[:, 1, 0:2048].rearrange("p (c t) -> p c t", t=512)]
            hid_s = [mixT[:, 2, 0:64].rearrange("p (c t) -> p c t", t=16),
                     mixT[:, 3, 0:64].rearrange("p (c t) -> p c t", t=16)]
            B_hid = [B_mix[0], B_mix[1]]
            B_hidc = [[Buf(f"hid{i}_{n}") for n in range(4)] for i in range(2)]
            B_hids = [B_mix[2], B_mix[3]]

            def ff1_items(k):
                s, t = steps[k]
                items = []
                if t == 0:
                    items.append(lambda s=s: need(u0 + 2 * s, NSLOT - 2))
                if t == 2:
                    items.append(lambda s=s: need(u0 + 2 * s + 1, NSLOT - 2))
                sl = (u0 + 2 * s) % NSLOT
                if t < 4:
                    hb = hid[k % 2]
                    for n in range(4):
                        def it(n=n, hb=hb, sl=sl, t=t, k=k):
                            pk = ps_single.next()
                            MMG([(bank(pk), ring[:, sl, kc * 512 + n * 128:kc * 512 + (n + 1) * 128],
                                  hT[:, kc, tokslice(t)], kc == 0, kc == 7) for kc in range(8)],
                                [B_ring[sl], B_hT[t]], [B_ps[pk]])
                            Wh = [B_hidc[k % 2][n]] + ([B_hid[k % 2]] if k < 2 else [])
                            ACT(hb[:, n, :], bank(pk), AF.Relu, [B_ps[pk]], Wh)
                            TT(hb[:, n, :], hb[:, n, :], hb[:, n, :], ALU.mult, [B_hidc[k % 2][n]], [B_hidc[k % 2][n]])
                        items.append(it)
                else:
                    hb = hid_s[s % 2]

                    def it(hb=hb, sl=sl, s=s):
                        pk = ps_single.next()
                        mms = []
                        for n in range(4):
                            for kc in range(8):
                                mms.append((bank(pk)[:, n * 16:(n + 1) * 16],
                                            ring[:, sl, kc * 512 + n * 128:kc * 512 + (n + 1) * 128],
                                            hT[:, kc, NT:NTOK], kc == 0, kc == 7))
                        MMG(mms, [B_ring[sl], B_hT[4]], [B_ps[pk]])
                        ACT(hb, bank(pk)[:, 0:64].rearrange("p (c t) -> p c t", t=16), AF.Relu, [B_ps[pk]], [B_hids[s % 2]])
                        TT(hb, hb, hb, ALU.mult, [B_hids[s % 2]], [B_hids[s % 2]])
                    items.append(it)
                return items

            def ff2_items(k):
                s, t = steps[k]
                sl2 = (u0 + 2 * s + 1) % NSLOT
                items = []
                if t < 4:
                    hb = hid[k % 2]
                    for j in range(4):
                        def it(j=j, hb=hb, sl2=sl2, t=t, s=s, k=k):
                            b = t * 4 + j
                            pi = ps_pr.next()
                            pp = pair(pi)
                            mms = []
                            for half in range(2):
                                for kc in range(4):
                                    mms.append((pp[:, half * 512:(half + 1) * 512], hb[:, kc, j * 128:(j + 1) * 128],
                                                ring[:, sl2, kc * 1024 + half * 512:kc * 1024 + (half + 1) * 512],
                                                kc == 0, kc == 3))
                            MMG(mms, [B_ring[sl2], B_hid[k % 2]] + B_hidc[k % 2], [B_ps[2 * pi], B_ps[2 * pi + 1]])
                            TT(x_sb[:, b, :], x_sb[:, b, :], pp, ALU.add, [B_x[b], B_ps[2 * pi], B_ps[2 * pi + 1]], [B_x[b]])
                            if s == 7:
                                emit_norm(b, norm_after)
                                flush_norm(keep=2)
                        items.append(it)
                else:
                    hb = hid_s[s % 2]

                    def it(hb=hb, sl2=sl2, s=s):
                        pi = ps_pr.next()
                        pp = pair(pi)
                        mms = []
                        for half in range(2):
                            for kc in range(4):
                                mms.append((pp[0:16, half * 512:(half + 1) * 512], hb[:, kc, :],
                                            ring[:, sl2, kc * 1024 + half * 512:kc * 1024 + (half + 1) * 512],
                                            kc == 0, kc == 3))
                        MMG(mms, [B_ring[sl2], B_hids[s % 2]], [B_ps[2 * pi], B_ps[2 * pi + 1]])
                        TT(xs_sb[0:16, :], xs_sb[0:16, :], pp[0:16, :], ALU.add,
                           [B_x[16], B_ps[2 * pi], B_ps[2 * pi + 1]], [B_x[16]])
                        if s == 7:
                            emit_norm(16, norm_after)
                            flush_norm()
                    items.append(it)
                return items

            for it in ff1_items(0):
                it()
            for k in range(len(steps)):
                a = ff1_items(k + 1) if k + 1 < len(steps) else []
                bb = ff2_items(k)
                for i in range(max(len(a), len(bb))):
                    if i < len(a):
                        a[i]()
                    if i < len(bb):
                        bb[i]()

        def VPGF(i):
            if i < NPAGE:
                return arena[:, i * 512:(i + 1) * 512]
            j = i - NPAGE
            return mixT[:, j // 2, (j % 2) * 1024:(j % 2 + 1) * 1024].bitcast(F32)

        def VPGB(i):
            if i < NPAGE:
                return [B_pg[i]]
            return [B_mix[(i - NPAGE) // 2]]

        class Small:
            def __init__(self, pages):
                self.pages = pages

            def slot(self, i, np_=16, w=128, n=1):
                assert (i % 4) + n <= 4
                return VPGF(self.pages[i // 4])[0:np_, (i % 4) * 128:(i % 4) * 128 + (n - 1) * 128 + w]

            def bufs(self):
                out = []
                for p in self.pages:
                    out += VPGB(p)
                return out

        class Cx:
            pass

        def make_cx(ci, prompt_pages, sh_base, vm, sbf, msk, small_pages, banks):
            cx = Cx()
            cx.ci = ci
            cx.pp = prompt_pages
            cx.sh_base, cx.vm, cx.sbf, cx.msk = sh_base, vm, sbf, msk
            cx.small = Small(small_pages)
            cx.ps = Rot(banks)
            cx.banks = banks
            cx.gsm = gsm2[ci]
            cx.B_gsm = B_gsm2[ci]
            cx.Sst, cx.Spb, cx.tSb = Sst2[ci], Spb2[ci], tSb2[ci]
            cx.B_S, cx.B_Sp, cx.B_tS = B_S2[ci], B_Sp2[ci], B_tS2[ci]
            cx.nrmc = 16 + 4 * ci
            cx.B_nrm = B_nrm2[ci]
            return cx

        def gen_head_prompt(h, t, sl, cx):
            assert norm_done[0] >= 4 * (t + 1) and (t == 0 or not norm_pending or norm_done[0] > 4 * (t + 1)), (t, norm_done[0])
            T0 = t * 512
            tk = slice(T0, T0 + 512)
            ps = cx.ps
            gs = cx.gsm
            Bg = [cx.B_gsm]

            def W(kc, a, b):
                return ring[:, sl, kc * 512 + a:kc * 512 + b]

            Rw = [B_ring[sl], B_hT[t]]
            pg = cx.pp
            sn, si, lf, G, Ei, osq, rstd = (VPGF(pg[i]) for i in range(7))
            Bsn, Bsi, Blf, BG, BEi, Bosq, Brstd = (VPGB(pg[i]) for i in range(7))
            Ee, rel = si, lf
            p7 = VPGF(pg[7]).bitcast(BF16)
            p8 = VPGF(pg[8]).bitcast(BF16)
            p9 = VPGF(pg[9]).bitcast(BF16)
            B7, B8, B9 = VPGB(pg[7]), VPGB(pg[8]), VPGB(pg[9])
            vtok, qd = p7[:, 0:512], p7[:, 512:1024]
            kd, kt = p8[:, 0:512], p8[:, 512:1024]
            sm, sg = p9[:, 0:512], p9[:, 512:1024]
            if t == 0 or not PREFETCH_F:
                cx.kf_next = ps.next()
                MMG([(bank(cx.kf_next), W(kc, 128, 256), hT[:, kc, tk], kc == 0, kc == 7) for kc in range(8)], Rw,
                    [B_ps[cx.kf_next]])
            kf = cx.kf_next
            yield
            ef, L2 = sn, Ei
            ACT(ef, bank(kf), AF.Exp, [B_ps[kf]], Bsn)
            yield
            kq = ps.next()
            MMG([(bank(kq), W(kc, 0, 128), hT[:, kc, tk], kc == 0, kc == 7) for kc in range(8)], Rw, [B_ps[kq]])
            yield
            ki = ps.next()
            MMG([(bank(ki)[:, j * 128:(j + 1) * 128], hT[:, kc, T0 + j * 128:T0 + (j + 1) * 128], W(kc, 256, 384),
                  kc == 0, kc == 7) for j in range(4) for kc in range(8)], Rw, [B_ps[ki]])
            yield
            kg = ps.next()
            MMG([(bank(kg), W(kc, 384, 512), hT[:, kc, tk], kc == 0, kc == 7) for kc in range(8)], Rw, [B_ps[kg]])
            yield
            ACT(L2, ef, AF.Ln, Bsn, BEi, bias=1.0)
            ACT(lf, ef, AF.Ln, Bsn + [B_hcols], Blf, bias=hcols[:, LB + h:LB + h + 1])
            yield
            if t == 0:
                MEMSET(cx.Sst[:, :], 0.0, [cx.B_S])
                P.op("dve", lambda e: e.tensor_tensor_scan(out=G, data0=lf, data1=L2, initial=0.0,
                                                           op0=ALU.add, op1=ALU.subtract), Blf + BEi, BG)
                MEMSET(gs[:, 16:17], 0.0, Bg)
            else:
                CP(gs[:, 16:17], gs[:, 20:21], Bg, Bg)
                P.op("dve", lambda e: e.tensor_tensor_scan(out=G, data0=lf, data1=L2, initial=gs[:, 16:17],
                                                           op0=ALU.add, op1=ALU.subtract), Blf + BEi + Bg, BG)
            yield
            G3 = G.rearrange("p (j t) -> p j t", t=128)
            TT(rel.rearrange("p (j t) -> p j t", t=128), G3, G3[:, :, 63:64].broadcast_to([128, 4, 128]), ALU.subtract,
               BG, Blf)
            yield
            ACT(Ee, rel, AF.Exp, Blf, Bsi)
            TT(L2, rel, L2, ALU.add, Blf + BEi, BEi)
            yield
            TT(qd, bank(kq), Ee, ALU.mult, [B_ps[kq]] + Bsi, B7)
            ACT(kd, L2, AF.Exp, BEi + [B_hcols], B8, scale=-1.0, bias=hcols[:, LNOML + h:LNOML + h + 1])
            yield
            CP(gs[:, 17:21], G3[:, :, 127], BG, Bg)
            TT(gs[:, 24:28], G3[:, :, 63], gs[:, 16:20], ALU.subtract, BG + Bg, Bg)
            TT(gs[:, 28:32], gs[:, 17:21], G3[:, :, 63], ALU.subtract, BG + Bg, Bg)
            TT(gs[:, 32:36], gs[:, 17:21], gs[:, 16:20], ALU.subtract, Bg, Bg)
            ACT(gs[:, 40:52], gs[:, 24:36], AF.Exp, Bg, Bg)
            eM, eBM, eB = 40, 44, 48
            yield
            ACT(si, bank(ki), AF.Sigmoid, [B_ps[ki]], Bsi)
            ACT(sg, bank(kg), AF.Sigmoid, [B_ps[kg]], B9)
            if cx.ci == 1:
                ACT(tmpc[:, 12 + cx.ci:13 + cx.ci], hcols[:, 0:1], AF.Exp, [B_hcols], [B_dummy[cx.ci]])
            yield
            TT(vtok, bank(ki), si, ALU.mult, [B_ps[ki]] + Bsi, B7)
            yield
            kt_ps = ps.next()
            TRG([(bank_bf(kt_ps)[:, j * 128:(j + 1) * 128], kd[:, j * 128:(j + 1) * 128], identb[:, :]) for j in range(4)],
                B8 + [B_const], [B_ps[kt_ps]])
            ks = ps.next()
            MMG([(bank(ks)[:, j * 128:(j + 1) * 128], kd[:, j * 128:(j + 1) * 128], qd[:, j * 128:(j + 1) * 128], True, True)
                 for j in range(4)], B7 + B8, [B_ps[ks]])
            yield
            ACT(kt, bank_bf(kt_ps)[:, 0:512], AF.Copy, [B_ps[kt_ps]], B8)
            TT(sm.rearrange("p (j t) -> p j t", t=128), bank(ks).rearrange("p (j t) -> p j t", t=128),
               maskT[:, :].unsqueeze(1).broadcast_to([128, 4, 128]), ALU.mult, [B_ps[ks], B_const], B9)
            yield
            kdS = ps.next()
            MMG([(bank(kdS)[:, j * 128:(j + 1) * 128], kt[:, j * 128:(j + 1) * 128], vtok[:, j * 128:(j + 1) * 128], True, True)
                 for j in range(4)], B7 + B8, [B_ps[kdS]])
            yield
            if PREFETCH_F and t < 3:
                cx.kf_next = ps.next()
                tk2 = slice(T0 + 512, T0 + 1024)
                MMG([(bank(cx.kf_next), W(kc, 128, 256), hT[:, kc, tk2], kc == 0, kc == 7) for kc in range(8)],
                    [B_ring[sl], B_hT[t + 1]], [B_ps[cx.kf_next]])
            tSa = lf.rearrange("p (j v) -> p j v", v=128)
            TT(tSa, bank(kdS).rearrange("p (j v) -> p j v", v=128),
               gs[:, eBM:eBM + 4].unsqueeze(2).broadcast_to([128, 4, 128]), ALU.mult, [B_ps[kdS]] + Bg, Blf)
            yield
            ko = ps.next()
            for j in range(4):
                sp_i = j % 2
                ACT(cx.Spb[:, sp_i, :], cx.Sst[:, :], AF.Copy, [cx.B_S] + Bg, [cx.B_Sp[sp_i]], scale=gs[:, eM + j:eM + j + 1])
                MMG([(bank(ko)[:, j * 128:(j + 1) * 128], cx.Spb[:, sp_i, :], qd[:, j * 128:(j + 1) * 128], True, False),
                     (bank(ko)[:, j * 128:(j + 1) * 128], vtok[:, j * 128:(j + 1) * 128], sm[:, j * 128:(j + 1) * 128], False, True)],
                    [cx.B_Sp[sp_i]] + B7 + B9, [B_ps[ko]])
                STT(cx.Sst[:, :], cx.Sst[:, :], gs[:, eB + j:eB + j + 1], tSa[:, j, :], ALU.mult, ALU.add,
                    [cx.B_S] + Blf + Bg, [cx.B_S])
                yield
            osqb = osq.bitcast(BF16)[:, 0:512]
            ACT(osqb, bank(ko), AF.Square, [B_ps[ko]], Bosq)
            yield
            km = ps.next()
            MMG([(bank(km), onesb[:, :], osqb, True, True)], Bosq + [B_const], [B_ps[km]])
            yield
            ACT(rstd, bank(km), AF.Ln, [B_ps[km]], Brstd, bias=EPS)
            ACT(rstd, rstd, AF.Exp, Brstd, Brstd, scale=-0.5)
            yield
            TT(osq, bank(ko), rstd, ALU.mult, [B_ps[ko]] + Brstd, Bosq)
            STT(mixT[:, 2 + h, tk], osq, hcols[:, GAIN + h:GAIN + h + 1], sg, ALU.mult, ALU.mult,
                Bosq + B9 + [B_hcols], [B_mix[2 + h]])
            if t == 3:
                DMA("sp", misc(), [(nhp[h], cx.Sst[:, :])], [cx.B_S], ())
            yield

        def gen_head_sample(h, sl, cx):
            ps = cx.ps
            Rw = [B_ring[sl], B_hT[4]]
            Sh = arena[:, cx.sh_base * 512:(cx.sh_base + 4) * 512].rearrange("p (b v) -> p b v", v=128)
            BSh = pgb(cx.sh_base, 4)
            DMA("sp", misc(), [(Sh[:, 4 * g:4 * g + 4, :], st_hgrn[4 * g:4 * g + 4, h].rearrange("b k v -> k b v"))
                               for g in range(4)], (), BSh)
            kp = ps.next()
            MMG([(bank(kp)[0:16, :], hT[:, kc, NT:NTOK], ring[:, sl, kc * 512:(kc + 1) * 512], kc == 0, kc == 7)
                 for kc in range(8)], Rw, [B_ps[kp]])
            yield
            S_ = cx.small
            Bs = S_.bufs()
            qfig = S_.slot(0, n=4)
            si_s, vv, kk = (S_.slot(i) for i in (4, 5, 6))
            wide = S_.slot(9, np_=128, w=128)
            rs_s, t1_s = wide[:, 0:16], wide[:, 16:32]
            osq_s = wide[:, 32:48].bitcast(BF16)[:, 0:16]
            qT = S_.slot(10, np_=128, w=16).bitcast(BF16)[:, 0:16]
            fm = S_.slot(11, np_=128, w=64)
            snT, kkT, fT, sgT = fm[:, 0:16], fm[:, 16:32], fm[:, 32:48], fm[:, 48:64]
            ACT(si_s, bank(kp)[0:16, 256:384], AF.Sigmoid, [B_ps[kp]], Bs)
            ACT(qfig, bank(kp)[0:16, 0:512], AF.Copy, [B_ps[kp]], Bs)
            yield
            TT(vv, bank(kp)[0:16, 256:384], si_s, ALU.mult, [B_ps[kp]] + Bs, Bs)
            k2 = ps.next()
            TRG([(bank(k2)[:, 0:16], qfig[:, 0:128], identf[0:16, 0:16]),
                 (bank(k2)[:, 16:32], qfig[:, 128:256], identf[0:16, 0:16]),
                 (bank(k2)[:, 32:48], qfig[:, 384:512], identf[0:16, 0:16])], Bs + [B_const], [B_ps[k2]])
            yield
            ACT(snT, bank(k2)[:, 16:32], AF.Sigmoid, [B_ps[k2]], Bs, scale=-1.0)
            ACT(sgT, bank(k2)[:, 32:48], AF.Sigmoid, [B_ps[k2]], Bs)
            ACT(qT, bank(k2)[:, 0:16], AF.Copy, [B_ps[k2]], Bs)
            yield
            TS(kkT, snT, hcols[:, OML + h:OML + h + 1], ALU.mult, Bs + [B_hcols], Bs)
            TS(fT, kkT, -1.0, ALU.mult, Bs, Bs, s2=1.0, op1=ALU.add)
            k3 = ps.next()
            TRG([(bank(k3)[0:16, 0:128], kkT, identf[:, :])], Bs + [B_const], [B_ps[k3]])
            yield
            kkb = kk.bitcast(BF16)[:, 0:128]
            ACT(kkb, bank(k3)[0:16, 0:128], AF.Copy, [B_ps[k3]], Bs)
            kk = kkb
            yield
            Vmf = VPGF(cx.vm)[0:16, :].bitcast(BF16)
            BVm = VPGB(cx.vm)
            for g in range(4):
                Vm = Vmf[:, (g % 2) * 512:(g % 2 + 1) * 512].rearrange("p (b v) -> p b v", v=128)
                TT(Vm, vv.unsqueeze(1).broadcast_to([16, 4, 128]),
                   id16[0:16, 4 * g:4 * g + 4].unsqueeze(2).broadcast_to([16, 4, 128]), ALU.mult, Bs + [B_const], BVm)
                kk_ps = ps.next()
                MMG([(bank(kk_ps), kk, Vm, True, True)], Bs + BVm, [B_ps[kk_ps]])
                TT(Sh[:, 4 * g:4 * g + 4, :], Sh[:, 4 * g:4 * g + 4, :],
                   fT[:, 4 * g:4 * g + 4].unsqueeze(2).broadcast_to([128, 4, 128]), ALU.mult, BSh + Bs, BSh)
                yield
                TT(Sh[:, 4 * g:4 * g + 4, :], Sh[:, 4 * g:4 * g + 4, :], bank(kk_ps).rearrange("p (b v) -> p b v", v=128),
                   ALU.add, BSh + [B_ps[kk_ps]], BSh)
                yield
            DMA("sp", misc(), [(nhs[4 * g:4 * g + 4, h].rearrange("b k v -> k b v"), Sh[:, 4 * g:4 * g + 4, :])
                               for g in range(4)], BSh, ())
            Sbf = VPGF(cx.sbf).bitcast(BF16)
            BSbf = VPGB(cx.sbf)
            ko_ = ps.next()
            oT = bank(ko_)[:, 0:16]
            for g in range(4):
                sb_g = Sbf[:, (g % 2) * 512:(g % 2 + 1) * 512].rearrange("p (b v) -> p b v", v=128)
                ACT(sb_g, Sh[:, 4 * g:4 * g + 4, :], AF.Copy, BSh, BSbf)
                yield
                MMG([(bank(ko_)[:, 4 * g + bb:4 * g + bb + 1], sb_g[:, bb, :], qT[:, 4 * g + bb:4 * g + bb + 1], True, True)
                     for bb in range(4)], Bs + BSbf, [B_ps[ko_]])
                yield
            ACT(osq_s, oT, AF.Square, [B_ps[ko_]], Bs)
            yield
            km_ = ps.next()
            MMG([(bank(km_)[:, 0:16], onesb[:, :], osq_s, True, True)], Bs + [B_const], [B_ps[km_]])
            yield
            ACT(rs_s, bank(km_)[:, 0:16], AF.Ln, [B_ps[km_]], Bs, bias=EPS)
            ACT(rs_s, rs_s, AF.Exp, Bs, Bs, scale=-0.5)
            yield
            TT(t1_s, oT, rs_s, ALU.mult, [B_ps[ko_]] + Bs, Bs)
            STT(mixT[:, 2 + h, NT:NTOK], t1_s, hcols[:, GAIN + h:GAIN + h + 1], sgT, ALU.mult, ALU.mult,
                Bs + [B_hcols], [B_mix[2 + h]])
            yield

        def gen_head(h, cx):
            sl = (U_HEAD + h) % NSLOT
            for t in range(4):
                yield from gen_head_prompt(h, t, sl, cx)
            yield from gen_head_sample(h, sl, cx)

        def interleave(gens, offset=0):
            gens = list(gens)
            for _ in range(offset):
                try:
                    next(gens[0])
                except StopIteration:
                    break
            while gens:
                for g in list(gens):
                    try:
                        next(g)
                    except StopIteration:
                        gens.remove(g)

        ps_all = Rot(list(range(8)))

        def gen_pool_chunk(c, sl):
            pm = ({"ue": 0, "A": 4, "B": 6, "C": 6, "d": 10, "fx": 16}, {"ue": 2, "A": 8, "B": 11, "C": 13, "d": 15, "fx": 17})[c]
            ue = pgf(pm["ue"], 2)[:, 0:528]
            Bu = pgb(pm["ue"], 2)
            A_ = pgf(pm["A"], 2)[:, 0:528]
            B_ = pgf(pm["B"], 2)[:, 0:528]
            C_ = pgf(pm["C"], 2)[:, 0:528]
            BA, BB, BC = pgb(pm["A"], 2), pgb(pm["B"], 2), pgb(pm["C"], 2)
            dbf = pgf(pm["d"]).bitcast(BF16)[:, 0:512]
            Bd = pgb(pm["d"])
            fx = pgf(pm["fx"])[:, 0:16]
            Bfx = pgb(pm["fx"])
            def proj(tt):
                k_ = ps_all.next()
                tk_ = slice(tt * 512, (tt + 1) * 512)
                MMG([(bank(k_), ring[:, sl, kc * 256 + c * 128:kc * 256 + (c + 1) * 128], hT[:, kc, tk_], kc == 0, kc == 7)
                     for kc in range(8)], [B_ring[sl], B_hT[tt]], [B_ps[k_]])
                return k_

            ku_next = proj(0)
            yield
            for t in range(4):
                T0 = t * 512
                tk = slice(T0, T0 + 512)
                ku = ku_next
                if t == 0:
                    MEMSET(ue[:, 0:16], 0.0, Bu)
                ACT(ue[:, 16:528], bank(ku), AF.Copy, [B_ps[ku]], Bu)
                if t < 3:
                    ku_next = proj(t + 1)
                yield
                TT(A_[:, 2:528], ue[:, 2:528], ue[:, 1:527], ALU.add, Bu, BA)
                yield
                if c == 0:
                    TT(B_[64:128, 4:528], A_[64:128, 4:528], A_[64:128, 2:526], ALU.add, BA, BB)
                    STT(dbf[0:64, :], A_[0:64, 16:528], 0.5, ue[0:64, 16:528], ALU.mult, ALU.subtract, BA + Bu, Bd)
                    yield
                    STT(dbf[64:128, :], B_[64:128, 16:528], 0.25, ue[64:128, 16:528], ALU.mult, ALU.subtract, BB + Bu, Bd)
                    sel_lo, sel_hi = A_, B_
                    yield
                    yield
                else:
                    TT(B_[:, 4:528], A_[:, 4:528], A_[:, 2:526], ALU.add, BA, BB)
                    yield
                    TT(C_[:, 8:528], B_[:, 8:528], B_[:, 4:524], ALU.add, BB, BC)
                    yield
                    TT(A_[64:128, 16:528], C_[64:128, 16:528], C_[64:128, 8:520], ALU.add, BC + BA, BA)
                    STT(dbf[0:64, :], C_[0:64, 16:528], 0.125, ue[0:64, 16:528], ALU.mult, ALU.subtract, BC + Bu, Bd)
                    yield
                    STT(dbf[64:128, :], A_[64:128, 16:528], 0.0625, ue[64:128, 16:528], ALU.mult, ALU.subtract, BA + Bu, Bd)
                    sel_lo, sel_hi = C_, A_
                if t == 0:
                    for (lo, hi, sel) in ((0, 64, sel_lo), (64, 128, sel_hi)):
                        TT(fx[lo:hi, :], sel[lo:hi, 16:32], cfixw[lo:hi, c * 16:(c + 1) * 16], ALU.mult,
                           [B_const] + BA + BB + BC, Bfx)
                        TT(dbf[lo:hi, 0:16], fx[lo:hi, :], ue[lo:hi, 16:32], ALU.subtract, Bfx + Bu, Bd)
                yield
                kp_ = ps_all.next()
                MMG([(bank(kp_), wblk[:, c, :], dbf, True, True)], Bd + [B_wblk], [B_ps[kp_]])
                yield
                ACT(mixT[:, c, tk], bank(kp_), AF.Copy, [B_ps[kp_], B_const], [B_mix[c]], scale=pscale[:, c:c + 1])
                if t < 3:
                    CP(ue[:, 0:16], ue[:, 512:528], Bu, Bu)
                else:
                    kx = ps_all.next()
                    TRG([(bank(kx)[0:15, 0:128], ue[:, 513:528], identf[:, :])], Bu + [B_const], [B_ps[kx]])
                    stg = pgf(pm["fx"])[0:15, 0:128]
                    ACT(stg, bank(kx)[0:15, 0:128], AF.Copy, [B_ps[kx]], Bfx)
                    DMA("sp", misc(), [(npp[:, c * 128:(c + 1) * 128], stg)], Bfx, ())
                yield

        def emit_pool_sample(sl):
            Rw = [B_ring[sl], B_hT[4]]
            stp = pgf(0, 8)[0:16, 0:3840]
            Bst = pgb(0, 8)
            DMA("sp", misc(), [(stp, st_pool)], (), Bst)
            stp3 = stp.rearrange("p (r c) -> p r c", c=256)
            ku = ps_all.next()
            MMG([(bank(ku)[0:16, 0:256], hT[:, kc, NT:NTOK], ring[:, sl, kc * 256:(kc + 1) * 256], kc == 0, kc == 7)
                 for kc in range(8)], Rw, [B_ps[ku]])
            sml = pgf(11, 3)
            Bs = pgb(11, 3)
            u_s = sml[0:16, 0:256]
            sums = sml[0:16, 256:512]
            dd = sml[0:16, 512:768].bitcast(BF16)[:, 0:256]
            ACT(u_s, bank(ku)[0:16, 0:256], AF.Copy, [B_ps[ku]], Bs)
            for g, w in enumerate((2, 4, 8, 16)):
                cs = slice(g * 64, (g + 1) * 64)
                if w == 2:
                    TT(sums[:, cs], stp3[:, 14, cs], u_s[:, cs], ALU.add, Bst + Bs, Bs)
                else:
                    P.op("dve", lambda e, cs=cs, w=w: e.tensor_reduce(
                        out=sums[:, cs], in_=stp3[:, 16 - w:15, cs].rearrange("p r c -> p c r"), axis=AX.X, op=ALU.add),
                        Bst, Bs)
                    TT(sums[:, cs], sums[:, cs], u_s[:, cs], ALU.add, Bs, Bs)
                STT(dd[:, cs], sums[:, cs], 1.0 / w, u_s[:, cs], ALU.mult, ALU.subtract, Bs, Bs)
            k2 = ps_all.next()
            TRG([(bank_bf(k2)[:, c * 16:(c + 1) * 16], dd[:, c * 128:(c + 1) * 128], identb[0:16, 0:16]) for c in range(2)],
                Bs + [B_const], [B_ps[k2]])
            dT = pgf(10).bitcast(BF16)[:, 0:32]
            ACT(dT, bank_bf(k2)[:, 0:32], AF.Copy, [B_ps[k2]], pgb(10))
            k3 = ps_all.next()
            MMG([(bank(k3)[:, c * 16:(c + 1) * 16], wblk[:, c, :], dT[:, c * 16:(c + 1) * 16], True, True) for c in range(2)],
                pgb(10) + [B_wblk], [B_ps[k3]])
            for c in range(2):
                ACT(mixT[:, c, NT:NTOK], bank(k3)[:, c * 16:(c + 1) * 16], AF.Copy, [B_ps[k3], B_const], [B_mix[c]],
                    scale=pscale[:, c:c + 1])
            DMA("sp", misc(), [(nps[:, 0:14 * 256], stp[:, 256:3840]), (nps[:, 14 * 256:15 * 256], u_s)], Bst + Bs, ())

        def gen_conv_prompt(c, t, sl, cx):
            ps = cx.ps
            T0 = t * 512
            tk = slice(T0, T0 + 512)
            Rw = [B_ring[sl], B_hT[t]]
            kc_, kh, kb = cx.banks[0], cx.banks[1], cx.banks[2]
            for (pk, off) in ((kc_, 128), (kh, 256), (kb, 0)):
                MMG([(bank(pk), ring[:, sl, k8 * 384 + off:k8 * 384 + off + 128], hT[:, k8, tk], k8 == 0, k8 == 7)
                     for k8 in range(8)], Rw, [B_ps[pk]])
            yield
            pg = cx.pp
            cgs = VPGF(pg[0])
            ze = arena[:, pg[1] * 512:(pg[1] + 2) * 512][:, 0:514]
            c1 = VPGF(pg[3])
            c2 = VPGF(pg[4])
            Bc, Bz, B1, B2 = VPGB(pg[0]), pgb(pg[1], 2), VPGB(pg[3]), VPGB(pg[4])
            if t == 0:
                MEMSET(ze[:, 0:2], 0.0, Bz)
            ACT(cgs, bank(kc_), AF.Copy, [B_ps[kc_]], Bc)
            yield
            TT(ze[:, 2:514], bank(kh), cgs, ALU.mult, [B_ps[kh]] + Bc, Bz)
            yield
            w0 = convwf[:, c * 3 + 0:c * 3 + 1]
            w1 = convwf[:, c * 3 + 1:c * 3 + 2]
            w2 = convwf[:, c * 3 + 2:c * 3 + 3]
            ACT(c1, ze[:, 0:512], AF.Copy, Bz + [B_const], B1, scale=w0)
            yield
            STT(c2, ze[:, 1:513], w1, c1, ALU.mult, ALU.add, Bz + B1 + [B_const], B2)
            yield
            STT(c1, ze[:, 2:514], w2, c2, ALU.mult, ALU.add, Bz + B2 + [B_const], B1)
            yield
            TT(mixT[:, c, tk], bank(kb), c1, ALU.mult, [B_ps[kb]] + B1, [B_mix[c]])
            if t < 3:
                CP(ze[:, 0:2], ze[:, 512:514], Bz, Bz)
            else:
                kx = cx.banks[0]
                TRG([(bank(kx)[0:2, 0:128], ze[:, 512:514], identf[:, :])], Bz + [B_const], [B_ps[kx]])
                stg = VPGF(pg[0])[0:2, 0:128]
                ACT(stg, bank(kx)[0:2, 0:128], AF.Copy, [B_ps[kx]], Bc)
                DMA("sp", misc(), [(ncp[:, c * 128:(c + 1) * 128], stg)], Bc, ())
            yield

        def conv_sample_part1(c, cx):
            sl = (U_CONV + c) % NSLOT
            Rw = [B_ring[sl], B_hT[4]]
            S_ = cx.small
            Bs = S_.bufs()
            prev = S_.slot(0, n=2).rearrange("p (r c) -> p r c", c=128)
            wbc = S_.slot(4, n=3).rearrange("p (r c) -> p r c", c=128)
            DMA("sp", misc(), [(prev, st_conv[:, :, c * 128:(c + 1) * 128]),
                               (wbc, convw_d[:, c * 128:(c + 1) * 128].partition_broadcast(16))], (), Bs)
            kp = cx.banks[3]
            MMG([(bank(kp)[0:16, 0:384], hT[:, k8, NT:NTOK], ring[:, sl, k8 * 384:(k8 + 1) * 384], k8 == 0, k8 == 7)
                 for k8 in range(8)], Rw, [B_ps[kp]])

        def gen_conv_sample_part2(c, cx):
            S_ = cx.small
            Bs = S_.bufs()
            prev = S_.slot(0, n=2).rearrange("p (r c) -> p r c", c=128)
            wbc = S_.slot(4, n=3).rearrange("p (r c) -> p r c", c=128)
            kp = cx.banks[3]
            cg_s, z_s, a_, b_ = S_.slot(2), S_.slot(3), S_.slot(7), S_.slot(8)
            ACT(cg_s, bank(kp)[0:16, 128:256], AF.Copy, [B_ps[kp]], Bs)
            yield
            TT(z_s, bank(kp)[0:16, 256:384], cg_s, ALU.mult, [B_ps[kp]] + Bs, Bs)
            TT(a_, prev[:, 0, :], wbc[:, 0, :], ALU.mult, Bs, Bs)
            yield
            TT(b_, prev[:, 1, :], wbc[:, 1, :], ALU.mult, Bs, Bs)
            TT(a_, a_, b_, ALU.add, Bs, Bs)
            yield
            TT(b_, z_s, wbc[:, 2, :], ALU.mult, Bs, Bs)
            TT(a_, a_, b_, ALU.add, Bs, Bs)
            yield
            yb = S_.slot(9).bitcast(BF16)[:, 0:128]
            TT(yb, bank(kp)[0:16, 0:128], a_, ALU.mult, [B_ps[kp]] + Bs, Bs)
            yield
            k2 = kp
            TRG([(bank_bf(k2)[:, 0:16], yb, identb[0:16, 0:16])], Bs + [B_const], [B_ps[k2]])
            yield
            ACT(mixT[:, c, NT:NTOK], bank_bf(k2)[:, 0:16], AF.Copy, [B_ps[k2]], [B_mix[c]])
            DMA("sp", misc(), [(ncs[:, 0, c * 128:(c + 1) * 128], prev[:, 1, :]),
                               (ncs[:, 1, c * 128:(c + 1) * 128], z_s)], Bs, ())
            yield

        def gen_conv(c, cx):
            sl = (U_CONV + c) % NSLOT
            for t in range(4):
                yield from gen_conv_prompt(c, t, sl, cx)

        for ci in range(2):
            MEMSET(gsm2[ci][:, 63:64], 1.0, [B_gsm2[ci]])
        for b in list(range(16)) + [16]:
            emit_norm(b, 0)
            flush_norm(keep=1)
        flush_norm()
        norm_done = [16]

        def gen_init_norms():
            norm_junk[0] = [NPAGE + 2, NPAGE + 3]
            for b in range(4, 16):
                emit_norm(b, 0)
                flush_norm(keep=1)
                norm_done[0] = b + 1
                yield
            flush_norm()
            norm_junk[0] = [15, 16, 17]
            yield

        hcx = [make_cx(0, list(range(0, 10)), 0, 4, 5, 6, [7, 8, 9], [0, 1, 2, 3]),
               make_cx(1, list(range(10, 20)), 10, 14, 15, 16, [17, 18, 19], [4, 5, 6, 7])]
        for h0 in (0, 2, 4):
            if h0 == 0:
                load_gate.append(B_hT[3])
            need(U_HEAD + h0)
            del load_gate[:]
            gens = [gen_head(h0, hcx[0]), gen_head(h0 + 1, hcx[1])]
            interleave(gens, HEAD_OFFSET)
        sl = need(U_POOL)
        interleave([gen_pool_chunk(0, sl), gen_pool_chunk(1, sl)])
        emit_pool_sample(sl)
        emit_outproj(U_WO, 1)
        emit_mlp(U_FF0, 2)
        ccx = [make_cx(0, [0, 1, 2, 3, 4], 0, 0, 0, 0, [12, 13, 14], [0, 1, 2, 3]),
               make_cx(1, [6, 7, 8, 9, 10], 0, 0, 0, 0, [15, 16, 17], [4, 5, 6, 7])]
        pend = []
        for c0 in (0, 2, 4, 6):
            need(U_CONV + c0)
            interleave([gen_conv(c0, ccx[0]), gen_conv(c0 + 1, ccx[1])] + pend)
            conv_sample_part1(c0, ccx[0])
            conv_sample_part1(c0 + 1, ccx[1])
            pend = [gen_conv_sample_part2(c0, ccx[0]), gen_conv_sample_part2(c0 + 1, ccx[1])]
        interleave(pend)
        emit_outproj(U_OO, 3)
        DMA("sp", misc(), [(pgf(10, 2), gfin_d.partition_broadcast(128))], (), pgb(10, 2))
        emit_mlp(U_FF1, 4)

        final_waits = [(d.h, d.count) for d in P.dsems if d.count > 0]

        def make_body(name, final=False):
            def body(eng):
                for waits, fn, inc in P.streams[name]:
                    embed = None
                    if EMBED_WAITS and inc is True and waits:
                        embed = waits[-1]
                        waits = waits[:-1]
                    for sem, val in waits:
                        eng.wait_ge(sem, val)
                    ins = fn(eng)
                    if embed is not None:
                        ins._wait_ge(embed[0], embed[1])
                    if inc:
                        ins.then_inc(P.esem[name], 1)
                if final:
                    for h_, v_ in final_waits:
                        eng.wait_ge(h_, v_)
            return body

        block.sync(make_body("sp", final=True))
        block.scalar(make_body("act"))
        block.vector(make_body("dve"))
        block.gpsimd(make_body("pool"))
        block.tensor(make_body("pe"))
    return nc


def _unit_k1024(W, cols):
    sub = W[:, cols]
    n = sub.shape[1]
    u = sub.reshape(8, 128, n).transpose(1, 0, 2).reshape(128, 8 * n)
    out = np.zeros((128, 4096), np.float32)
    out[:, :8 * n] = u
    return out


def _unit_w2(W2, s):
    sub = W2[s * 512:(s + 1) * 512, :]
    return np.ascontiguousarray(sub.reshape(4, 128, 1024).transpose(1, 0, 2).reshape(128, 4096))


def _build_wstream(even_w_in, even_w_out, odd_w_in, odd_w_out, ff_w1, ff_w2):
    wst = np.zeros((NU, 128, 4096), np.float32)
    win = even_w_in[0]
    for h in range(6):
        cols = np.concatenate([256 + r * 768 + h * 128 + np.arange(128) for r in range(4)])
        wst[U_HEAD + h] = _unit_k1024(win, cols)
    wst[U_POOL] = _unit_k1024(win, np.arange(256))
    for half in range(2):
        wst[U_WO + half] = _unit_k1024(even_w_out[0], half * 512 + np.arange(512))
        wst[U_OO + half] = _unit_k1024(odd_w_out[0], half * 512 + np.arange(512))
    for l, u0 in ((0, U_FF0), (1, U_FF1)):
        for s in range(8):
            wst[u0 + 2 * s] = _unit_k1024(ff_w1[l], s * 512 + np.arange(512))
            wst[u0 + 2 * s + 1] = _unit_w2(ff_w2[l], s)
    for c in range(8):
        cols = np.concatenate([r * 1024 + c * 128 + np.arange(128) for r in range(3)])
        wst[U_CONV + c] = _unit_k1024(odd_w_in[0], cols)
    return wst


_NC_CACHE = {}


def kernel(x_prompt, x_sample, state_pool, state_hgrn, state_conv, norm_mix, norm_mlp, norm_final,
           even_w_in, pool_w, pool_scale, hgrn_lb_logits, hgrn_gain, even_w_out, odd_w_in, conv_w,
           odd_w_out, ff_w1, ff_w2):
    f = lambda a: np.ascontiguousarray(np.asarray(a, dtype=np.float32))
    x_prompt, x_sample, state_pool, state_hgrn, state_conv = map(f, (x_prompt, x_sample, state_pool, state_hgrn, state_conv))
    norm_mix, norm_mlp, norm_final = f(norm_mix), f(norm_mlp), f(norm_final)
    even_w_in, pool_w, pool_scale, hgrn_lb_logits, hgrn_gain = map(f, (even_w_in, pool_w, pool_scale, hgrn_lb_logits, hgrn_gain))
    even_w_out, odd_w_in, conv_w, odd_w_out, ff_w1, ff_w2 = map(f, (even_w_out, odd_w_in, conv_w, odd_w_out, ff_w1, ff_w2))

    if "nc" not in _NC_CACHE:
        _NC_CACHE["nc"] = build()
    nc = _NC_CACHE["nc"]

    wst = _build_wstream(even_w_in, even_w_out, odd_w_in, odd_w_out, ff_w1, ff_w2)
    norms = np.stack([norm_mix[0], norm_mlp[0], norm_mix[1], norm_mlp[1]])
    gcols = np.ascontiguousarray(norms.reshape(4, 8, 128).transpose(2, 0, 1).reshape(128, 32))
    lbl = np.ascontiguousarray(hgrn_lb_logits.reshape(3, 6, 128).transpose(2, 0, 1).reshape(128, 18))
    gainf = np.ascontiguousarray(hgrn_gain[0].reshape(6, 128).T)
    pscalef = np.ascontiguousarray(pool_scale[0].reshape(2, 128).T)
    convwf = np.ascontiguousarray(conv_w[0].reshape(3, 8, 128).transpose(2, 1, 0).reshape(128, 24))
    identf = np.eye(128, dtype=np.float32)
    identb = identf.astype(ml_dtypes.bfloat16)
    maskT = np.triu(np.ones((128, 128), np.float32))
    onesm = np.full((128, 128), 1.0 / 128, np.float32)
    id16 = np.eye(16, dtype=np.float32)
    cfixw = np.zeros((128, 2, 16), np.float32)
    for c in range(2):
        for p in range(128):
            w = 2 ** (2 * c + p // 64 + 1)
            for t in range(16):
                cfixw[p, c, t] = 1.0 / min(w, t + 1)
    cfixw = cfixw.reshape(128, 32)

    shared = dict(wst=wst, gcols=gcols, gfin=norm_final.reshape(1, 1024), lbl=lbl, gainf=gainf, pscalef=pscalef,
                  poolw=pool_w[0], convwf=convwf, convw=conv_w[0], c_identb=identb, c_identf=identf, c_maskT=maskT,
                  c_onesm=onesm, c_id16=id16, c_cfixw=cfixw)
    in_maps = []
    for c in range(8):
        m = dict(shared)
        m["xp"] = x_prompt[c].reshape(16, 128, 1024)
        m["xs"] = x_sample[16 * c:16 * (c + 1), 0, :]
        m["st_pool"] = state_pool[0, 16 * c:16 * (c + 1)].reshape(16, 15 * 256)
        m["st_hgrn"] = state_hgrn[0, 16 * c:16 * (c + 1)]
        m["st_conv"] = state_conv[0, 16 * c:16 * (c + 1)]
        in_maps.append({k: np.ascontiguousarray(v) for k, v in m.items()})

    res = run_bass_kernel_spmd(nc, in_maps, core_ids=list(range(8)))
    R = res.results
    y_prompt = np.stack([R[c]["yp"].reshape(2048, 1024) for c in range(8)]).astype(np.float32)
    y_sample = np.concatenate([R[c]["ys"] for c in range(8)]).reshape(128, 1, 1024).astype(np.float32)
    new_pool_prompt = np.stack([R[c]["npp"] for c in range(8)])[None].astype(np.float32)
    new_hgrn_prompt = np.stack([R[c]["nhp"] for c in range(8)])[None].astype(np.float32)
    new_conv_prompt = np.stack([R[c]["ncp"] for c in range(8)])[None].astype(np.float32)
    new_pool_sample = np.concatenate([R[c]["nps"].reshape(16, 15, 256) for c in range(8)])[None].astype(np.float32)
    new_hgrn_sample = np.concatenate([R[c]["nhs"] for c in range(8)])[None].astype(np.float32)
    new_conv_sample = np.concatenate([R[c]["ncs"] for c in range(8)])[None].astype(np.float32)
    return (y_prompt, y_sample, new_pool_prompt, new_hgrn_prompt, new_conv_prompt,
            new_pool_sample, new_hgrn_sample, new_conv_sample)
```

```python
import numpy as np
import ml_dtypes
from contextlib import ExitStack
import concourse.bass as bass
import concourse.mybir as mybir
from concourse.bass_utils import run_bass_kernel_spmd

F32 = mybir.dt.float32
BF16 = mybir.dt.bfloat16
AF = mybir.ActivationFunctionType
ALU = mybir.AluOpType
AX = mybir.AxisListType

NT = 2048
NSMP = 16
NTOK = NT + NSMP
EPS = 1e-6
NSLOT = 4
NPAGE = 18
SAME_ENGINE_SYNC = True
EMBED_WAITS = True
HEAD_OFFSET = 0
PREFETCH_F = False

U_HEAD = 0
U_POOL = 6
U_WO = 7
U_FF0 = 9
U_CONV = 25
U_OO = 33
U_FF1 = 35
NU = 51


class Buf:
    __slots__ = ("name", "w", "r")

    def __init__(self, name):
        self.name = name
        self.w = None
        self.r = {}


class DSem:
    def __init__(self, h):
        self.h = h
        self.count = 0


class Prog:
    ENG = ("sp", "act", "dve", "pool", "pe")

    def __init__(self):
        self.streams = {e: [] for e in self.ENG}
        self.seq = {e: 0 for e in self.ENG}
        self.waited = {e: {} for e in self.ENG}
        self.esem = {}
        self.dsems = []

    @staticmethod
    def _flat(bufs):
        out = []
        for b in bufs:
            if isinstance(b, (list, tuple)):
                out.extend(Prog._flat(b))
            else:
                out.append(b)
        return out

    def _waits(self, eng, reads, writes, extra=()):
        need = {}
        reads = self._flat(reads)
        writes = self._flat(writes)

        def add(tok):
            if tok is None:
                return
            if tok[0] == "e":
                if tok[1] == eng and (eng == "pe" or not SAME_ENGINE_SYNC):
                    return
                k = ("e", tok[1])
            else:
                k = ("d", id(tok[1]))
            if self.waited[eng].get(k, 0) >= tok[2]:
                return
            if k not in need or need[k][2] < tok[2]:
                need[k] = tok

        for b in reads:
            add(b.w)
        for b in writes:
            add(b.w)
            for t in b.r.values():
                add(t)
        for t in extra:
            add(t)
        out = []
        for k, tok in need.items():
            self.waited[eng][k] = tok[2]
            sem = self.esem[tok[1]] if tok[0] == "e" else tok[1].h
            out.append((sem, tok[2]))
        return out

    @staticmethod
    def _mark(tok, reads, writes):
        reads = Prog._flat(reads)
        writes = Prog._flat(writes)
        k = ("e", tok[1]) if tok[0] == "e" else ("d", id(tok[1]))
        for b in reads:
            b.r[k] = tok
        for b in writes:
            b.w = tok
            b.r = {}

    def op(self, eng, fn, reads=(), writes=(), multi=False):
        waits = self._waits(eng, reads, writes)
        self.seq[eng] += 1
        tok = ("e", eng, self.seq[eng])
        self.streams[eng].append((waits, fn, "multi" if multi else True))
        self._mark(tok, reads, writes)
        return tok

    def dma(self, q, dsem, fns, reads=(), writes=()):
        extra = []
        if dsem.count > 0:
            extra.append(("d", dsem, dsem.count))
        waits = self._waits(q, reads, writes, extra)
        dsem.count += 16 * len(fns)
        tok = ("d", dsem, dsem.count)

        def run(eng, fns=fns, h=dsem.h):
            for f in fns:
                f(eng).then_inc(h, 16)
            return None

        self.streams[q].append((waits, run, False))
        self._mark(tok, reads, writes)
        return tok


def build():
    nc = bass.Bass("TRN2", target_bir_lowering=False)

    def din(name, shape, dt=F32):
        return nc.dram_tensor(name, shape, dt, kind="ExternalInput").ap()

    def dout(name, shape, dt=F32):
        return nc.dram_tensor(name, shape, dt, kind="ExternalOutput").ap()

    xp = din("xp", [16, 128, 1024])
    xs = din("xs", [16, 1024])
    st_pool = din("st_pool", [16, 15 * 256])
    st_hgrn = din("st_hgrn", [16, 6, 128, 128])
    st_conv = din("st_conv", [16, 2, 1024])
    wst = din("wst", [NU, 128, 4096])
    gcols_d = din("gcols", [128, 32])
    gfin_d = din("gfin", [1, 1024])
    lbl_d = din("lbl", [128, 18])
    gain_d = din("gainf", [128, 6])
    pscale_d = din("pscalef", [128, 2])
    poolw_d = din("poolw", [4, 64, 64])
    convwf_d = din("convwf", [128, 24])
    convw_d = din("convw", [3, 1024])
    c_identb = din("c_identb", [128, 128], BF16)
    c_identf = din("c_identf", [128, 128])
    c_maskT = din("c_maskT", [128, 128])
    c_onesm = din("c_onesm", [128, 128])
    c_id16 = din("c_id16", [16, 16])
    c_cfixw = din("c_cfixw", [128, 32])

    yp = dout("yp", [16, 128, 1024])
    ys = dout("ys", [16, 1024])
    npp = dout("npp", [15, 256])
    nhp = dout("nhp", [6, 128, 128])
    ncp = dout("ncp", [2, 1024])
    nps = dout("nps", [16, 15 * 256])
    nhs = dout("nhs", [16, 6, 128, 128])
    ncs = dout("ncs", [16, 2, 1024])

    P = Prog()
    with ExitStack() as es:
        E = es.enter_context

        def sb(name, shape, dt=F32):
            return E(nc.sbuf_tensor(name, shape, dt))

        x_sb = sb("x_sb", [128, 16, 1024])
        xs_sb = sb("xs_sb", [128, 1024])
        hT = sb("hT", [128, 8, NTOK], BF16)
        mixT = sb("mixT", [128, 8, NTOK], BF16)
        ring = sb("ring", [128, NSLOT, 4096], BF16)
        arena = sb("arena", [128, NPAGE * 512])
        identb = sb("identb", [128, 128], BF16)
        identf = sb("identf", [128, 128])
        maskT = sb("maskT", [128, 128])
        onesm = sb("onesm", [128, 128])
        onesb = sb("onesb", [128, 128], BF16)
        wblk = sb("wblk", [128, 2, 128], BF16)
        gcols = sb("gcols_s", [128, 32])
        lbl = sb("lbl_s", [128, 18])
        hcols = sb("hcols", [128, 32])
        pscale = sb("pscale_s", [128, 2])
        cfixw = sb("cfixw_s", [128, 32])
        convwf = sb("convwf_s", [128, 24])
        id16 = sb("id16_s", [128, 16])
        nrm = sb("nrm", [128, 32])
        tmpc = sb("tmpc", [128, 16])
        gsm2 = [sb(f"gsm{i}", [128, 64]) for i in range(2)]
        Sst2 = [sb(f"Sst{i}", [128, 128]) for i in range(2)]
        Spb2 = [sb(f"Spb{i}", [128, 2, 128], BF16) for i in range(2)]
        tSb2 = [sb(f"tSb{i}", [128, 128]) for i in range(2)]
        psT = [E(nc.psum_tensor(f"pp{i}", [128, 1024], F32)) for i in range(4)]

        for e in Prog.ENG:
            P.esem[e] = E(nc.semaphore(f"es_{e}"))

        def new_dsem(name):
            d = DSem(E(nc.semaphore(name)))
            P.dsems.append(d)
            return d

        ring_ds = [new_dsem(f"ds_ring{i}") for i in range(NSLOT)]
        xy_ds = [new_dsem(f"ds_xy{i}") for i in range(17)]
        misc_ds = [new_dsem(f"ds_misc{i}") for i in range(8)]
        misc_ptr = [0]

        def misc():
            d = misc_ds[misc_ptr[0] % len(misc_ds)]
            misc_ptr[0] += 1
            return d

        block = E(nc.Block())

        B_x = [Buf(f"x{b}") for b in range(16)] + [Buf("xs")]
        B_hT = [Buf(f"hT{t}") for t in range(5)]
        B_mix = [Buf(f"mix{c}") for c in range(8)]
        B_mix[0] = [Buf(f"mix0_{i}") for i in range(4)]
        B_mix[1] = [Buf(f"mix1_{i}") for i in range(4)]
        B_ring = [Buf(f"ring{i}") for i in range(NSLOT)]
        B_pg = [[Buf(f"pg{i}lo"), Buf(f"pg{i}hi")] for i in range(NPAGE)]
        B_ps = [Buf(f"ps{i}") for i in range(8)]
        B_const = Buf("const")
        B_wblk = Buf("wblk")
        B_hcols = Buf("hcols")
        B_nrm = [Buf("nrm0"), Buf("nrm1"), Buf("nrm1b")]
        B_tmpc = Buf("tmpc")
        B_dummy = [Buf("dummy0"), Buf("dummy1")]
        B_gsm2 = [Buf("gsm0"), Buf("gsm1")]
        B_S2 = [Buf("S0"), Buf("S1")]
        B_Sp2 = [[Buf("Sp00"), Buf("Sp01")], [Buf("Sp10"), Buf("Sp11")]]
        B_tS2 = [Buf("tS0"), Buf("tS1")]
        B_nrm2 = [Buf("nrm2"), Buf("nrm3")]

        def ACT(out, in_, func, R, W, scale=1.0, bias=0.0, accum=None):
            def f(e):
                if accum is None:
                    return e.activation(out=out, in_=in_, func=func, bias=bias, scale=scale)
                return e.activation(out=out, in_=in_, func=func, bias=bias, scale=scale, accum_out=accum)
            return P.op("act", f, R, W, multi=accum is not None)

        def TT(out, in0, in1, op, R, W, eng="dve"):
            return P.op(eng, lambda e: e.tensor_tensor(out=out, in0=in0, in1=in1, op=op), R, W)

        def TS(out, in0, s1, op0, R, W, s2=None, op1=None, eng="dve"):
            def f(e):
                if op1 is None:
                    return e.tensor_scalar(out=out, in0=in0, scalar1=s1, scalar2=None, op0=op0)
                return e.tensor_scalar(out=out, in0=in0, scalar1=s1, scalar2=s2, op0=op0, op1=op1)
            return P.op(eng, f, R, W)

        def STT(out, in0, scalar, in1, op0, op1, R, W):
            return P.op("dve", lambda e: e.scalar_tensor_tensor(out=out, in0=in0, scalar=scalar, in1=in1,
                                                                 op0=op0, op1=op1), R, W)

        def CP(out, in_, R, W, eng="dve"):
            return P.op(eng, lambda e: e.tensor_copy(out=out, in_=in_), R, W)

        def MEMSET(ap, val, W, eng="dve"):
            return P.op(eng, lambda e: e.memset(ap, val), (), W)

        def MMG(mms, R, W):
            def f(e):
                ins = None
                for (o, l, r, s, t) in mms:
                    ins = e.matmul(o, l, r, start=s, stop=t)
                return ins
            return P.op("pe", f, R, W, multi=True)

        def TRG(trs, R, W):
            def f(e):
                ins = None
                for (o, i, idn) in trs:
                    ins = e.transpose(out=o, in_=i, identity=idn)
                return ins
            return P.op("pe", f, R, W, multi=True)

        def DMA(q, dsem, pairs, R, W, **kw):
            fns = [(lambda e, o=o, i=i: e.dma_start(out=o, in_=i, **kw)) for (o, i) in pairs]
            return P.dma(q, dsem, fns, R, W)

        def bank(k):
            return psT[k // 2][:, (k % 2) * 512:(k % 2 + 1) * 512]

        def bank_bf(k):
            return bank(k).bitcast(BF16)

        def pair(i):
            return psT[i][:, :]

        class Rot:
            def __init__(self, items):
                self.items = items
                self.i = 0

            def next(self):
                v = self.items[self.i % len(self.items)]
                self.i += 1
                return v

        def pgf(p, n=1):
            return arena[:, p * 512:(p + n) * 512]

        def pgb(p, n=1):
            return [B_pg[i] for i in range(p, p + n)]

        unit_cols = {}
        for u in range(NU):
            unit_cols[u] = 4096
        unit_cols[U_POOL] = 2048
        for c in range(8):
            unit_cols[U_CONV + c] = 3072
        next_load = [0]

        def issue_load(u):
            s = u % NSLOT
            ncol = unit_cols[u]
            pairs = []
            c0 = 0
            while c0 < ncol:
                c1 = min(c0 + 2048, ncol)
                pairs.append((ring[:, s, c0:c1], wst[u, :, c0:c1]))
                c0 = c1
            DMA("pool", ring_ds[s], pairs, list(load_gate), [B_ring[s]])

        load_gate = []

        def need(u, la=NSLOT - 1):
            lim = min(u + la, NU - 1)
            while next_load[0] <= lim:
                issue_load(next_load[0])
                next_load[0] += 1
            return u % NSLOT

        def tokslice(t):
            if t < 4:
                return slice(t * 512, (t + 1) * 512)
            return slice(NT, NTOK)

        need(0, 1)
        DMA("sp", misc(), [
            (identb[:, :], c_identb), (identf[:, :], c_identf), (maskT[:, :], c_maskT), (onesm[:, :], c_onesm),
            (gcols[:, :], gcols_d), (lbl[:, :], lbl_d), (hcols[:, 18:24], gain_d), (pscale[:, :], pscale_d),
            (cfixw[:, :], c_cfixw), (convwf[:, :], convwf_d), (id16[0:16, :], c_id16),
        ], (), [B_const, B_hcols])
        for b in range(16):
            DMA("sp", xy_ds[b], [(x_sb[:, b, :], xp[b])], (), [B_x[b]])
        DMA("sp", xy_ds[16], [(xs_sb[0:16, :], xs)], (), [B_x[16]])
        MEMSET(wblk[:, :, :], 0.0, [B_wblk])
        DMA("pool", new_dsem("ds_wblk"), [
            (wblk[(g % 2) * 64:(g % 2) * 64 + 64, g // 2, (g % 2) * 64:(g % 2) * 64 + 64], poolw_d[g]) for g in range(4)
        ], (), [B_wblk])
        ACT(onesb[:, :], onesm[:, :], AF.Copy, [B_const], [B_const])
        ACT(lbl[:, :], lbl[:, :], AF.Exp, [B_const], [B_const])
        TT(tmpc[:, 0:6], lbl[:, 0:6], lbl[:, 6:12], ALU.add, [B_const], [B_tmpc])
        TT(tmpc[:, 0:6], tmpc[:, 0:6], lbl[:, 12:18], ALU.add, [B_const, B_tmpc], [B_tmpc])
        P.op("dve", lambda e: e.reciprocal(out=tmpc[:, 6:12], in_=tmpc[:, 0:6]), [B_tmpc], [B_tmpc])
        TT(hcols[:, 0:6], lbl[:, 0:6], tmpc[:, 6:12], ALU.mult, [B_const, B_tmpc], [B_hcols])
        TS(hcols[:, 6:12], hcols[:, 0:6], -1.0, ALU.mult, [B_hcols], [B_hcols], s2=1.0, op1=ALU.add)
        TS(hcols[:, 12:18], hcols[:, 6:12], -1.0, ALU.mult, [B_hcols], [B_hcols])
        ACT(hcols[:, 24:30], hcols[:, 6:12], AF.Ln, [B_hcols], [B_hcols])
        LB, OML, NOML, GAIN, LNOML = 0, 6, 12, 18, 24

        ps_norm = Rot([0, 1])
        norm_ctr = [0]
        norm_pending = []
        norm_junk = [[15, 16, 17]]

        def emit_norm(b, n):
            nj = len(norm_junk[0])
            k = norm_ctr[0] % nj
            norm_ctr[0] += 1
            np_ = 128 if b < 16 else 16
            xb = x_sb[:, b, :] if b < 16 else xs_sb[0:16, :]
            Bx = B_x[b]
            junk = VPGF(norm_junk[0][k])[0:np_, :].bitcast(BF16)
            Bj = VPGB(norm_junk[0][k])
            ss = nrm[0:np_, 4 * k + 0:4 * k + 1]
            lt = nrm[0:np_, 4 * k + 1:4 * k + 2]
            rs = nrm[0:np_, 4 * k + 2:4 * k + 3]
            Bn = [B_nrm[k]]
            ACT(junk, xb, AF.Square, [Bx], Bj + Bn, accum=ss)
            ACT(lt, ss, AF.Ln, Bn, Bn, scale=1.0 / 1024, bias=EPS)
            ACT(rs, lt, AF.Exp, Bn, Bn, scale=-0.5)
            if n == 4:
                gfin = pgf(10, 2)
                STT(xb, xb, rs, gfin[0:np_, :], ALU.mult, ALU.mult, [Bx] + Bn + pgb(10, 2), [Bx])
                if b < 16:
                    DMA("sp", xy_ds[b], [(yp[b], x_sb[:, b, :])], [Bx], ())
                else:
                    DMA("sp", xy_ds[16], [(ys, xs_sb[0:16, :])], [Bx], ())
                return
            if n == 0:
                TS(junk, xb, rs, ALU.mult, [Bx] + Bn, Bj)
            else:
                ACT(junk, xb, AF.Copy, [Bx] + Bn, Bj, scale=rs)

            def part_b():
                pk = ps_norm.next()
                pb = bank_bf(pk)
                if b < 16:
                    TRG([(pb[:, c * 128:(c + 1) * 128], junk[:, c * 128:(c + 1) * 128], identb[:, :]) for c in range(8)],
                        Bj + [B_const], [B_ps[pk]])
                    t = b // 4
                    TT(hT[:, :, b * 128:(b + 1) * 128], pb[:, 0:1024].rearrange("p (c t) -> p c t", t=128),
                       gcols[:, n * 8:(n + 1) * 8].unsqueeze(2).broadcast_to([128, 8, 128]), ALU.mult,
                       [B_ps[pk], B_const], [B_hT[t]])
                else:
                    TRG([(pb[:, c * 16:(c + 1) * 16], junk[:, c * 128:(c + 1) * 128], identb[0:16, 0:16]) for c in range(8)],
                        Bj + [B_const], [B_ps[pk]])
                    TT(hT[:, :, NT:NTOK], pb[:, 0:128].rearrange("p (c t) -> p c t", t=16),
                       gcols[:, n * 8:(n + 1) * 8].unsqueeze(2).broadcast_to([128, 8, 16]), ALU.mult,
                       [B_ps[pk], B_const], [B_hT[4]])
            norm_pending.append(part_b)

        def flush_norm(keep=0):
            while len(norm_pending) > keep:
                norm_pending.pop(0)()

        ps_pair = Rot([2, 3])

        def emit_outproj(u0, norm_after):
            need(u0)
            s0, s1 = u0 % NSLOT, (u0 + 1) % NSLOT
            for b in list(range(16)) + [16]:
                pi = ps_pair.next()
                pp = pair(pi)
                tk = slice(b * 128, (b + 1) * 128) if b < 16 else slice(NT, NTOK)
                np_ = 128 if b < 16 else 16
                mms = []
                for half, s in ((0, s0), (1, s1)):
                    for kc in range(8):
                        mms.append((pp[0:np_, half * 512:(half + 1) * 512], mixT[:, kc, tk],
                                    ring[:, s, kc * 512:(kc + 1) * 512], kc == 0, kc == 7))
                MMG(mms, B_mix + [B_ring[s0], B_ring[s1]], [B_ps[2 * pi], B_ps[2 * pi + 1]])
                xb = x_sb[:, b, :] if b < 16 else xs_sb[0:16, :]
                TT(xb, xb, pp[0:np_, :], ALU.add, [B_x[b], B_ps[2 * pi], B_ps[2 * pi + 1]], [B_x[b]])
                emit_norm(b, norm_after)
                flush_norm(keep=2)
            flush_norm()

        def emit_mlp(u0, norm_after):
            ps_single = Rot([0, 1, 2, 3])
            ps_pr = Rot([2, 3])
            steps = [(s, t) for s in range(8) for t in range(5)]
            hid = [mixT[:, 0, 0:2048].rearrange("p (c t) -> p c t", t=512),
                   mixT[:, 1, 0:2048].rearrange("p (c t) -> p c t", t=512)]
            hid_s = [mixT[:, 2, 0:64].rearrange("p (c t) -> p c t", t=16),
                     mixT[:, 3, 0:64].rearrange("p (c t) -> p c t", t=16)]
            B_hid = [B_mix[0], B_mix[1]]
            B_hidc = [[Buf(f"hid{i}_{n}") for n in range(4)] for i in range(2)]
            B_hids = [B_mix[2], B_mix[3]]

            def ff1_items(k):
                s, t = steps[k]
                items = []
                if t == 0:
                    items.append(lambda s=s: need(u0 + 2 * s, NSLOT - 2))
                if t == 2:
                    items.append(lambda s=s: need(u0 + 2 * s + 1, NSLOT - 2))
                sl = (u0 + 2 * s) % NSLOT
                if t < 4:
                    hb = hid[k % 2]
                    for n in range(4):
                        def it(n=n, hb=hb, sl=sl, t=t, k=k):
                            pk = ps_single.next()
                            MMG([(bank(pk), ring[:, sl, kc * 512 + n * 128:kc * 512 + (n + 1) * 128],
                                  hT[:, kc, tokslice(t)], kc == 0, kc == 7) for kc in range(8)],
                                [B_ring[sl], B_hT[t]], [B_ps[pk]])
                            Wh = [B_hidc[k % 2][n]] + ([B_hid[k % 2]] if k < 2 else [])
                            ACT(hb[:, n, :], bank(pk), AF.Relu, [B_ps[pk]], Wh)
                            TT(hb[:, n, :], hb[:, n, :], hb[:, n, :], ALU.mult, [B_hidc[k % 2][n]], [B_hidc[k % 2][n]])
                        items.append(it)
                else:
                    hb = hid_s[s % 2]

                    def it(hb=hb, sl=sl, s=s):
                        pk = ps_single.next()
                        mms = []
                        for n in range(4):
                            for kc in range(8):
                                mms.append((bank(pk)[:, n * 16:(n + 1) * 16],
                                            ring[:, sl, kc * 512 + n * 128:kc * 512 + (n + 1) * 128],
                                            hT[:, kc, NT:NTOK], kc == 0, kc == 7))
                        MMG(mms, [B_ring[sl], B_hT[4]], [B_ps[pk]])
                        ACT(hb, bank(pk)[:, 0:64].rearrange("p (c t) -> p c t", t=16), AF.Relu, [B_ps[pk]], [B_hids[s % 2]])
                        TT(hb, hb, hb, ALU.mult, [B_hids[s % 2]], [B_hids[s % 2]])
                    items.append(it)
                return items

            def ff2_items(k):
                s, t = steps[k]
                sl2 = (u0 + 2 * s + 1) % NSLOT
                items = []
                if t < 4:
                    hb = hid[k % 2]
                    for j in range(4):
                        def it(j=j, hb=hb, sl2=sl2, t=t, s=s, k=k):
                            b = t * 4 + j
                            pi = ps_pr.next()
                            pp = pair(pi)
                            mms = []
                            for half in range(2):
                                for kc in range(4):
                                    mms.append((pp[:, half * 512:(half + 1) * 512], hb[:, kc, j * 128:(j + 1) * 128],
                                                ring[:, sl2, kc * 1024 + half * 512:kc * 1024 + (half + 1) * 512],
                                                kc == 0, kc == 3))
                            MMG(mms, [B_ring[sl2], B_hid[k % 2]] + B_hidc[k % 2], [B_ps[2 * pi], B_ps[2 * pi + 1]])
                            TT(x_sb[:, b, :], x_sb[:, b, :], pp, ALU.add, [B_x[b], B_ps[2 * pi], B_ps[2 * pi + 1]], [B_x[b]])
                            if s == 7:
                                emit_norm(b, norm_after)
                                flush_norm(keep=2)
                        items.append(it)
                else:
                    hb = hid_s[s % 2]

                    def it(hb=hb, sl2=sl2, s=s):
                        pi = ps_pr.next()
                        pp = pair(pi)
                        mms = []
                        for half in range(2):
                            for kc in range(4):
                                mms.append((pp[0:16, half * 512:(half + 1) * 512], hb[:, kc, :],
                                            ring[:, sl2, kc * 1024 + half * 512:kc * 1024 + (half + 1) * 512],
                                            kc == 0, kc == 3))
                        MMG(mms, [B_ring[sl2], B_hids[s % 2]], [B_ps[2 * pi], B_ps[2 * pi + 1]])
                        TT(xs_sb[0:16, :], xs_sb[0:16, :], pp[0:16, :], ALU.add,
                           [B_x[16], B_ps[2 * pi], B_ps[2 * pi + 1]], [B_x[16]])
                        if s == 7:
                            emit_norm(16, norm_after)
                            flush_norm()
                    items.append(it)
                return items

            for it in ff1_items(0):
                it()
            for k in range(len(steps)):
                a = ff1_items(k + 1) if k + 1 < len(steps) else []
                bb = ff2_items(k)
                for i in range(max(len(a), len(bb))):
                    if i < len(a):
                        a[i]()
                    if i < len(bb):
                        bb[i]()

        def VPGF(i):
            if i < NPAGE:
                return arena[:, i * 512:(i + 1) * 512]
            j = i - NPAGE
            return mixT[:, j // 2, (j % 2) * 1024:(j % 2 + 1) * 1024].bitcast(F32)

        def VPGB(i):
            if i < NPAGE:
                return list(B_pg[i])
            j = i - NPAGE
            return list(B_mix[j // 2][2 * (j % 2):2 * (j % 2) + 2])

        class Small:
            def __init__(self, pages):
                self.pages = pages

            def slot(self, i, np_=16, w=128, n=1):
                assert (i % 4) + n <= 4
                return VPGF(self.pages[i // 4])[0:np_, (i % 4) * 128:(i % 4) * 128 + (n - 1) * 128 + w]

            def bufs(self):
                out = []
                for p in self.pages:
                    out += VPGB(p)
                return out

        class Cx:
            pass

        def make_cx(ci, prompt_pages, sh_base, vm, sbf, msk, small_pages, banks):
            cx = Cx()
            cx.ci = ci
            cx.pp = prompt_pages
            cx.sh_base, cx.vm, cx.sbf, cx.msk = sh_base, vm, sbf, msk
            cx.small = Small(small_pages)
            cx.ps = Rot(banks)
            cx.banks = banks
            cx.gsm = gsm2[ci]
            cx.B_gsm = B_gsm2[ci]
            cx.B_gsf = [Buf(f"gs{ci}_{n_}") for n_ in ("GS", "DM", "DBM", "DB", "eD")]
            cx.Sst, cx.Spb, cx.tSb = Sst2[ci], Spb2[ci], tSb2[ci]
            cx.B_S, cx.B_Sp, cx.B_tS = B_S2[ci], B_Sp2[ci], B_tS2[ci]
            cx.nrmc = 16 + 4 * ci
            cx.B_nrm = B_nrm2[ci]
            return cx

        def gen_head_prompt(h, t, sl, cx):
            assert norm_done[0] >= 4 * (t + 1) and (t == 0 or not norm_pending or norm_done[0] > 4 * (t + 1)), (t, norm_done[0])
            T0 = t * 512
            tk = slice(T0, T0 + 512)
            ps = cx.ps
            gs = cx.gsm
            Bg = [cx.B_gsm]
            GSb, DMb, DBMb, DBb, eDb = ([b_] for b_ in cx.B_gsf)

            def W(kc, a, b):
                return ring[:, sl, kc * 512 + a:kc * 512 + b]

            Rw = [B_ring[sl], B_hT[t]]
            pg = cx.pp
            sn, si, lf, G, Ei, osq, rstd = (VPGF(pg[i]) for i in range(7))
            Bsn, Bsi, Blf, BG, BEi, Bosq, Brstd = (VPGB(pg[i]) for i in range(7))
            Ee, rel = si, lf
            p7 = VPGF(pg[7]).bitcast(BF16)
            p8 = VPGF(pg[8]).bitcast(BF16)
            p9 = VPGF(pg[9]).bitcast(BF16)
            B7, B8, B9 = VPGB(pg[7]), VPGB(pg[8]), VPGB(pg[9])
            B7a, B7b, B8a, B8b, B9a, B9b = [B7[0]], [B7[1]], [B8[0]], [B8[1]], [B9[0]], [B9[1]]
            vtok, qd = p7[:, 0:512], p7[:, 512:1024]
            kd, kt = p8[:, 0:512], p8[:, 512:1024]
            sm, sg = p9[:, 0:512], p9[:, 512:1024]
            if t == 0 or not PREFETCH_F:
                cx.kf_next = ps.next()
                MMG([(bank(cx.kf_next), W(kc, 128, 256), hT[:, kc, tk], kc == 0, kc == 7) for kc in range(8)], Rw,
                    [B_ps[cx.kf_next]])
            kf = cx.kf_next
            yield
            ef, L2 = sn, Ei
            ACT(ef, bank(kf), AF.Exp, [B_ps[kf]], Bsn)
            yield
            kq = ps.next()
            MMG([(bank(kq), W(kc, 0, 128), hT[:, kc, tk], kc == 0, kc == 7) for kc in range(8)], Rw, [B_ps[kq]])
            yield
            ki = ps.next()
            MMG([(bank(ki)[:, j * 128:(j + 1) * 128], hT[:, kc, T0 + j * 128:T0 + (j + 1) * 128], W(kc, 256, 384),
                  kc == 0, kc == 7) for j in range(4) for kc in range(8)], Rw, [B_ps[ki]])
            yield
            kg = ps.next()
            MMG([(bank(kg), W(kc, 384, 512), hT[:, kc, tk], kc == 0, kc == 7) for kc in range(8)], Rw, [B_ps[kg]])
            yield
            ACT(L2, ef, AF.Ln, Bsn, BEi, bias=1.0)
            ACT(lf, ef, AF.Ln, Bsn + [B_hcols], Blf, bias=hcols[:, LB + h:LB + h + 1])
            yield
            if t == 0:
                MEMSET(cx.Sst[:, :], 0.0, [cx.B_S])
                P.op("dve", lambda e: e.tensor_tensor_scan(out=G, data0=lf, data1=L2, initial=0.0,
                                                           op0=ALU.add, op1=ALU.subtract), Blf + BEi, BG)
                MEMSET(gs[:, 16:17], 0.0, GSb)
            else:
                CP(gs[:, 16:17], gs[:, 20:21], GSb, GSb)
                P.op("dve", lambda e: e.tensor_tensor_scan(out=G, data0=lf, data1=L2, initial=gs[:, 16:17],
                                                           op0=ALU.add, op1=ALU.subtract), Blf + BEi + GSb, BG)
            yield
            G3 = G.rearrange("p (j t) -> p j t", t=128)
            TT(rel.rearrange("p (j t) -> p j t", t=128), G3, G3[:, :, 63:64].broadcast_to([128, 4, 128]), ALU.subtract,
               BG, Blf)
            yield
            ACT(Ee, rel, AF.Exp, Blf, Bsi)
            TT(L2, rel, L2, ALU.add, Blf + BEi, BEi)
            yield
            TT(qd, bank(kq), Ee, ALU.mult, [B_ps[kq]] + Bsi, B7b)
            ACT(kd, L2, AF.Exp, BEi + [B_hcols], B8a, scale=-1.0, bias=hcols[:, LNOML + h:LNOML + h + 1])
            yield
            CP(gs[:, 17:21], G3[:, :, 127], BG, GSb)
            TT(gs[:, 24:28], G3[:, :, 63], gs[:, 16:20], ALU.subtract, BG + GSb, DMb)
            TT(gs[:, 28:32], gs[:, 17:21], G3[:, :, 63], ALU.subtract, BG + GSb, DBMb)
            TT(gs[:, 32:36], gs[:, 17:21], gs[:, 16:20], ALU.subtract, GSb, DBb)
            ACT(gs[:, 40:52], gs[:, 24:36], AF.Exp, DMb + DBMb + DBb, eDb)
            eM, eBM, eB = 40, 44, 48
            yield
            ACT(si, bank(ki), AF.Sigmoid, [B_ps[ki]], Bsi)
            ACT(sg, bank(kg), AF.Sigmoid, [B_ps[kg]], B9b)
            if cx.ci == 1:
                ACT(tmpc[:, 12 + cx.ci:13 + cx.ci], hcols[:, 0:1], AF.Exp, [B_hcols], [B_dummy[cx.ci]])
            yield
            TT(vtok, bank(ki), si, ALU.mult, [B_ps[ki]] + Bsi, B7a)
            yield
            kt_ps = ps.next()
            TRG([(bank_bf(kt_ps)[:, j * 128:(j + 1) * 128], kd[:, j * 128:(j + 1) * 128], identb[:, :]) for j in range(4)],
                B8a + [B_const], [B_ps[kt_ps]])
            ks = ps.next()
            MMG([(bank(ks)[:, j * 128:(j + 1) * 128], kd[:, j * 128:(j + 1) * 128], qd[:, j * 128:(j + 1) * 128], True, True)
                 for j in range(4)], B7b + B8a, [B_ps[ks]])
            yield
            ACT(kt, bank_bf(kt_ps)[:, 0:512], AF.Copy, [B_ps[kt_ps]], B8b)
            TT(sm.rearrange("p (j t) -> p j t", t=128), bank(ks).rearrange("p (j t) -> p j t", t=128),
               maskT[:, :].unsqueeze(1).broadcast_to([128, 4, 128]), ALU.mult, [B_ps[ks], B_const], B9a)
            yield
            kdS = ps.next()
            MMG([(bank(kdS)[:, j * 128:(j + 1) * 128], kt[:, j * 128:(j + 1) * 128], vtok[:, j * 128:(j + 1) * 128], True, True)
                 for j in range(4)], B7a + B8b, [B_ps[kdS]])
            yield
            if PREFETCH_F and t < 3:
                cx.kf_next = ps.next()
                tk2 = slice(T0 + 512, T0 + 1024)
                MMG([(bank(cx.kf_next), W(kc, 128, 256), hT[:, kc, tk2], kc == 0, kc == 7) for kc in range(8)],
                    [B_ring[sl], B_hT[t + 1]], [B_ps[cx.kf_next]])
            tSa = lf.rearrange("p (j v) -> p j v", v=128)
            TT(tSa, bank(kdS).rearrange("p (j v) -> p j v", v=128),
               gs[:, eBM:eBM + 4].unsqueeze(2).broadcast_to([128, 4, 128]), ALU.mult, [B_ps[kdS]] + eDb, Blf)
            yield
            ko = ps.next()
            for j in range(4):
                sp_i = j % 2
                ACT(cx.Spb[:, sp_i, :], cx.Sst[:, :], AF.Copy, [cx.B_S] + eDb, [cx.B_Sp[sp_i]], scale=gs[:, eM + j:eM + j + 1])
                MMG([(bank(ko)[:, j * 128:(j + 1) * 128], cx.Spb[:, sp_i, :], qd[:, j * 128:(j + 1) * 128], True, False),
                     (bank(ko)[:, j * 128:(j + 1) * 128], vtok[:, j * 128:(j + 1) * 128], sm[:, j * 128:(j + 1) * 128], False, True)],
                    [cx.B_Sp[sp_i]] + B7 + B9a, [B_ps[ko]])
                STT(cx.Sst[:, :], cx.Sst[:, :], gs[:, eB + j:eB + j + 1], tSa[:, j, :], ALU.mult, ALU.add,
                    [cx.B_S] + Blf + eDb, [cx.B_S])
                yield
            osqb = osq.bitcast(BF16)[:, 0:512]
            ACT(osqb, bank(ko), AF.Square, [B_ps[ko]], Bosq)
            yield
            km = ps.next()
            MMG([(bank(km), onesb[:, :], osqb, True, True)], Bosq + [B_const], [B_ps[km]])
            yield
            ACT(rstd, bank(km), AF.Ln, [B_ps[km]], Brstd, bias=EPS)
            ACT(rstd, rstd, AF.Exp, Brstd, Brstd, scale=-0.5)
            yield
            TT(osq, bank(ko), rstd, ALU.mult, [B_ps[ko]] + Brstd, Bosq)
            STT(mixT[:, 2 + h, tk], osq, hcols[:, GAIN + h:GAIN + h + 1], sg, ALU.mult, ALU.mult,
                Bosq + B9b + [B_hcols], [B_mix[2 + h]])
            if t == 3:
                DMA("sp", misc(), [(nhp[h], cx.Sst[:, :])], [cx.B_S], ())
            yield

        def gen_head_sample(h, sl, cx):
            ps = cx.ps
            Rw = [B_ring[sl], B_hT[4]]
            Sh = arena[:, cx.sh_base * 512:(cx.sh_base + 4) * 512].rearrange("p (b v) -> p b v", v=128)
            BSh = pgb(cx.sh_base, 4)
            DMA("sp", misc(), [(Sh[:, 4 * g:4 * g + 4, :], st_hgrn[4 * g:4 * g + 4, h].rearrange("b k v -> k b v"))
                               for g in range(4)], (), BSh)
            kp = ps.next()
            MMG([(bank(kp)[0:16, :], hT[:, kc, NT:NTOK], ring[:, sl, kc * 512:(kc + 1) * 512], kc == 0, kc == 7)
                 for kc in range(8)], Rw, [B_ps[kp]])
            yield
            S_ = cx.small
            Bs = S_.bufs()
            qfig = S_.slot(0, n=4)
            si_s, vv, kk = (S_.slot(i) for i in (4, 5, 6))
            wide = S_.slot(9, np_=128, w=128)
            rs_s, t1_s = wide[:, 0:16], wide[:, 16:32]
            osq_s = wide[:, 32:48].bitcast(BF16)[:, 0:16]
            qT = S_.slot(10, np_=128, w=16).bitcast(BF16)[:, 0:16]
            fm = S_.slot(11, np_=128, w=64)
            snT, kkT, fT, sgT = fm[:, 0:16], fm[:, 16:32], fm[:, 32:48], fm[:, 48:64]
            ACT(si_s, bank(kp)[0:16, 256:384], AF.Sigmoid, [B_ps[kp]], Bs)
            ACT(qfig, bank(kp)[0:16, 0:512], AF.Copy, [B_ps[kp]], Bs)
            yield
            TT(vv, bank(kp)[0:16, 256:384], si_s, ALU.mult, [B_ps[kp]] + Bs, Bs)
            k2 = ps.next()
            TRG([(bank(k2)[:, 0:16], qfig[:, 0:128], identf[0:16, 0:16]),
                 (bank(k2)[:, 16:32], qfig[:, 128:256], identf[0:16, 0:16]),
                 (bank(k2)[:, 32:48], qfig[:, 384:512], identf[0:16, 0:16])], Bs + [B_const], [B_ps[k2]])
            yield
            ACT(snT, bank(k2)[:, 16:32], AF.Sigmoid, [B_ps[k2]], Bs, scale=-1.0)
            ACT(sgT, bank(k2)[:, 32:48], AF.Sigmoid, [B_ps[k2]], Bs)
            ACT(qT, bank(k2)[:, 0:16], AF.Copy, [B_ps[k2]], Bs)
            yield
            TS(kkT, snT, hcols[:, OML + h:OML + h + 1], ALU.mult, Bs + [B_hcols], Bs)
            TS(fT, kkT, -1.0, ALU.mult, Bs, Bs, s2=1.0, op1=ALU.add)
            k3 = ps.next()
            TRG([(bank(k3)[0:16, 0:128], kkT, identf[:, :])], Bs + [B_const], [B_ps[k3]])
            yield
            kkb = kk.bitcast(BF16)[:, 0:128]
            ACT(kkb, bank(k3)[0:16, 0:128], AF.Copy, [B_ps[k3]], Bs)
            kk = kkb
            yield
            Vmf = VPGF(cx.vm)[0:16, :].bitcast(BF16)
            BVm = VPGB(cx.vm)
            for g in range(4):
                Vm = Vmf[:, (g % 2) * 512:(g % 2 + 1) * 512].rearrange("p (b v) -> p b v", v=128)
                TT(Vm, vv.unsqueeze(1).broadcast_to([16, 4, 128]),
                   id16[0:16, 4 * g:4 * g + 4].unsqueeze(2).broadcast_to([16, 4, 128]), ALU.mult, Bs + [B_const], BVm)
                kk_ps = ps.next()
                MMG([(bank(kk_ps), kk, Vm, True, True)], Bs + BVm, [B_ps[kk_ps]])
                TT(Sh[:, 4 * g:4 * g + 4, :], Sh[:, 4 * g:4 * g + 4, :],
                   fT[:, 4 * g:4 * g + 4].unsqueeze(2).broadcast_to([128, 4, 128]), ALU.mult, BSh + Bs, BSh)
                yield
                TT(Sh[:, 4 * g:4 * g + 4, :], Sh[:, 4 * g:4 * g + 4, :], bank(kk_ps).rearrange("p (b v) -> p b v", v=128),
                   ALU.add, BSh + [B_ps[kk_ps]], BSh)
                yield
            DMA("sp", misc(), [(nhs[4 * g:4 * g + 4, h].rearrange("b k v -> k b v"), Sh[:, 4 * g:4 * g + 4, :])
                               for g in range(4)], BSh, ())
            Sbf = VPGF(cx.sbf).bitcast(BF16)
            BSbf = VPGB(cx.sbf)
            ko_ = ps.next()
            oT = bank(ko_)[:, 0:16]
            for g in range(4):
                sb_g = Sbf[:, (g % 2) * 512:(g % 2 + 1) * 512].rearrange("p (b v) -> p b v", v=128)
                ACT(sb_g, Sh[:, 4 * g:4 * g + 4, :], AF.Copy, BSh, BSbf)
                yield
                MMG([(bank(ko_)[:, 4 * g + bb:4 * g + bb + 1], sb_g[:, bb, :], qT[:, 4 * g + bb:4 * g + bb + 1], True, True)
                     for bb in range(4)], Bs + BSbf, [B_ps[ko_]])
                yield
            ACT(osq_s, oT, AF.Square, [B_ps[ko_]], Bs)
            yield
            km_ = ps.next()
            MMG([(bank(km_)[:, 0:16], onesb[:, :], osq_s, True, True)], Bs + [B_const], [B_ps[km_]])
            yield
            ACT(rs_s, bank(km_)[:, 0:16], AF.Ln, [B_ps[km_]], Bs, bias=EPS)
            ACT(rs_s, rs_s, AF.Exp, Bs, Bs, scale=-0.5)
            yield
            TT(t1_s, oT, rs_s, ALU.mult, [B_ps[ko_]] + Bs, Bs)
            STT(mixT[:, 2 + h, NT:NTOK], t1_s, hcols[:, GAIN + h:GAIN + h + 1], sgT, ALU.mult, ALU.mult,
                Bs + [B_hcols], [B_mix[2 + h]])
            yield

        def gen_head(h, cx):
            sl = (U_HEAD + h) % NSLOT
            for t in range(4):
                yield from gen_head_prompt(h, t, sl, cx)
            yield from gen_head_sample(h, sl, cx)

        def interleave(gens, offset=0):
            gens = list(gens)
            for _ in range(offset):
                try:
                    next(gens[0])
                except StopIteration:
                    break
            while gens:
                for g in list(gens):
                    try:
                        next(g)
                    except StopIteration:
                        gens.remove(g)

        ps_all = Rot(list(range(8)))

        def gen_pool_chunk(c, sl):
            pm = ({"ue": 0, "A": 4, "B": 6, "C": 6, "d": 10, "fx": 16}, {"ue": 2, "A": 8, "B": 11, "C": 13, "d": 15, "fx": 17})[c]
            ue = pgf(pm["ue"], 2)[:, 0:528]
            Bu = pgb(pm["ue"], 2)
            A_ = pgf(pm["A"], 2)[:, 0:528]
            B_ = pgf(pm["B"], 2)[:, 0:528]
            C_ = pgf(pm["C"], 2)[:, 0:528]
            BA, BB, BC = pgb(pm["A"], 2), pgb(pm["B"], 2), pgb(pm["C"], 2)
            dbf = pgf(pm["d"]).bitcast(BF16)[:, 0:512]
            Bd = pgb(pm["d"])
            fx = pgf(pm["fx"])[:, 0:16]
            Bfx = pgb(pm["fx"])
            def proj(tt):
                k_ = ps_all.next()
                tk_ = slice(tt * 512, (tt + 1) * 512)
                MMG([(bank(k_), ring[:, sl, kc * 256 + c * 128:kc * 256 + (c + 1) * 128], hT[:, kc, tk_], kc == 0, kc == 7)
                     for kc in range(8)], [B_ring[sl], B_hT[tt]], [B_ps[k_]])
                return k_

            ku_next = proj(0)
            yield
            for t in range(4):
                T0 = t * 512
                tk = slice(T0, T0 + 512)
                ku = ku_next
                if t == 0:
                    MEMSET(ue[:, 0:16], 0.0, Bu)
                ACT(ue[:, 16:528], bank(ku), AF.Copy, [B_ps[ku]], Bu)
                if t < 3:
                    ku_next = proj(t + 1)
                yield
                TT(A_[:, 2:528], ue[:, 2:528], ue[:, 1:527], ALU.add, Bu, BA)
                yield
                if c == 0:
                    TT(B_[64:128, 4:528], A_[64:128, 4:528], A_[64:128, 2:526], ALU.add, BA, BB)
                    STT(dbf[0:64, :], A_[0:64, 16:528], 0.5, ue[0:64, 16:528], ALU.mult, ALU.subtract, BA + Bu, Bd)
                    yield
                    STT(dbf[64:128, :], B_[64:128, 16:528], 0.25, ue[64:128, 16:528], ALU.mult, ALU.subtract, BB + Bu, Bd)
                    sel_lo, sel_hi = A_, B_
                    yield
                    yield
                else:
                    TT(B_[:, 4:528], A_[:, 4:528], A_[:, 2:526], ALU.add, BA, BB)
                    yield
                    TT(C_[:, 8:528], B_[:, 8:528], B_[:, 4:524], ALU.add, BB, BC)
                    yield
                    TT(A_[64:128, 16:528], C_[64:128, 16:528], C_[64:128, 8:520], ALU.add, BC + BA, BA)
                    STT(dbf[0:64, :], C_[0:64, 16:528], 0.125, ue[0:64, 16:528], ALU.mult, ALU.subtract, BC + Bu, Bd)
                    yield
                    STT(dbf[64:128, :], A_[64:128, 16:528], 0.0625, ue[64:128, 16:528], ALU.mult, ALU.subtract, BA + Bu, Bd)
                    sel_lo, sel_hi = C_, A_
                if t == 0:
                    for (lo, hi, sel) in ((0, 64, sel_lo), (64, 128, sel_hi)):
                        TT(fx[lo:hi, :], sel[lo:hi, 16:32], cfixw[lo:hi, c * 16:(c + 1) * 16], ALU.mult,
                           [B_const] + BA + BB + BC, Bfx)
                        TT(dbf[lo:hi, 0:16], fx[lo:hi, :], ue[lo:hi, 16:32], ALU.subtract, Bfx + Bu, Bd)
                yield
                kp_ = ps_all.next()
                MMG([(bank(kp_), wblk[:, c, :], dbf, True, True)], Bd + [B_wblk], [B_ps[kp_]])
                yield
                ACT(mixT[:, c, tk], bank(kp_), AF.Copy, [B_ps[kp_], B_const], [B_mix[c]], scale=pscale[:, c:c + 1])
                if t < 3:
                    CP(ue[:, 0:16], ue[:, 512:528], Bu, Bu)
                else:
                    kx = ps_all.next()
                    TRG([(bank(kx)[0:15, 0:128], ue[:, 513:528], identf[:, :])], Bu + [B_const], [B_ps[kx]])
                    stg = pgf(pm["fx"])[0:15, 0:128]
                    ACT(stg, bank(kx)[0:15, 0:128], AF.Copy, [B_ps[kx]], Bfx)
                    DMA("sp", misc(), [(npp[:, c * 128:(c + 1) * 128], stg)], Bfx, ())
                yield

        def emit_pool_sample(sl):
            Rw = [B_ring[sl], B_hT[4]]
            stp = pgf(0, 8)[0:16, 0:3840]
            Bst = pgb(0, 8)
            DMA("sp", misc(), [(stp, st_pool)], (), Bst)
            stp3 = stp.rearrange("p (r c) -> p r c", c=256)
            ku = ps_all.next()
            MMG([(bank(ku)[0:16, 0:256], hT[:, kc, NT:NTOK], ring[:, sl, kc * 256:(kc + 1) * 256], kc == 0, kc == 7)
                 for kc in range(8)], Rw, [B_ps[ku]])
            sml = pgf(11, 3)
            Bs = pgb(11, 3)
            u_s = sml[0:16, 0:256]
            sums = sml[0:16, 256:512]
            dd = sml[0:16, 512:768].bitcast(BF16)[:, 0:256]
            ACT(u_s, bank(ku)[0:16, 0:256], AF.Copy, [B_ps[ku]], Bs)
            for g, w in enumerate((2, 4, 8, 16)):
                cs = slice(g * 64, (g + 1) * 64)
                if w == 2:
                    TT(sums[:, cs], stp3[:, 14, cs], u_s[:, cs], ALU.add, Bst + Bs, Bs)
                else:
                    P.op("dve", lambda e, cs=cs, w=w: e.tensor_reduce(
                        out=sums[:, cs], in_=stp3[:, 16 - w:15, cs].rearrange("p r c -> p c r"), axis=AX.X, op=ALU.add),
                        Bst, Bs)
                    TT(sums[:, cs], sums[:, cs], u_s[:, cs], ALU.add, Bs, Bs)
                STT(dd[:, cs], sums[:, cs], 1.0 / w, u_s[:, cs], ALU.mult, ALU.subtract, Bs, Bs)
            k2 = ps_all.next()
            TRG([(bank_bf(k2)[:, c * 16:(c + 1) * 16], dd[:, c * 128:(c + 1) * 128], identb[0:16, 0:16]) for c in range(2)],
                Bs + [B_const], [B_ps[k2]])
            dT = pgf(10).bitcast(BF16)[:, 0:32]
            ACT(dT, bank_bf(k2)[:, 0:32], AF.Copy, [B_ps[k2]], pgb(10))
            k3 = ps_all.next()
            MMG([(bank(k3)[:, c * 16:(c + 1) * 16], wblk[:, c, :], dT[:, c * 16:(c + 1) * 16], True, True) for c in range(2)],
                pgb(10) + [B_wblk], [B_ps[k3]])
            for c in range(2):
                ACT(mixT[:, c, NT:NTOK], bank(k3)[:, c * 16:(c + 1) * 16], AF.Copy, [B_ps[k3], B_const], [B_mix[c]],
                    scale=pscale[:, c:c + 1])
            DMA("sp", misc(), [(nps[:, 0:14 * 256], stp[:, 256:3840]), (nps[:, 14 * 256:15 * 256], u_s)], Bst + Bs, ())

        def gen_conv_prompt(c, t, sl, cx):
            ps = cx.ps
            T0 = t * 512
            tk = slice(T0, T0 + 512)
            Rw = [B_ring[sl], B_hT[t]]
            kc_, kh, kb = cx.banks[0], cx.banks[1], cx.banks[2]
            for (pk, off) in ((kc_, 128), (kh, 256), (kb, 0)):
                MMG([(bank(pk), ring[:, sl, k8 * 384 + off:k8 * 384 + off + 128], hT[:, k8, tk], k8 == 0, k8 == 7)
                     for k8 in range(8)], Rw, [B_ps[pk]])
            yield
            pg = cx.pp
            cgs = VPGF(pg[0])
            ze = arena[:, pg[1] * 512:(pg[1] + 2) * 512][:, 0:514]
            c1 = VPGF(pg[3])
            c2 = VPGF(pg[4])
            Bc, Bz, B1, B2 = VPGB(pg[0]), pgb(pg[1], 2), VPGB(pg[3]), VPGB(pg[4])
            if t == 0:
                MEMSET(ze[:, 0:2], 0.0, Bz)
            ACT(cgs, bank(kc_), AF.Copy, [B_ps[kc_]], Bc)
            yield
            TT(ze[:, 2:514], bank(kh), cgs, ALU.mult, [B_ps[kh]] + Bc, Bz)
            yield
            w0 = convwf[:, c * 3 + 0:c * 3 + 1]
            w1 = convwf[:, c * 3 + 1:c * 3 + 2]
            w2 = convwf[:, c * 3 + 2:c * 3 + 3]
            ACT(c1, ze[:, 0:512], AF.Copy, Bz + [B_const], B1, scale=w0)
            yield
            STT(c2, ze[:, 1:513], w1, c1, ALU.mult, ALU.add, Bz + B1 + [B_const], B2)
            yield
            STT(c1, ze[:, 2:514], w2, c2, ALU.mult, ALU.add, Bz + B2 + [B_const], B1)
            yield
            TT(mixT[:, c, tk], bank(kb), c1, ALU.mult, [B_ps[kb]] + B1, [B_mix[c]])
            if t < 3:
                CP(ze[:, 0:2], ze[:, 512:514], Bz, Bz)
            else:
                kx = cx.banks[0]
                TRG([(bank(kx)[0:2, 0:128], ze[:, 512:514], identf[:, :])], Bz + [B_const], [B_ps[kx]])
                stg = VPGF(pg[0])[0:2, 0:128]
                ACT(stg, bank(kx)[0:2, 0:128], AF.Copy, [B_ps[kx]], Bc)
                DMA("sp", misc(), [(ncp[:, c * 128:(c + 1) * 128], stg)], Bc, ())
            yield

        def conv_sample_part1(c, cx):
            sl = (U_CONV + c) % NSLOT
            Rw = [B_ring[sl], B_hT[4]]
            S_ = cx.small
            Bs = S_.bufs()
            prev = S_.slot(0, n=2).rearrange("p (r c) -> p r c", c=128)
            wbc = S_.slot(4, n=3).rearrange("p (r c) -> p r c", c=128)
            DMA("sp", misc(), [(prev, st_conv[:, :, c * 128:(c + 1) * 128]),
                               (wbc, convw_d[:, c * 128:(c + 1) * 128].partition_broadcast(16))], (), Bs)
            kp = cx.banks[3]
            MMG([(bank(kp)[0:16, 0:384], hT[:, k8, NT:NTOK], ring[:, sl, k8 * 384:(k8 + 1) * 384], k8 == 0, k8 == 7)
                 for k8 in range(8)], Rw, [B_ps[kp]])

        def gen_conv_sample_part2(c, cx):
            S_ = cx.small
            Bs = S_.bufs()
            prev = S_.slot(0, n=2).rearrange("p (r c) -> p r c", c=128)
            wbc = S_.slot(4, n=3).rearrange("p (r c) -> p r c", c=128)
            kp = cx.banks[3]
            cg_s, z_s, a_, b_ = S_.slot(2), S_.slot(3), S_.slot(7), S_.slot(8)
            ACT(cg_s, bank(kp)[0:16, 128:256], AF.Copy, [B_ps[kp]], Bs)
            yield
            TT(z_s, bank(kp)[0:16, 256:384], cg_s, ALU.mult, [B_ps[kp]] + Bs, Bs)
            TT(a_, prev[:, 0, :], wbc[:, 0, :], ALU.mult, Bs, Bs)
            yield
            TT(b_, prev[:, 1, :], wbc[:, 1, :], ALU.mult, Bs, Bs)
            TT(a_, a_, b_, ALU.add, Bs, Bs)
            yield
            TT(b_, z_s, wbc[:, 2, :], ALU.mult, Bs, Bs)
            TT(a_, a_, b_, ALU.add, Bs, Bs)
            yield
            yb = S_.slot(9).bitcast(BF16)[:, 0:128]
            TT(yb, bank(kp)[0:16, 0:128], a_, ALU.mult, [B_ps[kp]] + Bs, Bs)
            yield
            k2 = kp
            TRG([(bank_bf(k2)[:, 0:16], yb, identb[0:16, 0:16])], Bs + [B_const], [B_ps[k2]])
            yield
            ACT(mixT[:, c, NT:NTOK], bank_bf(k2)[:, 0:16], AF.Copy, [B_ps[k2]], [B_mix[c]])
            DMA("sp", misc(), [(ncs[:, 0, c * 128:(c + 1) * 128], prev[:, 1, :]),
                               (ncs[:, 1, c * 128:(c + 1) * 128], z_s)], Bs, ())
            yield

        def gen_conv(c, cx):
            sl = (U_CONV + c) % NSLOT
            for t in range(4):
                yield from gen_conv_prompt(c, t, sl, cx)

        for ci in range(2):
            MEMSET(gsm2[ci][:, 63:64], 1.0, [B_gsm2[ci]])
        for b in list(range(16)) + [16]:
            emit_norm(b, 0)
            flush_norm(keep=1)
        flush_norm()
        norm_done = [16]

        def gen_init_norms():
            norm_junk[0] = [NPAGE + 2, NPAGE + 3]
            for b in range(4, 16):
                emit_norm(b, 0)
                flush_norm(keep=1)
                norm_done[0] = b + 1
                yield
            flush_norm()
            norm_junk[0] = [15, 16, 17]
            yield

        hcx = [make_cx(0, list(range(0, 10)), 0, 4, 5, 6, [7, 8, 9], [0, 1, 2, 3]),
               make_cx(1, list(range(10, 20)), 10, 14, 15, 16, [17, 18, 19], [4, 5, 6, 7])]
        for h0 in (0, 2, 4):
            if h0 == 0:
                load_gate.append(B_hT[3])
            need(U_HEAD + h0)
            del load_gate[:]
            gens = [gen_head(h0, hcx[0]), gen_head(h0 + 1, hcx[1])]
            interleave(gens, HEAD_OFFSET)
        sl = need(U_POOL)
        interleave([gen_pool_chunk(0, sl), gen_pool_chunk(1, sl)])
        emit_pool_sample(sl)
        emit_outproj(U_WO, 1)
        emit_mlp(U_FF0, 2)
        ccx = [make_cx(0, [0, 1, 2, 3, 4], 0, 0, 0, 0, [12, 13, 14], [0, 1, 2, 3]),
               make_cx(1, [6, 7, 8, 9, 10], 0, 0, 0, 0, [15, 16, 17], [4, 5, 6, 7])]
        pend = []
        for c0 in (0, 2, 4, 6):
            need(U_CONV + c0)
            interleave([gen_conv(c0, ccx[0]), gen_conv(c0 + 1, ccx[1])] + pend)
            conv_sample_part1(c0, ccx[0])
            conv_sample_part1(c0 + 1, ccx[1])
            pend = [gen_conv_sample_part2(c0, ccx[0]), gen_conv_sample_part2(c0 + 1, ccx[1])]
        interleave(pend)
        emit_outproj(U_OO, 3)
        DMA("sp", misc(), [(pgf(10, 2), gfin_d.partition_broadcast(128))], (), pgb(10, 2))
        emit_mlp(U_FF1, 4)

        final_waits = [(d.h, d.count) for d in P.dsems if d.count > 0]

        def make_body(name, final=False):
            def body(eng):
                for waits, fn, inc in P.streams[name]:
                    embed = None
                    if EMBED_WAITS and inc is True and waits:
                        embed = waits[-1]
                        waits = waits[:-1]
                    for sem, val in waits:
                        eng.wait_ge(sem, val)
                    ins = fn(eng)
                    if embed is not None:
                        ins._wait_ge(embed[0], embed[1])
                    if inc:
                        ins.then_inc(P.esem[name], 1)
                if final:
                    for h_, v_ in final_waits:
                        eng.wait_ge(h_, v_)
            return body

        block.sync(make_body("sp", final=True))
        block.scalar(make_body("act"))
        block.vector(make_body("dve"))
        block.gpsimd(make_body("pool"))
        block.tensor(make_body("pe"))
    return nc


def _unit_k1024(W, cols):
    sub = W[:, cols]
    n = sub.shape[1]
    u = sub.reshape(8, 128, n).transpose(1, 0, 2).reshape(128, 8 * n)
    out = np.zeros((128, 4096), np.float32)
    out[:, :8 * n] = u
    return out


def _unit_w2(W2, s):
    sub = W2[s * 512:(s + 1) * 512, :]
    return np.ascontiguousarray(sub.reshape(4, 128, 1024).transpose(1, 0, 2).reshape(128, 4096))


def _build_wstream(even_w_in, even_w_out, odd_w_in, odd_w_out, ff_w1, ff_w2):
    wst = np.zeros((NU, 128, 4096), np.float32)
    win = even_w_in[0]
    for h in range(6):
        cols = np.concatenate([256 + r * 768 + h * 128 + np.arange(128) for r in range(4)])
        wst[U_HEAD + h] = _unit_k1024(win, cols)
    wst[U_POOL] = _unit_k1024(win, np.arange(256))
    for half in range(2):
        wst[U_WO + half] = _unit_k1024(even_w_out[0], half * 512 + np.arange(512))
        wst[U_OO + half] = _unit_k1024(odd_w_out[0], half * 512 + np.arange(512))
    for l, u0 in ((0, U_FF0), (1, U_FF1)):
        for s in range(8):
            wst[u0 + 2 * s] = _unit_k1024(ff_w1[l], s * 512 + np.arange(512))
            wst[u0 + 2 * s + 1] = _unit_w2(ff_w2[l], s)
    for c in range(8):
        cols = np.concatenate([r * 1024 + c * 128 + np.arange(128) for r in range(3)])
        wst[U_CONV + c] = _unit_k1024(odd_w_in[0], cols)
    return wst


_NC_CACHE = {}


def kernel(x_prompt, x_sample, state_pool, state_hgrn, state_conv, norm_mix, norm_mlp, norm_final,
           even_w_in, pool_w, pool_scale, hgrn_lb_logits, hgrn_gain, even_w_out, odd_w_in, conv_w,
           odd_w_out, ff_w1, ff_w2):
    f = lambda a: np.ascontiguousarray(np.asarray(a, dtype=np.float32))
    x_prompt, x_sample, state_pool, state_hgrn, state_conv = map(f, (x_prompt, x_sample, state_pool, state_hgrn, state_conv))
    norm_mix, norm_mlp, norm_final = f(norm_mix), f(norm_mlp), f(norm_final)
    even_w_in, pool_w, pool_scale, hgrn_lb_logits, hgrn_gain = map(f, (even_w_in, pool_w, pool_scale, hgrn_lb_logits, hgrn_gain))
    even_w_out, odd_w_in, conv_w, odd_w_out, ff_w1, ff_w2 = map(f, (even_w_out, odd_w_in, conv_w, odd_w_out, ff_w1, ff_w2))

    if "nc" not in _NC_CACHE:
        _NC_CACHE["nc"] = build()
    nc = _NC_CACHE["nc"]

    wst = _build_wstream(even_w_in, even_w_out, odd_w_in, odd_w_out, ff_w1, ff_w2)
    norms = np.stack([norm_mix[0], norm_mlp[0], norm_mix[1], norm_mlp[1]])
    gcols = np.ascontiguousarray(norms.reshape(4, 8, 128).transpose(2, 0, 1).reshape(128, 32))
    lbl = np.ascontiguousarray(hgrn_lb_logits.reshape(3, 6, 128).transpose(2, 0, 1).reshape(128, 18))
    gainf = np.ascontiguousarray(hgrn_gain[0].reshape(6, 128).T)
    pscalef = np.ascontiguousarray(pool_scale[0].reshape(2, 128).T)
    convwf = np.ascontiguousarray(conv_w[0].reshape(3, 8, 128).transpose(2, 1, 0).reshape(128, 24))
    identf = np.eye(128, dtype=np.float32)
    identb = identf.astype(ml_dtypes.bfloat16)
    maskT = np.triu(np.ones((128, 128), np.float32))
    onesm = np.full((128, 128), 1.0 / 128, np.float32)
    id16 = np.eye(16, dtype=np.float32)
    cfixw = np.zeros((128, 2, 16), np.float32)
    for c in range(2):
        for p in range(128):
            w = 2 ** (2 * c + p // 64 + 1)
            for t in range(16):
                cfixw[p, c, t] = 1.0 / min(w, t + 1)
    cfixw = cfixw.reshape(128, 32)

    shared = dict(wst=wst, gcols=gcols, gfin=norm_final.reshape(1, 1024), lbl=lbl, gainf=gainf, pscalef=pscalef,
                  poolw=pool_w[0], convwf=convwf, convw=conv_w[0], c_identb=identb, c_identf=identf, c_maskT=maskT,
                  c_onesm=onesm, c_id16=id16, c_cfixw=cfixw)
    in_maps = []
    for c in range(8):
        m = dict(shared)
        m["xp"] = x_prompt[c].reshape(16, 128, 1024)
        m["xs"] = x_sample[16 * c:16 * (c + 1), 0, :]
        m["st_pool"] = state_pool[0, 16 * c:16 * (c + 1)].reshape(16, 15 * 256)
        m["st_hgrn"] = state_hgrn[0, 16 * c:16 * (c + 1)]
        m["st_conv"] = state_conv[0, 16 * c:16 * (c + 1)]
        in_maps.append({k: np.ascontiguousarray(v) for k, v in m.items()})

    res = run_bass_kernel_spmd(nc, in_maps, core_ids=list(range(8)))
    R = res.results
    y_prompt = np.stack([R[c]["yp"].reshape(2048, 1024) for c in range(8)]).astype(np.float32)
    y_sample = np.concatenate([R[c]["ys"] for c in range(8)]).reshape(128, 1, 1024).astype(np.float32)
    new_pool_prompt = np.stack([R[c]["npp"] for c in range(8)])[None].astype(np.float32)
    new_hgrn_prompt = np.stack([R[c]["nhp"] for c in range(8)])[None].astype(np.float32)
    new_conv_prompt = np.stack([R[c]["ncp"] for c in range(8)])[None].astype(np.float32)
    new_pool_sample = np.concatenate([R[c]["nps"].reshape(16, 15, 256) for c in range(8)])[None].astype(np.float32)
    new_hgrn_sample = np.concatenate([R[c]["nhs"] for c in range(8)])[None].astype(np.float32)
    new_conv_sample = np.concatenate([R[c]["ncs"] for c in range(8)])[None].astype(np.float32)
    return (y_prompt, y_sample, new_pool_prompt, new_hgrn_prompt, new_conv_prompt,
            new_pool_sample, new_hgrn_sample, new_conv_sample)
```

```python
import numpy as np
import ml_dtypes
from contextlib import ExitStack
import concourse.bass as bass
import concourse.mybir as mybir
from concourse.bass_utils import run_bass_kernel_spmd

F32 = mybir.dt.float32
BF16 = mybir.dt.bfloat16
AF = mybir.ActivationFunctionType
ALU = mybir.AluOpType
AX = mybir.AxisListType

NT = 2048
NSMP = 16
NTOK = NT + NSMP
EPS = 1e-6
NSLOT = 4
NPAGE = 18
SAME_ENGINE_SYNC = True
EMBED_WAITS = True
HEAD_OFFSET = 0
PREFETCH_F = False

U_HEAD = 0
U_POOL = 6
U_WO = 7
U_FF0 = 9
U_CONV = 25
U_OO = 33
U_FF1 = 35
NU = 51


class Buf:
    __slots__ = ("name", "w", "r")

    def __init__(self, name):
        self.name = name
        self.w = None
        self.r = {}


class DSem:
    def __init__(self, h):
        self.h = h
        self.count = 0


class Prog:
    ENG = ("sp", "act", "dve", "pool", "pe")

    def __init__(self):
        self.streams = {e: [] for e in self.ENG}
        self.seq = {e: 0 for e in self.ENG}
        self.waited = {e: {} for e in self.ENG}
        self.esem = {}
        self.dsems = []

    @staticmethod
    def _flat(bufs):
        out = []
        for b in bufs:
            if isinstance(b, (list, tuple)):
                out.extend(Prog._flat(b))
            else:
                out.append(b)
        return out

    def _waits(self, eng, reads, writes, extra=()):
        need = {}
        reads = self._flat(reads)
        writes = self._flat(writes)

        def add(tok):
            if tok is None:
                return
            if tok[0] == "e":
                if tok[1] == eng and (eng == "pe" or not SAME_ENGINE_SYNC):
                    return
                k = ("e", tok[1])
            else:
                k = ("d", id(tok[1]))
            if self.waited[eng].get(k, 0) >= tok[2]:
                return
            if k not in need or need[k][2] < tok[2]:
                need[k] = tok

        for b in reads:
            add(b.w)
        for b in writes:
            add(b.w)
            for t in b.r.values():
                add(t)
        for t in extra:
            add(t)
        out = []
        for k, tok in need.items():
            self.waited[eng][k] = tok[2]
            sem = self.esem[tok[1]] if tok[0] == "e" else tok[1].h
            out.append((sem, tok[2]))
        return out

    @staticmethod
    def _mark(tok, reads, writes):
        reads = Prog._flat(reads)
        writes = Prog._flat(writes)
        k = ("e", tok[1]) if tok[0] == "e" else ("d", id(tok[1]))
        for b in reads:
            b.r[k] = tok
        for b in writes:
            b.w = tok
            b.r = {}

    def op(self, eng, fn, reads=(), writes=(), multi=False):
        waits = self._waits(eng, reads, writes)
        self.seq[eng] += 1
        tok = ("e", eng, self.seq[eng])
        self.streams[eng].append((waits, fn, "multi" if multi else True))
        self._mark(tok, reads, writes)
        return tok

    def dma(self, q, dsem, fns, reads=(), writes=()):
        extra = []
        if dsem.count > 0:
            extra.append(("d", dsem, dsem.count))
        waits = self._waits(q, reads, writes, extra)
        dsem.count += 16 * len(fns)
        tok = ("d", dsem, dsem.count)

        def run(eng, fns=fns, h=dsem.h):
            for f in fns:
                f(eng).then_inc(h, 16)
            return None

        self.streams[q].append((waits, run, False))
        self._mark(tok, reads, writes)
        return tok


def build():
    nc = bass.Bass("TRN2", target_bir_lowering=False)

    def din(name, shape, dt=F32):
        return nc.dram_tensor(name, shape, dt, kind="ExternalInput").ap()

    def dout(name, shape, dt=F32):
        return nc.dram_tensor(name, shape, dt, kind="ExternalOutput").ap()

    xp = din("xp", [16, 128, 1024])
    xs = din("xs", [16, 1024])
    st_pool = din("st_pool", [16, 15 * 256])
    st_hgrn = din("st_hgrn", [16, 6, 128, 128])
    st_conv = din("st_conv", [16, 2, 1024])
    wst = din("wst", [NU, 128, 4096])
    gcols_d = din("gcols", [128, 32])
    gfin_d = din("gfin", [1, 1024])
    lbl_d = din("lbl", [128, 18])
    gain_d = din("gainf", [128, 6])
    pscale_d = din("pscalef", [128, 2])
    poolw_d = din("poolw", [4, 64, 64])
    convwf_d = din("convwf", [128, 24])
    convw_d = din("convw", [3, 1024])
    c_identb = din("c_identb", [128, 128], BF16)
    c_identf = din("c_identf", [128, 128])
    c_maskT = din("c_maskT", [128, 128])
    c_onesm = din("c_onesm", [128, 128])
    c_id16 = din("c_id16", [16, 16])
    c_cfixw = din("c_cfixw", [128, 32])

    yp = dout("yp", [16, 128, 1024])
    ys = dout("ys", [16, 1024])
    npp = dout("npp", [15, 256])
    nhp = dout("nhp", [6, 128, 128])
    ncp = dout("ncp", [2, 1024])
    nps = dout("nps", [16, 15 * 256])
    nhs = dout("nhs", [16, 6, 128, 128])
    ncs = dout("ncs", [16, 2, 1024])

    P = Prog()
    with ExitStack() as es:
        E = es.enter_context

        def sb(name, shape, dt=F32):
            return E(nc.sbuf_tensor(name, shape, dt))

        x_sb = sb("x_sb", [128, 16, 1024])
        xs_sb = sb("xs_sb", [128, 1024])
        hT = sb("hT", [128, 8, NTOK], BF16)
        mixT = sb("mixT", [128, 8, NTOK], BF16)
        ring = sb("ring", [128, NSLOT, 4096], BF16)
        arena = sb("arena", [128, NPAGE * 512])
        identb = sb("identb", [128, 128], BF16)
        identf = sb("identf", [128, 128])
        maskT = sb("maskT", [128, 128])
        onesm = sb("onesm", [128, 128])
        onesb = sb("onesb", [128, 128], BF16)
        wblk = sb("wblk", [128, 2, 128], BF16)
        gcols = sb("gcols_s", [128, 32])
        lbl = sb("lbl_s", [128, 18])
        hcols = sb("hcols", [128, 32])
        pscale = sb("pscale_s", [128, 2])
        cfixw = sb("cfixw_s", [128, 32])
        convwf = sb("convwf_s", [128, 24])
        id16 = sb("id16_s", [128, 16])
        nrm = sb("nrm", [128, 32])
        tmpc = sb("tmpc", [128, 16])
        gsm2 = [sb(f"gsm{i}", [128, 64]) for i in range(2)]
        Sst2 = [sb(f"Sst{i}", [128, 128]) for i in range(2)]
        Spb2 = [sb(f"Spb{i}", [128, 2, 128], BF16) for i in range(2)]
        tSb2 = [sb(f"tSb{i}", [128, 128]) for i in range(2)]
        psT = [E(nc.psum_tensor(f"pp{i}", [128, 1024], F32)) for i in range(4)]

        for e in Prog.ENG:
            P.esem[e] = E(nc.semaphore(f"es_{e}"))

        def new_dsem(name):
            d = DSem(E(nc.semaphore(name)))
            P.dsems.append(d)
            return d

        ring_ds = [new_dsem(f"ds_ring{i}") for i in range(NSLOT)]
        xy_ds = [new_dsem(f"ds_xy{i}") for i in range(17)]
        misc_ds = [new_dsem(f"ds_misc{i}") for i in range(8)]
        misc_ptr = [0]

        def misc():
            d = misc_ds[misc_ptr[0] % len(misc_ds)]
            misc_ptr[0] += 1
            return d

        block = E(nc.Block())

        B_x = [Buf(f"x{b}") for b in range(16)] + [Buf("xs")]
        B_hT = [Buf(f"hT{t}") for t in range(5)]
        B_mix = [Buf(f"mix{c}") for c in range(8)]
        B_mix[0] = [Buf(f"mix0_{i}") for i in range(4)]
        B_mix[1] = [Buf(f"mix1_{i}") for i in range(4)]
        B_ring = [Buf(f"ring{i}") for i in range(NSLOT)]
        B_pg = [[Buf(f"pg{i}lo"), Buf(f"pg{i}hi")] for i in range(NPAGE)]
        B_ps = [Buf(f"ps{i}") for i in range(8)]
        B_const = Buf("const")
        B_wblk = Buf("wblk")
        B_hcols = Buf("hcols")
        B_nrm = [Buf("nrm0"), Buf("nrm1"), Buf("nrm1b")]
        B_tmpc = Buf("tmpc")
        B_dummy = [Buf("dummy0"), Buf("dummy1")]
        B_gsm2 = [Buf("gsm0"), Buf("gsm1")]
        B_S2 = [Buf("S0"), Buf("S1")]
        B_Sp2 = [[Buf("Sp00"), Buf("Sp01")], [Buf("Sp10"), Buf("Sp11")]]
        B_tS2 = [Buf("tS0"), Buf("tS1")]
        B_nrm2 = [Buf("nrm2"), Buf("nrm3")]

        def ACT(out, in_, func, R, W, scale=1.0, bias=0.0, accum=None):
            def f(e):
                if accum is None:
                    return e.activation(out=out, in_=in_, func=func, bias=bias, scale=scale)
                return e.activation(out=out, in_=in_, func=func, bias=bias, scale=scale, accum_out=accum)
            return P.op("act", f, R, W, multi=accum is not None)

        def TT(out, in0, in1, op, R, W, eng="dve"):
            return P.op(eng, lambda e: e.tensor_tensor(out=out, in0=in0, in1=in1, op=op), R, W)

        def TS(out, in0, s1, op0, R, W, s2=None, op1=None, eng="dve"):
            def f(e):
                if op1 is None:
                    return e.tensor_scalar(out=out, in0=in0, scalar1=s1, scalar2=None, op0=op0)
                return e.tensor_scalar(out=out, in0=in0, scalar1=s1, scalar2=s2, op0=op0, op1=op1)
            return P.op(eng, f, R, W)

        def STT(out, in0, scalar, in1, op0, op1, R, W):
            return P.op("dve", lambda e: e.scalar_tensor_tensor(out=out, in0=in0, scalar=scalar, in1=in1,
                                                                 op0=op0, op1=op1), R, W)

        def CP(out, in_, R, W, eng="dve"):
            return P.op(eng, lambda e: e.tensor_copy(out=out, in_=in_), R, W)

        def MEMSET(ap, val, W, eng="dve"):
            return P.op(eng, lambda e: e.memset(ap, val), (), W)

        def MMG(mms, R, W):
            def f(e):
                ins = None
                for (o, l, r, s, t) in mms:
                    ins = e.matmul(o, l, r, start=s, stop=t)
                return ins
            return P.op("pe", f, R, W, multi=True)

        def TRG(trs, R, W):
            def f(e):
                ins = None
                for (o, i, idn) in trs:
                    ins = e.transpose(out=o, in_=i, identity=idn)
                return ins
            return P.op("pe", f, R, W, multi=True)

        def DMA(q, dsem, pairs, R, W, **kw):
            fns = [(lambda e, o=o, i=i: e.dma_start(out=o, in_=i, **kw)) for (o, i) in pairs]
            return P.dma(q, dsem, fns, R, W)

        def bank(k):
            return psT[k // 2][:, (k % 2) * 512:(k % 2 + 1) * 512]

        def bank_bf(k):
            return bank(k).bitcast(BF16)

        def pair(i):
            return psT[i][:, :]

        class Rot:
            def __init__(self, items):
                self.items = items
                self.i = 0

            def next(self):
                v = self.items[self.i % len(self.items)]
                self.i += 1
                return v

        def pgf(p, n=1):
            return arena[:, p * 512:(p + n) * 512]

        def pgb(p, n=1):
            return [B_pg[i] for i in range(p, p + n)]

        unit_cols = {}
        for u in range(NU):
            unit_cols[u] = 4096
        unit_cols[U_POOL] = 2048
        for c in range(8):
            unit_cols[U_CONV + c] = 3072
        next_load = [0]

        def issue_load(u):
            s = u % NSLOT
            ncol = unit_cols[u]
            pairs = []
            c0 = 0
            while c0 < ncol:
                c1 = min(c0 + 2048, ncol)
                pairs.append((ring[:, s, c0:c1], wst[u, :, c0:c1]))
                c0 = c1
            DMA("pool", ring_ds[s], pairs, list(load_gate), [B_ring[s]])

        load_gate = []

        def need(u, la=NSLOT - 1):
            lim = min(u + la, NU - 1)
            while next_load[0] <= lim:
                issue_load(next_load[0])
                next_load[0] += 1
            return u % NSLOT

        def tokslice(t):
            if t < 4:
                return slice(t * 512, (t + 1) * 512)
            return slice(NT, NTOK)

        need(0, 1)
        DMA("sp", misc(), [
            (identb[:, :], c_identb), (identf[:, :], c_identf), (maskT[:, :], c_maskT), (onesm[:, :], c_onesm),
            (gcols[:, :], gcols_d), (lbl[:, :], lbl_d), (hcols[:, 18:24], gain_d), (pscale[:, :], pscale_d),
            (cfixw[:, :], c_cfixw), (convwf[:, :], convwf_d), (id16[0:16, :], c_id16),
        ], (), [B_const, B_hcols])
        for b in range(16):
            DMA("sp", xy_ds[b], [(x_sb[:, b, :], xp[b])], (), [B_x[b]])
        DMA("sp", xy_ds[16], [(xs_sb[0:16, :], xs)], (), [B_x[16]])
        MEMSET(wblk[:, :, :], 0.0, [B_wblk])
        DMA("pool", new_dsem("ds_wblk"), [
            (wblk[(g % 2) * 64:(g % 2) * 64 + 64, g // 2, (g % 2) * 64:(g % 2) * 64 + 64], poolw_d[g]) for g in range(4)
        ], (), [B_wblk])
        ACT(onesb[:, :], onesm[:, :], AF.Copy, [B_const], [B_const])
        ACT(lbl[:, :], lbl[:, :], AF.Exp, [B_const], [B_const])
        TT(tmpc[:, 0:6], lbl[:, 0:6], lbl[:, 6:12], ALU.add, [B_const], [B_tmpc])
        TT(tmpc[:, 0:6], tmpc[:, 0:6], lbl[:, 12:18], ALU.add, [B_const, B_tmpc], [B_tmpc])
        P.op("dve", lambda e: e.reciprocal(out=tmpc[:, 6:12], in_=tmpc[:, 0:6]), [B_tmpc], [B_tmpc])
        TT(hcols[:, 0:6], lbl[:, 0:6], tmpc[:, 6:12], ALU.mult, [B_const, B_tmpc], [B_hcols])
        TS(hcols[:, 6:12], hcols[:, 0:6], -1.0, ALU.mult, [B_hcols], [B_hcols], s2=1.0, op1=ALU.add)
        TS(hcols[:, 12:18], hcols[:, 6:12], -1.0, ALU.mult, [B_hcols], [B_hcols])
        ACT(hcols[:, 24:30], hcols[:, 6:12], AF.Ln, [B_hcols], [B_hcols])
        LB, OML, NOML, GAIN, LNOML = 0, 6, 12, 18, 24

        ps_norm = Rot([0, 1])
        norm_ctr = [0]
        norm_pending = []
        norm_junk = [[15, 16, 17]]

        def emit_norm(b, n):
            nj = len(norm_junk[0])
            k = norm_ctr[0] % nj
            norm_ctr[0] += 1
            np_ = 128 if b < 16 else 16
            xb = x_sb[:, b, :] if b < 16 else xs_sb[0:16, :]
            Bx = B_x[b]
            junk = VPGF(norm_junk[0][k])[0:np_, :].bitcast(BF16)
            Bj = VPGB(norm_junk[0][k])
            ss = nrm[0:np_, 4 * k + 0:4 * k + 1]
            lt = nrm[0:np_, 4 * k + 1:4 * k + 2]
            rs = nrm[0:np_, 4 * k + 2:4 * k + 3]
            Bn = [B_nrm[k]]
            ACT(junk, xb, AF.Square, [Bx], Bj + Bn, accum=ss)
            ACT(lt, ss, AF.Ln, Bn, Bn, scale=1.0 / 1024, bias=EPS)
            ACT(rs, lt, AF.Exp, Bn, Bn, scale=-0.5)
            if n == 4:
                gfin = pgf(10, 2)
                STT(xb, xb, rs, gfin[0:np_, :], ALU.mult, ALU.mult, [Bx] + Bn + pgb(10, 2), [Bx])
                if b < 16:
                    DMA("sp", xy_ds[b], [(yp[b], x_sb[:, b, :])], [Bx], ())
                else:
                    DMA("sp", xy_ds[16], [(ys, xs_sb[0:16, :])], [Bx], ())
                return
            if n == 0 and b % 3 != 2:
                TS(junk, xb, rs, ALU.mult, [Bx] + Bn, Bj)
            else:
                ACT(junk, xb, AF.Copy, [Bx] + Bn, Bj, scale=rs)

            def part_b():
                pk = ps_norm.next()
                pb = bank_bf(pk)
                if b < 16:
                    TRG([(pb[:, c * 128:(c + 1) * 128], junk[:, c * 128:(c + 1) * 128], identb[:, :]) for c in range(8)],
                        Bj + [B_const], [B_ps[pk]])
                    t = b // 4
                    TT(hT[:, :, b * 128:(b + 1) * 128], pb[:, 0:1024].rearrange("p (c t) -> p c t", t=128),
                       gcols[:, n * 8:(n + 1) * 8].unsqueeze(2).broadcast_to([128, 8, 128]), ALU.mult,
                       [B_ps[pk], B_const], [B_hT[t]])
                else:
                    TRG([(pb[:, c * 16:(c + 1) * 16], junk[:, c * 128:(c + 1) * 128], identb[0:16, 0:16]) for c in range(8)],
                        Bj + [B_const], [B_ps[pk]])
                    TT(hT[:, :, NT:NTOK], pb[:, 0:128].rearrange("p (c t) -> p c t", t=16),
                       gcols[:, n * 8:(n + 1) * 8].unsqueeze(2).broadcast_to([128, 8, 16]), ALU.mult,
                       [B_ps[pk], B_const], [B_hT[4]])
            norm_pending.append(part_b)

        def flush_norm(keep=0):
            while len(norm_pending) > keep:
                norm_pending.pop(0)()

        ps_pair = Rot([2, 3])

        def emit_outproj(u0, norm_after):
            need(u0)
            s0, s1 = u0 % NSLOT, (u0 + 1) % NSLOT
            for b in list(range(16)) + [16]:
                pi = ps_pair.next()
                pp = pair(pi)
                tk = slice(b * 128, (b + 1) * 128) if b < 16 else slice(NT, NTOK)
                np_ = 128 if b < 16 else 16
                mms = []
                for half, s in ((0, s0), (1, s1)):
                    for kc in range(8):
                        mms.append((pp[0:np_, half * 512:(half + 1) * 512], mixT[:, kc, tk],
                                    ring[:, s, kc * 512:(kc + 1) * 512], kc == 0, kc == 7))
                MMG(mms, B_mix + [B_ring[s0], B_ring[s1]], [B_ps[2 * pi], B_ps[2 * pi + 1]])
                xb = x_sb[:, b, :] if b < 16 else xs_sb[0:16, :]
                TT(xb, xb, pp[0:np_, :], ALU.add, [B_x[b], B_ps[2 * pi], B_ps[2 * pi + 1]], [B_x[b]])
                emit_norm(b, norm_after)
                flush_norm(keep=2)
            flush_norm()

        def emit_mlp(u0, norm_after):
            ps_single = Rot([0, 1, 2, 3])
            ps_pr = Rot([2, 3])
            steps = [(s, t) for s in range(8) for t in range(5)]
            hid = [mixT[:, 0, 0:2048].rearrange("p (c t) -> p c t", t=512),
                   mixT[:, 1, 0:2048].rearrange("p (c t) -> p c t", t=512)]
            hid_s = [mixT[:, 2, 0:64].rearrange("p (c t) -> p c t", t=16),
                     mixT[:, 3, 0:64].rearrange("p (c t) -> p c t", t=16)]
            B_hid = [B_mix[0], B_mix[1]]
            B_hidc = [[Buf(f"hid{i}_{n}") for n in range(4)] for i in range(2)]
            B_hids = [B_mix[2], B_mix[3]]

            def ff1_items(k):
                s, t = steps[k]
                items = []
                if t == 0:
                    items.append(lambda s=s: need(u0 + 2 * s, NSLOT - 2))
                if t == 2:
                    items.append(lambda s=s: need(u0 + 2 * s + 1, NSLOT - 2))
                sl = (u0 + 2 * s) % NSLOT
                if t < 4:
                    hb = hid[k % 2]
                    for n in range(4):
                        def it(n=n, hb=hb, sl=sl, t=t, k=k):
                            pk = ps_single.next()
                            MMG([(bank(pk), ring[:, sl, kc * 512 + n * 128:kc * 512 + (n + 1) * 128],
                                  hT[:, kc, tokslice(t)], kc == 0, kc == 7) for kc in range(8)],
                                [B_ring[sl], B_hT[t]], [B_ps[pk]])
                            Wh = [B_hidc[k % 2][n]] + ([B_hid[k % 2]] if k < 2 else [])
                            ACT(hb[:, n, :], bank(pk), AF.Relu, [B_ps[pk]], Wh)
                            TT(hb[:, n, :], hb[:, n, :], hb[:, n, :], ALU.mult, [B_hidc[k % 2][n]], [B_hidc[k % 2][n]])
                        items.append(it)
                else:
                    hb = hid_s[s % 2]

                    def it(hb=hb, sl=sl, s=s):
                        pk = ps_single.next()
                        mms = []
                        for n in range(4):
                            for kc in range(8):
                                mms.append((bank(pk)[:, n * 16:(n + 1) * 16],
                                            ring[:, sl, kc * 512 + n * 128:kc * 512 + (n + 1) * 128],
                                            hT[:, kc, NT:NTOK], kc == 0, kc == 7))
                        MMG(mms, [B_ring[sl], B_hT[4]], [B_ps[pk]])
                        ACT(hb, bank(pk)[:, 0:64].rearrange("p (c t) -> p c t", t=16), AF.Relu, [B_ps[pk]], [B_hids[s % 2]])
                        TT(hb, hb, hb, ALU.mult, [B_hids[s % 2]], [B_hids[s % 2]])
                    items.append(it)
                return items

            def ff2_items(k):
                s, t = steps[k]
                sl2 = (u0 + 2 * s + 1) % NSLOT
                items = []
                if t < 4:
                    hb = hid[k % 2]
                    for j in range(4):
                        def it(j=j, hb=hb, sl2=sl2, t=t, s=s, k=k):
                            b = t * 4 + j
                            pi = ps_pr.next()
                            pp = pair(pi)
                            mms = []
                            for half in range(2):
                                for kc in range(4):
                                    mms.append((pp[:, half * 512:(half + 1) * 512], hb[:, kc, j * 128:(j + 1) * 128],
                                                ring[:, sl2, kc * 1024 + half * 512:kc * 1024 + (half + 1) * 512],
                                                kc == 0, kc == 3))
                            MMG(mms, [B_ring[sl2], B_hid[k % 2]] + B_hidc[k % 2], [B_ps[2 * pi], B_ps[2 * pi + 1]])
                            TT(x_sb[:, b, :], x_sb[:, b, :], pp, ALU.add, [B_x[b], B_ps[2 * pi], B_ps[2 * pi + 1]], [B_x[b]])
                            if s == 7:
                                emit_norm(b, norm_after)
                                flush_norm(keep=2)
                        items.append(it)
                else:
                    hb = hid_s[s % 2]

                    def it(hb=hb, sl2=sl2, s=s):
                        pi = ps_pr.next()
                        pp = pair(pi)
                        mms = []
                        for half in range(2):
                            for kc in range(4):
                                mms.append((pp[0:16, half * 512:(half + 1) * 512], hb[:, kc, :],
                                            ring[:, sl2, kc * 1024 + half * 512:kc * 1024 + (half + 1) * 512],
                                            kc == 0, kc == 3))
                        MMG(mms, [B_ring[sl2], B_hids[s % 2]], [B_ps[2 * pi], B_ps[2 * pi + 1]])
                        TT(xs_sb[0:16, :], xs_sb[0:16, :], pp[0:16, :], ALU.add,
                           [B_x[16], B_ps[2 * pi], B_ps[2 * pi + 1]], [B_x[16]])
                        if s == 7:
                            emit_norm(16, norm_after)
                            flush_norm()
                    items.append(it)
                return items

            for it in ff1_items(0):
                it()
            for k in range(len(steps)):
                a = ff1_items(k + 1) if k + 1 < len(steps) else []
                bb = ff2_items(k)
                for i in range(max(len(a), len(bb))):
                    if i < len(a):
                        a[i]()
                    if i < len(bb):
                        bb[i]()

        def VPGF(i):
            if i < NPAGE:
                return arena[:, i * 512:(i + 1) * 512]
            j = i - NPAGE
            return mixT[:, j // 2, (j % 2) * 1024:(j % 2 + 1) * 1024].bitcast(F32)

        def VPGB(i):
            if i < NPAGE:
                return list(B_pg[i])
            j = i - NPAGE
            return list(B_mix[j // 2][2 * (j % 2):2 * (j % 2) + 2])

        class Small:
            def __init__(self, pages):
                self.pages = pages

            def slot(self, i, np_=16, w=128, n=1):
                assert (i % 4) + n <= 4
                return VPGF(self.pages[i // 4])[0:np_, (i % 4) * 128:(i % 4) * 128 + (n - 1) * 128 + w]

            def bufs(self):
                out = []
                for p in self.pages:
                    out += VPGB(p)
                return out

        class Cx:
            pass

        def make_cx(ci, prompt_pages, sh_base, vm, sbf, msk, small_pages, banks):
            cx = Cx()
            cx.ci = ci
            cx.pp = prompt_pages
            cx.sh_base, cx.vm, cx.sbf, cx.msk = sh_base, vm, sbf, msk
            cx.small = Small(small_pages)
            cx.ps = Rot(banks)
            cx.banks = banks
            cx.gsm = gsm2[ci]
            cx.B_gsm = B_gsm2[ci]
            cx.B_gsf = [Buf(f"gs{ci}_{n_}") for n_ in ("GS", "DM", "DBM", "DB", "eD")]
            cx.Sst, cx.Spb, cx.tSb = Sst2[ci], Spb2[ci], tSb2[ci]
            cx.B_S, cx.B_Sp, cx.B_tS = B_S2[ci], B_Sp2[ci], B_tS2[ci]
            cx.nrmc = 16 + 4 * ci
            cx.B_nrm = B_nrm2[ci]
            return cx

        def gen_head_prompt(h, t, sl, cx):
            assert norm_done[0] >= 4 * (t + 1) and (t == 0 or not norm_pending or norm_done[0] > 4 * (t + 1)), (t, norm_done[0])
            T0 = t * 512
            tk = slice(T0, T0 + 512)
            ps = cx.ps
            gs = cx.gsm
            Bg = [cx.B_gsm]
            GSb, DMb, DBMb, DBb, eDb = ([b_] for b_ in cx.B_gsf)

            def W(kc, a, b):
                return ring[:, sl, kc * 512 + a:kc * 512 + b]

            Rw = [B_ring[sl], B_hT[t]]
            pg = cx.pp
            sn, si, lf, G, Ei, osq, rstd = (VPGF(pg[i]) for i in range(7))
            Bsn, Bsi, Blf, BG, BEi, Bosq, Brstd = (VPGB(pg[i]) for i in range(7))
            Ee, rel = si, lf
            p7 = VPGF(pg[7]).bitcast(BF16)
            p8 = VPGF(pg[8]).bitcast(BF16)
            p9 = VPGF(pg[9]).bitcast(BF16)
            B7, B8, B9 = VPGB(pg[7]), VPGB(pg[8]), VPGB(pg[9])
            B7a, B7b, B8a, B8b, B9a, B9b = [B7[0]], [B7[1]], [B8[0]], [B8[1]], [B9[0]], [B9[1]]
            vtok, qd = p7[:, 0:512], p7[:, 512:1024]
            kd, kt = p8[:, 0:512], p8[:, 512:1024]
            sm, sg = p9[:, 0:512], p9[:, 512:1024]
            if t == 0 or not PREFETCH_F:
                cx.kf_next = ps.next()
                MMG([(bank(cx.kf_next), W(kc, 128, 256), hT[:, kc, tk], kc == 0, kc == 7) for kc in range(8)], Rw,
                    [B_ps[cx.kf_next]])
            kf = cx.kf_next
            yield
            ef, L2 = sn, Ei
            ACT(ef, bank(kf), AF.Exp, [B_ps[kf]], Bsn)
            yield
            kq = ps.next()
            MMG([(bank(kq), W(kc, 0, 128), hT[:, kc, tk], kc == 0, kc == 7) for kc in range(8)], Rw, [B_ps[kq]])
            yield
            ki = ps.next()
            MMG([(bank(ki)[:, j * 128:(j + 1) * 128], hT[:, kc, T0 + j * 128:T0 + (j + 1) * 128], W(kc, 256, 384),
                  kc == 0, kc == 7) for j in range(4) for kc in range(8)], Rw, [B_ps[ki]])
            yield
            kg = ps.next()
            MMG([(bank(kg), W(kc, 384, 512), hT[:, kc, tk], kc == 0, kc == 7) for kc in range(8)], Rw, [B_ps[kg]])
            yield
            ACT(L2, ef, AF.Ln, Bsn, BEi, bias=1.0)
            ACT(lf, ef, AF.Ln, Bsn + [B_hcols], Blf, bias=hcols[:, LB + h:LB + h + 1])
            yield
            if t == 0:
                MEMSET(cx.Sst[:, :], 0.0, [cx.B_S])
                P.op("dve", lambda e: e.tensor_tensor_scan(out=G, data0=lf, data1=L2, initial=0.0,
                                                           op0=ALU.add, op1=ALU.subtract), Blf + BEi, BG)
                MEMSET(gs[:, 16:17], 0.0, GSb)
            else:
                CP(gs[:, 16:17], gs[:, 20:21], GSb, GSb)
                P.op("dve", lambda e: e.tensor_tensor_scan(out=G, data0=lf, data1=L2, initial=gs[:, 16:17],
                                                           op0=ALU.add, op1=ALU.subtract), Blf + BEi + GSb, BG)
            yield
            G3 = G.rearrange("p (j t) -> p j t", t=128)
            TT(rel.rearrange("p (j t) -> p j t", t=128), G3, G3[:, :, 63:64].broadcast_to([128, 4, 128]), ALU.subtract,
               BG, Blf)
            yield
            ACT(Ee, rel, AF.Exp, Blf, Bsi)
            TT(L2, rel, L2, ALU.add, Blf + BEi, BEi)
            yield
            TT(qd, bank(kq), Ee, ALU.mult, [B_ps[kq]] + Bsi, B7b)
            ACT(kd, L2, AF.Exp, BEi + [B_hcols], B8a, scale=-1.0, bias=hcols[:, LNOML + h:LNOML + h + 1])
            yield
            CP(gs[:, 17:21], G3[:, :, 127], BG, GSb)
            TT(gs[:, 24:28], G3[:, :, 63], gs[:, 16:20], ALU.subtract, BG + GSb, DMb)
            TT(gs[:, 28:32], gs[:, 17:21], G3[:, :, 63], ALU.subtract, BG + GSb, DBMb)
            TT(gs[:, 32:36], gs[:, 17:21], gs[:, 16:20], ALU.subtract, GSb, DBb)
            ACT(gs[:, 40:52], gs[:, 24:36], AF.Exp, DMb + DBMb + DBb, eDb)
            eM, eBM, eB = 40, 44, 48
            yield
            ACT(si, bank(ki), AF.Sigmoid, [B_ps[ki]], Bsi)
            ACT(sg, bank(kg), AF.Sigmoid, [B_ps[kg]], B9b)
            if cx.ci == 1:
                ACT(tmpc[:, 12 + cx.ci:13 + cx.ci], hcols[:, 0:1], AF.Exp, [B_hcols], [B_dummy[cx.ci]])
            yield
            TT(vtok, bank(ki), si, ALU.mult, [B_ps[ki]] + Bsi, B7a)
            yield
            kt_ps = ps.next()
            TRG([(bank_bf(kt_ps)[:, j * 128:(j + 1) * 128], kd[:, j * 128:(j + 1) * 128], identb[:, :]) for j in range(4)],
                B8a + [B_const], [B_ps[kt_ps]])
            ks = ps.next()
            MMG([(bank(ks)[:, j * 128:(j + 1) * 128], kd[:, j * 128:(j + 1) * 128], qd[:, j * 128:(j + 1) * 128], True, True)
                 for j in range(4)], B7b + B8a, [B_ps[ks]])
            yield
            ACT(kt, bank_bf(kt_ps)[:, 0:512], AF.Copy, [B_ps[kt_ps]], B8b)
            TT(sm.rearrange("p (j t) -> p j t", t=128), bank(ks).rearrange("p (j t) -> p j t", t=128),
               maskT[:, :].unsqueeze(1).broadcast_to([128, 4, 128]), ALU.mult, [B_ps[ks], B_const], B9a)
            yield
            kdS = ps.next()
            MMG([(bank(kdS)[:, j * 128:(j + 1) * 128], kt[:, j * 128:(j + 1) * 128], vtok[:, j * 128:(j + 1) * 128], True, True)
                 for j in range(4)], B7a + B8b, [B_ps[kdS]])
            yield
            if PREFETCH_F and t < 3:
                cx.kf_next = ps.next()
                tk2 = slice(T0 + 512, T0 + 1024)
                MMG([(bank(cx.kf_next), W(kc, 128, 256), hT[:, kc, tk2], kc == 0, kc == 7) for kc in range(8)],
                    [B_ring[sl], B_hT[t + 1]], [B_ps[cx.kf_next]])
            tSa = lf.rearrange("p (j v) -> p j v", v=128)
            TT(tSa, bank(kdS).rearrange("p (j v) -> p j v", v=128),
               gs[:, eBM:eBM + 4].unsqueeze(2).broadcast_to([128, 4, 128]), ALU.mult, [B_ps[kdS]] + eDb, Blf)
            yield
            ko = ps.next()
            for j in range(4):
                sp_i = j % 2
                ACT(cx.Spb[:, sp_i, :], cx.Sst[:, :], AF.Copy, [cx.B_S] + eDb, [cx.B_Sp[sp_i]], scale=gs[:, eM + j:eM + j + 1])
                MMG([(bank(ko)[:, j * 128:(j + 1) * 128], cx.Spb[:, sp_i, :], qd[:, j * 128:(j + 1) * 128], True, False),
                     (bank(ko)[:, j * 128:(j + 1) * 128], vtok[:, j * 128:(j + 1) * 128], sm[:, j * 128:(j + 1) * 128], False, True)],
                    [cx.B_Sp[sp_i]] + B7 + B9a, [B_ps[ko]])
                STT(cx.Sst[:, :], cx.Sst[:, :], gs[:, eB + j:eB + j + 1], tSa[:, j, :], ALU.mult, ALU.add,
                    [cx.B_S] + Blf + eDb, [cx.B_S])
                yield
            osqb = osq.bitcast(BF16)[:, 0:512]
            ACT(osqb, bank(ko), AF.Square, [B_ps[ko]], Bosq)
            yield
            km = ps.next()
            MMG([(bank(km), onesb[:, :], osqb, True, True)], Bosq + [B_const], [B_ps[km]])
            yield
            ACT(rstd, bank(km), AF.Ln, [B_ps[km]], Brstd, bias=EPS)
            ACT(rstd, rstd, AF.Exp, Brstd, Brstd, scale=-0.5)
            yield
            TT(osq, bank(ko), rstd, ALU.mult, [B_ps[ko]] + Brstd, Bosq)
            STT(mixT[:, 2 + h, tk], osq, hcols[:, GAIN + h:GAIN + h + 1], sg, ALU.mult, ALU.mult,
                Bosq + B9b + [B_hcols], [B_mix[2 + h]])
            if t == 3:
                DMA("sp", misc(), [(nhp[h], cx.Sst[:, :])], [cx.B_S], ())
            yield

        def gen_head_sample(h, sl, cx):
            ps = cx.ps
            Rw = [B_ring[sl], B_hT[4]]
            Sh = arena[:, cx.sh_base * 512:(cx.sh_base + 4) * 512].rearrange("p (b v) -> p b v", v=128)
            BSh = pgb(cx.sh_base, 4)
            DMA("sp", misc(), [(Sh[:, 4 * g:4 * g + 4, :], st_hgrn[4 * g:4 * g + 4, h].rearrange("b k v -> k b v"))
                               for g in range(4)], (), BSh)
            kp = ps.next()
            MMG([(bank(kp)[0:16, :], hT[:, kc, NT:NTOK], ring[:, sl, kc * 512:(kc + 1) * 512], kc == 0, kc == 7)
                 for kc in range(8)], Rw, [B_ps[kp]])
            yield
            S_ = cx.small
            Bs = S_.bufs()
            qfig = S_.slot(0, n=4)
            si_s, vv, kk = (S_.slot(i) for i in (4, 5, 6))
            wide = S_.slot(9, np_=128, w=128)
            rs_s, t1_s = wide[:, 0:16], wide[:, 16:32]
            osq_s = wide[:, 32:48].bitcast(BF16)[:, 0:16]
            qT = S_.slot(10, np_=128, w=16).bitcast(BF16)[:, 0:16]
            fm = S_.slot(11, np_=128, w=64)
            snT, kkT, fT, sgT = fm[:, 0:16], fm[:, 16:32], fm[:, 32:48], fm[:, 48:64]
            ACT(si_s, bank(kp)[0:16, 256:384], AF.Sigmoid, [B_ps[kp]], Bs)
            ACT(qfig, bank(kp)[0:16, 0:512], AF.Copy, [B_ps[kp]], Bs)
            yield
            TT(vv, bank(kp)[0:16, 256:384], si_s, ALU.mult, [B_ps[kp]] + Bs, Bs)
            k2 = ps.next()
            TRG([(bank(k2)[:, 0:16], qfig[:, 0:128], identf[0:16, 0:16]),
                 (bank(k2)[:, 16:32], qfig[:, 128:256], identf[0:16, 0:16]),
                 (bank(k2)[:, 32:48], qfig[:, 384:512], identf[0:16, 0:16])], Bs + [B_const], [B_ps[k2]])
            yield
            ACT(snT, bank(k2)[:, 16:32], AF.Sigmoid, [B_ps[k2]], Bs, scale=-1.0)
            ACT(sgT, bank(k2)[:, 32:48], AF.Sigmoid, [B_ps[k2]], Bs)
            ACT(qT, bank(k2)[:, 0:16], AF.Copy, [B_ps[k2]], Bs)
            yield
            TS(kkT, snT, hcols[:, OML + h:OML + h + 1], ALU.mult, Bs + [B_hcols], Bs)
            TS(fT, kkT, -1.0, ALU.mult, Bs, Bs, s2=1.0, op1=ALU.add)
            k3 = ps.next()
            TRG([(bank(k3)[0:16, 0:128], kkT, identf[:, :])], Bs + [B_const], [B_ps[k3]])
            yield
            kkb = kk.bitcast(BF16)[:, 0:128]
            ACT(kkb, bank(k3)[0:16, 0:128], AF.Copy, [B_ps[k3]], Bs)
            kk = kkb
            yield
            Vmf = VPGF(cx.vm)[0:16, :].bitcast(BF16)
            BVm = VPGB(cx.vm)
            for g in range(4):
                Vm = Vmf[:, (g % 2) * 512:(g % 2 + 1) * 512].rearrange("p (b v) -> p b v", v=128)
                TT(Vm, vv.unsqueeze(1).broadcast_to([16, 4, 128]),
                   id16[0:16, 4 * g:4 * g + 4].unsqueeze(2).broadcast_to([16, 4, 128]), ALU.mult, Bs + [B_const], BVm)
                kk_ps = ps.next()
                MMG([(bank(kk_ps), kk, Vm, True, True)], Bs + BVm, [B_ps[kk_ps]])
                TT(Sh[:, 4 * g:4 * g + 4, :], Sh[:, 4 * g:4 * g + 4, :],
                   fT[:, 4 * g:4 * g + 4].unsqueeze(2).broadcast_to([128, 4, 128]), ALU.mult, pgb(cx.sh_base + g, 1) + Bs, pgb(cx.sh_base + g, 1))
                yield
                TT(Sh[:, 4 * g:4 * g + 4, :], Sh[:, 4 * g:4 * g + 4, :], bank(kk_ps).rearrange("p (b v) -> p b v", v=128),
                   ALU.add, pgb(cx.sh_base + g, 1) + [B_ps[kk_ps]], pgb(cx.sh_base + g, 1))
                yield
            DMA("sp", misc(), [(nhs[4 * g:4 * g + 4, h].rearrange("b k v -> k b v"), Sh[:, 4 * g:4 * g + 4, :])
                               for g in range(4)], BSh, ())
            Sbf = VPGF(cx.sbf).bitcast(BF16)
            BSbf = VPGB(cx.sbf)
            ko_ = ps.next()
            oT = bank(ko_)[:, 0:16]
            for g in range(4):
                sb_g = Sbf[:, (g % 2) * 512:(g % 2 + 1) * 512].rearrange("p (b v) -> p b v", v=128)
                ACT(sb_g, Sh[:, 4 * g:4 * g + 4, :], AF.Copy, pgb(cx.sh_base + g, 1), BSbf)
                yield
                MMG([(bank(ko_)[:, 4 * g + bb:4 * g + bb + 1], sb_g[:, bb, :], qT[:, 4 * g + bb:4 * g + bb + 1], True, True)
                     for bb in range(4)], Bs + BSbf, [B_ps[ko_]])
                yield
            ACT(osq_s, oT, AF.Square, [B_ps[ko_]], Bs)
            yield
            km_ = ps.next()
            MMG([(bank(km_)[:, 0:16], onesb[:, :], osq_s, True, True)], Bs + [B_const], [B_ps[km_]])
            yield
            ACT(rs_s, bank(km_)[:, 0:16], AF.Ln, [B_ps[km_]], Bs, bias=EPS)
            ACT(rs_s, rs_s, AF.Exp, Bs, Bs, scale=-0.5)
            yield
            TT(t1_s, oT, rs_s, ALU.mult, [B_ps[ko_]] + Bs, Bs)
            STT(mixT[:, 2 + h, NT:NTOK], t1_s, hcols[:, GAIN + h:GAIN + h + 1], sgT, ALU.mult, ALU.mult,
                Bs + [B_hcols], [B_mix[2 + h]])
            yield

        def gen_head(h, cx):
            sl = (U_HEAD + h) % NSLOT
            for t in range(4):
                yield from gen_head_prompt(h, t, sl, cx)
            yield from gen_head_sample(h, sl, cx)

        def interleave(gens, offset=0):
            gens = list(gens)
            for _ in range(offset):
                try:
                    next(gens[0])
                except StopIteration:
                    break
            while gens:
                for g in list(gens):
                    try:
                        next(g)
                    except StopIteration:
                        gens.remove(g)

        ps_all = Rot(list(range(8)))

        def gen_pool_chunk(c, sl):
            pm = ({"ue": 0, "A": 4, "B": 6, "C": 6, "d": 10, "fx": 16}, {"ue": 2, "A": 8, "B": 11, "C": 13, "d": 15, "fx": 17})[c]
            ue = pgf(pm["ue"], 2)[:, 0:528]
            Bu = pgb(pm["ue"], 2)
            A_ = pgf(pm["A"], 2)[:, 0:528]
            B_ = pgf(pm["B"], 2)[:, 0:528]
            C_ = pgf(pm["C"], 2)[:, 0:528]
            BA, BB, BC = pgb(pm["A"], 2), pgb(pm["B"], 2), pgb(pm["C"], 2)
            dbf = pgf(pm["d"]).bitcast(BF16)[:, 0:512]
            Bd = pgb(pm["d"])
            fx = pgf(pm["fx"])[:, 0:16]
            Bfx = pgb(pm["fx"])
            def proj(tt):
                k_ = ps_all.next()
                tk_ = slice(tt * 512, (tt + 1) * 512)
                MMG([(bank(k_), ring[:, sl, kc * 256 + c * 128:kc * 256 + (c + 1) * 128], hT[:, kc, tk_], kc == 0, kc == 7)
                     for kc in range(8)], [B_ring[sl], B_hT[tt]], [B_ps[k_]])
                return k_

            ku_next = proj(0)
            yield
            for t in range(4):
                T0 = t * 512
                tk = slice(T0, T0 + 512)
                ku = ku_next
                if t == 0:
                    MEMSET(ue[:, 0:16], 0.0, Bu)
                ACT(ue[:, 16:528], bank(ku), AF.Copy, [B_ps[ku]], Bu)
                if t < 3:
                    ku_next = proj(t + 1)
                yield
                TT(A_[:, 2:528], ue[:, 2:528], ue[:, 1:527], ALU.add, Bu, BA)
                yield
                if c == 0:
                    TT(B_[64:128, 4:528], A_[64:128, 4:528], A_[64:128, 2:526], ALU.add, BA, BB)
                    STT(dbf[0:64, :], A_[0:64, 16:528], 0.5, ue[0:64, 16:528], ALU.mult, ALU.subtract, BA + Bu, Bd)
                    yield
                    STT(dbf[64:128, :], B_[64:128, 16:528], 0.25, ue[64:128, 16:528], ALU.mult, ALU.subtract, BB + Bu, Bd)
                    sel_lo, sel_hi = A_, B_
                    yield
                    yield
                else:
                    TT(B_[:, 4:528], A_[:, 4:528], A_[:, 2:526], ALU.add, BA, BB)
                    yield
                    TT(C_[:, 8:528], B_[:, 8:528], B_[:, 4:524], ALU.add, BB, BC)
                    yield
                    TT(A_[64:128, 16:528], C_[64:128, 16:528], C_[64:128, 8:520], ALU.add, BC + BA, BA)
                    STT(dbf[0:64, :], C_[0:64, 16:528], 0.125, ue[0:64, 16:528], ALU.mult, ALU.subtract, BC + Bu, Bd)
                    yield
                    STT(dbf[64:128, :], A_[64:128, 16:528], 0.0625, ue[64:128, 16:528], ALU.mult, ALU.subtract, BA + Bu, Bd)
                    sel_lo, sel_hi = C_, A_
                if t == 0:
                    for (lo, hi, sel) in ((0, 64, sel_lo), (64, 128, sel_hi)):
                        TT(fx[lo:hi, :], sel[lo:hi, 16:32], cfixw[lo:hi, c * 16:(c + 1) * 16], ALU.mult,
                           [B_const] + BA + BB + BC, Bfx)
                        TT(dbf[lo:hi, 0:16], fx[lo:hi, :], ue[lo:hi, 16:32], ALU.subtract, Bfx + Bu, Bd)
                yield
                kp_ = ps_all.next()
                MMG([(bank(kp_), wblk[:, c, :], dbf, True, True)], Bd + [B_wblk], [B_ps[kp_]])
                yield
                ACT(mixT[:, c, tk], bank(kp_), AF.Copy, [B_ps[kp_], B_const], [B_mix[c]], scale=pscale[:, c:c + 1])
                if t < 3:
                    CP(ue[:, 0:16], ue[:, 512:528], Bu, Bu)
                else:
                    kx = ps_all.next()
                    TRG([(bank(kx)[0:15, 0:128], ue[:, 513:528], identf[:, :])], Bu + [B_const], [B_ps[kx]])
                    stg = pgf(pm["fx"])[0:15, 0:128]
                    ACT(stg, bank(kx)[0:15, 0:128], AF.Copy, [B_ps[kx]], Bfx)
                    DMA("sp", misc(), [(npp[:, c * 128:(c + 1) * 128], stg)], Bfx, ())
                yield

        def emit_pool_sample(sl):
            Rw = [B_ring[sl], B_hT[4]]
            stp = pgf(0, 8)[0:16, 0:3840]
            Bst = pgb(0, 8)
            DMA("sp", misc(), [(stp, st_pool)], (), Bst)
            stp3 = stp.rearrange("p (r c) -> p r c", c=256)
            ku = ps_all.next()
            MMG([(bank(ku)[0:16, 0:256], hT[:, kc, NT:NTOK], ring[:, sl, kc * 256:(kc + 1) * 256], kc == 0, kc == 7)
                 for kc in range(8)], Rw, [B_ps[ku]])
            sml = pgf(11, 3)
            Bs = pgb(11, 3)
            u_s = sml[0:16, 0:256]
            sums = sml[0:16, 256:512]
            dd = sml[0:16, 512:768].bitcast(BF16)[:, 0:256]
            ACT(u_s, bank(ku)[0:16, 0:256], AF.Copy, [B_ps[ku]], Bs)
            for g, w in enumerate((2, 4, 8, 16)):
                cs = slice(g * 64, (g + 1) * 64)
                if w == 2:
                    TT(sums[:, cs], stp3[:, 14, cs], u_s[:, cs], ALU.add, Bst + Bs, Bs)
                else:
                    P.op("dve", lambda e, cs=cs, w=w: e.tensor_reduce(
                        out=sums[:, cs], in_=stp3[:, 16 - w:15, cs].rearrange("p r c -> p c r"), axis=AX.X, op=ALU.add),
                        Bst, Bs)
                    TT(sums[:, cs], sums[:, cs], u_s[:, cs], ALU.add, Bs, Bs)
                STT(dd[:, cs], sums[:, cs], 1.0 / w, u_s[:, cs], ALU.mult, ALU.subtract, Bs, Bs)
            k2 = ps_all.next()
            TRG([(bank_bf(k2)[:, c * 16:(c + 1) * 16], dd[:, c * 128:(c + 1) * 128], identb[0:16, 0:16]) for c in range(2)],
                Bs + [B_const], [B_ps[k2]])
            dT = pgf(10).bitcast(BF16)[:, 0:32]
            ACT(dT, bank_bf(k2)[:, 0:32], AF.Copy, [B_ps[k2]], pgb(10))
            k3 = ps_all.next()
            MMG([(bank(k3)[:, c * 16:(c + 1) * 16], wblk[:, c, :], dT[:, c * 16:(c + 1) * 16], True, True) for c in range(2)],
                pgb(10) + [B_wblk], [B_ps[k3]])
            for c in range(2):
                ACT(mixT[:, c, NT:NTOK], bank(k3)[:, c * 16:(c + 1) * 16], AF.Copy, [B_ps[k3], B_const], [B_mix[c]],
                    scale=pscale[:, c:c + 1])
            DMA("sp", misc(), [(nps[:, 0:14 * 256], stp[:, 256:3840]), (nps[:, 14 * 256:15 * 256], u_s)], Bst + Bs, ())

        def gen_conv_prompt(c, t, sl, cx):
            ps = cx.ps
            T0 = t * 512
            tk = slice(T0, T0 + 512)
            Rw = [B_ring[sl], B_hT[t]]
            kc_, kh, kb = cx.banks[0], cx.banks[1], cx.banks[2]
            for (pk, off) in ((kc_, 128), (kh, 256), (kb, 0)):
                MMG([(bank(pk), ring[:, sl, k8 * 384 + off:k8 * 384 + off + 128], hT[:, k8, tk], k8 == 0, k8 == 7)
                     for k8 in range(8)], Rw, [B_ps[pk]])
            yield
            pg = cx.pp
            cgs = VPGF(pg[0])
            ze = arena[:, pg[1] * 512:(pg[1] + 2) * 512][:, 0:514]
            c1 = VPGF(pg[3])
            c2 = VPGF(pg[4])
            Bc, Bz, B1, B2 = VPGB(pg[0]), pgb(pg[1], 2), VPGB(pg[3]), VPGB(pg[4])
            if t == 0:
                MEMSET(ze[:, 0:2], 0.0, Bz)
            ACT(cgs, bank(kc_), AF.Copy, [B_ps[kc_]], Bc)
            yield
            TT(ze[:, 2:514], bank(kh), cgs, ALU.mult, [B_ps[kh]] + Bc, Bz)
            yield
            w0 = convwf[:, c * 3 + 0:c * 3 + 1]
            w1 = convwf[:, c * 3 + 1:c * 3 + 2]
            w2 = convwf[:, c * 3 + 2:c * 3 + 3]
            ACT(c1, ze[:, 0:512], AF.Copy, Bz + [B_const], B1, scale=w0)
            yield
            STT(c2, ze[:, 1:513], w1, c1, ALU.mult, ALU.add, Bz + B1 + [B_const], B2)
            yield
            STT(c1, ze[:, 2:514], w2, c2, ALU.mult, ALU.add, Bz + B2 + [B_const], B1)
            yield
            TT(mixT[:, c, tk], bank(kb), c1, ALU.mult, [B_ps[kb]] + B1, [B_mix[c]])
            if t < 3:
                CP(ze[:, 0:2], ze[:, 512:514], Bz, Bz)
            else:
                kx = cx.banks[0]
                TRG([(bank(kx)[0:2, 0:128], ze[:, 512:514], identf[:, :])], Bz + [B_const], [B_ps[kx]])
                stg = VPGF(pg[0])[0:2, 0:128]
                ACT(stg, bank(kx)[0:2, 0:128], AF.Copy, [B_ps[kx]], Bc)
                DMA("sp", misc(), [(ncp[:, c * 128:(c + 1) * 128], stg)], Bc, ())
            yield

        def conv_sample_part1(c, cx):
            sl = (U_CONV + c) % NSLOT
            Rw = [B_ring[sl], B_hT[4]]
            S_ = cx.small
            Bs = S_.bufs()
            prev = S_.slot(0, n=2).rearrange("p (r c) -> p r c", c=128)
            wbc = S_.slot(4, n=3).rearrange("p (r c) -> p r c", c=128)
            DMA("sp", misc(), [(prev, st_conv[:, :, c * 128:(c + 1) * 128]),
                               (wbc, convw_d[:, c * 128:(c + 1) * 128].partition_broadcast(16))], (), Bs)
            kp = cx.banks[3]
            MMG([(bank(kp)[0:16, 0:384], hT[:, k8, NT:NTOK], ring[:, sl, k8 * 384:(k8 + 1) * 384], k8 == 0, k8 == 7)
                 for k8 in range(8)], Rw, [B_ps[kp]])

        def gen_conv_sample_part2(c, cx):
            S_ = cx.small
            Bs = S_.bufs()
            prev = S_.slot(0, n=2).rearrange("p (r c) -> p r c", c=128)
            wbc = S_.slot(4, n=3).rearrange("p (r c) -> p r c", c=128)
            kp = cx.banks[3]
            cg_s, z_s, a_, b_ = S_.slot(2), S_.slot(3), S_.slot(7), S_.slot(8)
            ACT(cg_s, bank(kp)[0:16, 128:256], AF.Copy, [B_ps[kp]], Bs)
            yield
            TT(z_s, bank(kp)[0:16, 256:384], cg_s, ALU.mult, [B_ps[kp]] + Bs, Bs)
            TT(a_, prev[:, 0, :], wbc[:, 0, :], ALU.mult, Bs, Bs)
            yield
            TT(b_, prev[:, 1, :], wbc[:, 1, :], ALU.mult, Bs, Bs)
            TT(a_, a_, b_, ALU.add, Bs, Bs)
            yield
            TT(b_, z_s, wbc[:, 2, :], ALU.mult, Bs, Bs)
            TT(a_, a_, b_, ALU.add, Bs, Bs)
            yield
            yb = S_.slot(9).bitcast(BF16)[:, 0:128]
            TT(yb, bank(kp)[0:16, 0:128], a_, ALU.mult, [B_ps[kp]] + Bs, Bs)
            yield
            k2 = kp
            TRG([(bank_bf(k2)[:, 0:16], yb, identb[0:16, 0:16])], Bs + [B_const], [B_ps[k2]])
            yield
            ACT(mixT[:, c, NT:NTOK], bank_bf(k2)[:, 0:16], AF.Copy, [B_ps[k2]], [B_mix[c]])
            DMA("sp", misc(), [(ncs[:, 0, c * 128:(c + 1) * 128], prev[:, 1, :]),
                               (ncs[:, 1, c * 128:(c + 1) * 128], z_s)], Bs, ())
            yield

        def gen_conv(c, cx):
            sl = (U_CONV + c) % NSLOT
            for t in range(4):
                yield from gen_conv_prompt(c, t, sl, cx)

        for ci in range(2):
            MEMSET(gsm2[ci][:, 63:64], 1.0, [B_gsm2[ci]])
        for b in list(range(16)) + [16]:
            emit_norm(b, 0)
            flush_norm(keep=1)
        flush_norm()
        norm_done = [16]

        def gen_init_norms():
            norm_junk[0] = [NPAGE + 2, NPAGE + 3]
            for b in range(4, 16):
                emit_norm(b, 0)
                flush_norm(keep=1)
                norm_done[0] = b + 1
                yield
            flush_norm()
            norm_junk[0] = [15, 16, 17]
            yield

        hcx = [make_cx(0, list(range(0, 10)), 0, 4, 5, 6, [7, 8, 9], [0, 1, 2, 3]),
               make_cx(1, list(range(10, 20)), 10, 14, 15, 16, [17, 18, 19], [4, 5, 6, 7])]
        for h0 in (0, 2, 4):
            if h0 == 0:
                load_gate.append(B_hT[3])
            need(U_HEAD + h0)
            del load_gate[:]
            gens = [gen_head(h0, hcx[0]), gen_head(h0 + 1, hcx[1])]
            interleave(gens, HEAD_OFFSET)
        sl = need(U_POOL)
        interleave([gen_pool_chunk(0, sl), gen_pool_chunk(1, sl)])
        emit_pool_sample(sl)
        emit_outproj(U_WO, 1)
        emit_mlp(U_FF0, 2)
        ccx = [make_cx(0, [0, 1, 2, 3, 4], 0, 0, 0, 0, [12, 13, 14], [0, 1, 2, 3]),
               make_cx(1, [6, 7, 8, 9, 10], 0, 0, 0, 0, [15, 16, 17], [4, 5, 6, 7])]
        pend = []
        for c0 in (0, 2, 4, 6):
            need(U_CONV + c0)
            interleave([gen_conv(c0, ccx[0]), gen_conv(c0 + 1, ccx[1])] + pend)
            conv_sample_part1(c0, ccx[0])
            conv_sample_part1(c0 + 1, ccx[1])
            pend = [gen_conv_sample_part2(c0, ccx[0]), gen_conv_sample_part2(c0 + 1, ccx[1])]
        interleave(pend)
        emit_outproj(U_OO, 3)
        DMA("sp", misc(), [(pgf(10, 2), gfin_d.partition_broadcast(128))], (), pgb(10, 2))
        emit_mlp(U_FF1, 4)

        final_waits = [(d.h, d.count) for d in P.dsems if d.count > 0]

        def make_body(name, final=False):
            def body(eng):
                for waits, fn, inc in P.streams[name]:
                    embed = None
                    if EMBED_WAITS and inc is True and waits:
                        embed = waits[-1]
                        waits = waits[:-1]
                    for sem, val in waits:
                        eng.wait_ge(sem, val)
                    ins = fn(eng)
                    if embed is not None:
                        ins._wait_ge(embed[0], embed[1])
                    if inc:
                        ins.then_inc(P.esem[name], 1)
                if final:
                    for h_, v_ in final_waits:
                        eng.wait_ge(h_, v_)
            return body

        block.sync(make_body("sp", final=True))
        block.scalar(make_body("act"))
        block.vector(make_body("dve"))
        block.gpsimd(make_body("pool"))
        block.tensor(make_body("pe"))
    return nc


def _unit_k1024(W, cols):
    sub = W[:, cols]
    n = sub.shape[1]
    u = sub.reshape(8, 128, n).transpose(1, 0, 2).reshape(128, 8 * n)
    out = np.zeros((128, 4096), np.float32)
    out[:, :8 * n] = u
    return out


def _unit_w2(W2, s):
    sub = W2[s * 512:(s + 1) * 512, :]
    return np.ascontiguousarray(sub.reshape(4, 128, 1024).transpose(1, 0, 2).reshape(128, 4096))


def _build_wstream(even_w_in, even_w_out, odd_w_in, odd_w_out, ff_w1, ff_w2):
    wst = np.zeros((NU, 128, 4096), np.float32)
    win = even_w_in[0]
    for h in range(6):
        cols = np.concatenate([256 + r * 768 + h * 128 + np.arange(128) for r in range(4)])
        wst[U_HEAD + h] = _unit_k1024(win, cols)
    wst[U_POOL] = _unit_k1024(win, np.arange(256))
    for half in range(2):
        wst[U_WO + half] = _unit_k1024(even_w_out[0], half * 512 + np.arange(512))
        wst[U_OO + half] = _unit_k1024(odd_w_out[0], half * 512 + np.arange(512))
    for l, u0 in ((0, U_FF0), (1, U_FF1)):
        for s in range(8):
            wst[u0 + 2 * s] = _unit_k1024(ff_w1[l], s * 512 + np.arange(512))
            wst[u0 + 2 * s + 1] = _unit_w2(ff_w2[l], s)
    for c in range(8):
        cols = np.concatenate([r * 1024 + c * 128 + np.arange(128) for r in range(3)])
        wst[U_CONV + c] = _unit_k1024(odd_w_in[0], cols)
    return wst


_NC_CACHE = {}


def kernel(x_prompt, x_sample, state_pool, state_hgrn, state_conv, norm_mix, norm_mlp, norm_final,
           even_w_in, pool_w, pool_scale, hgrn_lb_logits, hgrn_gain, even_w_out, odd_w_in, conv_w,
           odd_w_out, ff_w1, ff_w2):
    f = lambda a: np.ascontiguousarray(np.asarray(a, dtype=np.float32))
    x_prompt, x_sample, state_pool, state_hgrn, state_conv = map(f, (x_prompt, x_sample, state_pool, state_hgrn, state_conv))
    norm_mix, norm_mlp, norm_final = f(norm_mix), f(norm_mlp), f(norm_final)
    even_w_in, pool_w, pool_scale, hgrn_lb_logits, hgrn_gain = map(f, (even_w_in, pool_w, pool_scale, hgrn_lb_logits, hgrn_gain))
    even_w_out, odd_w_in, conv_w, odd_w_out, ff_w1, ff_w2 = map(f, (even_w_out, odd_w_in, conv_w, odd_w_out, ff_w1, ff_w2))

    if "nc" not in _NC_CACHE:
        _NC_CACHE["nc"] = build()
    nc = _NC_CACHE["nc"]

    wst = _build_wstream(even_w_in, even_w_out, odd_w_in, odd_w_out, ff_w1, ff_w2)
    norms = np.stack([norm_mix[0], norm_mlp[0], norm_mix[1], norm_mlp[1]])
    gcols = np.ascontiguousarray(norms.reshape(4, 8, 128).transpose(2, 0, 1).reshape(128, 32))
    lbl = np.ascontiguousarray(hgrn_lb_logits.reshape(3, 6, 128).transpose(2, 0, 1).reshape(128, 18))
    gainf = np.ascontiguousarray(hgrn_gain[0].reshape(6, 128).T)
    pscalef = np.ascontiguousarray(pool_scale[0].reshape(2, 128).T)
    convwf = np.ascontiguousarray(conv_w[0].reshape(3, 8, 128).transpose(2, 1, 0).reshape(128, 24))
    identf = np.eye(128, dtype=np.float32)
    identb = identf.astype(ml_dtypes.bfloat16)
    maskT = np.triu(np.ones((128, 128), np.float32))
    onesm = np.full((128, 128), 1.0 / 128, np.float32)
    id16 = np.eye(16, dtype=np.float32)
    cfixw = np.zeros((128, 2, 16), np.float32)
    for c in range(2):
        for p in range(128):
            w = 2 ** (2 * c + p // 64 + 1)
            for t in range(16):
                cfixw[p, c, t] = 1.0 / min(w, t + 1)
    cfixw = cfixw.reshape(128, 32)

    shared = dict(wst=wst, gcols=gcols, gfin=norm_final.reshape(1, 1024), lbl=lbl, gainf=gainf, pscalef=pscalef,
                  poolw=pool_w[0], convwf=convwf, convw=conv_w[0], c_identb=identb, c_identf=identf, c_maskT=maskT,
                  c_onesm=onesm, c_id16=id16, c_cfixw=cfixw)
    in_maps = []
    for c in range(8):
        m = dict(shared)
        m["xp"] = x_prompt[c].reshape(16, 128, 1024)
        m["xs"] = x_sample[16 * c:16 * (c + 1), 0, :]
        m["st_pool"] = state_pool[0, 16 * c:16 * (c + 1)].reshape(16, 15 * 256)
        m["st_hgrn"] = state_hgrn[0, 16 * c:16 * (c + 1)]
        m["st_conv"] = state_conv[0, 16 * c:16 * (c + 1)]
        in_maps.append({k: np.ascontiguousarray(v) for k, v in m.items()})

    res = run_bass_kernel_spmd(nc, in_maps, core_ids=list(range(8)))
    R = res.results
    y_prompt = np.stack([R[c]["yp"].reshape(2048, 1024) for c in range(8)]).astype(np.float32)
    y_sample = np.concatenate([R[c]["ys"] for c in range(8)]).reshape(128, 1, 1024).astype(np.float32)
    new_pool_prompt = np.stack([R[c]["npp"] for c in range(8)])[None].astype(np.float32)
    new_hgrn_prompt = np.stack([R[c]["nhp"] for c in range(8)])[None].astype(np.float32)
    new_conv_prompt = np.stack([R[c]["ncp"] for c in range(8)])[None].astype(np.float32)
    new_pool_sample = np.concatenate([R[c]["nps"].reshape(16, 15, 256) for c in range(8)])[None].astype(np.float32)
    new_hgrn_sample = np.concatenate([R[c]["nhs"] for c in range(8)])[None].astype(np.float32)
    new_conv_sample = np.concatenate([R[c]["ncs"] for c in range(8)])[None].astype(np.float32)
    return (y_prompt, y_sample, new_pool_prompt, new_hgrn_prompt, new_conv_prompt,
            new_pool_sample, new_hgrn_sample, new_conv_sample)
```
